# Optimizing a Trainium2 kernel written in Bass

```python
import jax, jax.numpy as jnp
from jax import lax
import numpy as np

D_MODEL = 1024
BATCH = 8
SEQ = 4096
DEPTH = 2

N_A_LAYERS = DEPTH // 2
N_B_LAYERS = DEPTH - N_A_LAYERS

RWKV_HEAD_SIZE = 64
RWKV_HEADS = D_MODEL // RWKV_HEAD_SIZE
DECAY_LORA = 64
AAA_LORA = 64
GATE_LORA = 128
GN_EPS = 64e-5

MLA_HEADS = 8
QK_NOPE_DIM = 128
QK_ROPE_DIM = 64
V_HEAD_DIM = 128
Q_LORA_RANK = 512
KV_LORA_RANK = 256
ROPE_THETA = 10000.0
Q_BLOCK = 128
MAX_POS_OFFSET = 2048

D_FF = 2816
RMS_EPS = 1e-6

kernel_name = "hybrid_rwkv7_mla_yoco_macaron"


def rmsnorm(x, g):
    xf = x.astype(jnp.float32)
    y = xf * lax.rsqrt(jnp.mean(xf * xf, axis=-1, keepdims=True) + RMS_EPS)
    return (y * g.astype(jnp.float32)).astype(x.dtype)


def swiglu(x, w_gate, w_up, w_down):
    return (jax.nn.silu(x @ w_gate) * (x @ w_up)) @ w_down


def rope_tables(positions):
    inv_freq = ROPE_THETA ** (-jnp.arange(0, QK_ROPE_DIM, 2, dtype=jnp.float32) / QK_ROPE_DIM)
    ang = positions.astype(jnp.float32)[..., None] * inv_freq
    return jnp.cos(ang), jnp.sin(ang)


def apply_rope(x, cos, sin):
    half = x.shape[-1] // 2
    xf = x.astype(jnp.float32)
    x1, x2 = xf[..., :half], xf[..., half:]
    return jnp.concatenate([x1 * cos - x2 * sin, x2 * cos + x1 * sin], axis=-1).astype(x.dtype)


def wkv7_scan(r, decay, k, v, kk, a):
    B, S, H, N = r.shape

    def step(state, inp):
        r_t, w_t, k_t, v_t, kk_t, a_t = inp
        sa = jnp.einsum("bhvk,bhk->bhv", state, kk_t)
        state = (state * w_t[:, :, None, :]
                 - sa[..., None] * (kk_t * a_t)[:, :, None, :]
                 + v_t[..., None] * k_t[:, :, None, :])
        y_t = jnp.einsum("bhvk,bhk->bhv", state, r_t)
        return state, y_t

    xs = tuple(jnp.moveaxis(t, 1, 0) for t in (r, decay, k, v, kk, a))
    state0 = jnp.zeros((B, H, N, N), jnp.float32)
    _, ys = lax.scan(step, state0, xs)
    return jnp.moveaxis(ys, 0, 1)


def rwkv7_time_mix(x, mix, w_r, w_k, w_v, w_o, w0, w1, w2, a0, a1, a2, g1, g2, k_k, k_a, r_k, gn_w, gn_b):
    B, S, D = x.shape
    H, N = RWKV_HEADS, RWKV_HEAD_SIZE
    f32 = jnp.float32
    xx = jnp.pad(x, ((0, 0), (1, 0), (0, 0)))[:, :-1] - x
    xr, xw, xk, xv, xa, xg = (x + xx * mix[i] for i in range(6))
    r = xr @ w_r
    k = xk @ w_k
    v = xv @ w_v
    w_log = -jax.nn.softplus(-(w0 + jnp.tanh(xw @ w1) @ w2).astype(f32)) - 0.5
    decay = jnp.exp(-jnp.exp(w_log))
    a = jax.nn.sigmoid((a0 + (xa @ a1) @ a2).astype(f32))
    g = jax.nn.sigmoid(xg @ g1) @ g2
    kk = (k * k_k).astype(f32).reshape(B, S, H, N)
    kk = kk * lax.rsqrt(jnp.maximum(jnp.sum(kk * kk, axis=-1, keepdims=True), 1e-24))
    k = k.astype(f32) * (1.0 + (a - 1.0) * k_a.astype(f32))
    heads = lambda t: t.astype(f32).reshape(B, S, H, N)
    rh, kh, vh = heads(r), heads(k), heads(v)
    y = wkv7_scan(rh, heads(decay), kh, vh, kk, heads(a))
    mu = jnp.mean(y, axis=-1, keepdims=True)
    var = jnp.mean(jnp.square(y - mu), axis=-1, keepdims=True)
    yn = ((y - mu) * lax.rsqrt(var + GN_EPS)).reshape(B, S, D) * gn_w.astype(f32) + gn_b.astype(f32)
    bonus = jnp.sum(rh * kh * r_k.astype(f32), axis=-1, keepdims=True) * vh
    out = (yn + bonus.reshape(B, S, D)).astype(x.dtype) * g
    return out @ w_o


def mla_shared_kv(h, kv_norm_g, w_dkv, kv_latent_g, w_ukv, cos, sin):
    B, S, _ = h.shape
    ckv = rmsnorm(h, kv_norm_g) @ w_dkv
    c, k_rope = ckv[..., :KV_LORA_RANK], ckv[..., KV_LORA_RANK:]
    c = rmsnorm(c, kv_latent_g)
    kv = (c @ w_ukv).reshape(B, S, MLA_HEADS, QK_NOPE_DIM + V_HEAD_DIM)
    k_nope, v = kv[..., :QK_NOPE_DIM], kv[..., QK_NOPE_DIM:]
    k_rope = apply_rope(k_rope, cos, sin)
    return k_nope, k_rope, v


def mla_attention(x, w_dq, q_latent_g, w_uq, w_o, k_nope, k_rope, v, cos, sin):
    B, S, _ = x.shape
    q = (rmsnorm(x @ w_dq, q_latent_g) @ w_uq).reshape(B, S, MLA_HEADS, QK_NOPE_DIM + QK_ROPE_DIM)
    q_nope = q[..., :QK_NOPE_DIM]
    q_rope = apply_rope(q[..., QK_NOPE_DIM:], cos[:, :, None, :], sin[:, :, None, :])
    scale = (QK_NOPE_DIM + QK_ROPE_DIM) ** -0.5
    outs = []
    for start in range(0, S, Q_BLOCK):
        end = start + Q_BLOCK
        s = (jnp.einsum("bqhd,bkhd->bhqk", q_nope[:, start:end], k_nope[:, :end])
             + jnp.einsum("bqhd,bkd->bhqk", q_rope[:, start:end], k_rope[:, :end]))
        s = s.astype(jnp.float32) * scale
        mask = (start + jnp.arange(Q_BLOCK))[:, None] >= jnp.arange(end)[None, :]
        p = jax.nn.softmax(jnp.where(mask, s, -1e30), axis=-1).astype(v.dtype)
        outs.append(jnp.einsum("bhqk,bkhd->bqhd", p, v[:, :end]))
    o = jnp.concatenate(outs, axis=1).reshape(B, S, MLA_HEADS * V_HEAD_DIM)
    return o @ w_o


def setup_inputs(seed: int = 0) -> dict:
    key = jax.random.key(seed)
    k = jax.random.split(key, 40)
    f32 = jnp.float32
    D, H, N = D_MODEL, RWKV_HEADS, RWKV_HEAD_SIZE
    na, nb = N_A_LAYERS, N_B_LAYERS
    nrm = lambda i, shape, scale: jax.random.normal(k[i], shape, f32) * scale
    x = nrm(0, (BATCH, SEQ, D), 1.0)
    positions = (jnp.arange(SEQ, dtype=jnp.int32)[None, :]
                 + jax.random.randint(k[1], (BATCH, 1), 0, MAX_POS_OFFSET, dtype=jnp.int32))
    return {
        "x": x,
        "positions": positions,
        "norm_g": 1.0 + nrm(2, (DEPTH, 3, D), 0.02),
        "ffn_w_gate": nrm(3, (DEPTH, 2, D, D_FF), D ** -0.5),
        "ffn_w_up": nrm(4, (DEPTH, 2, D, D_FF), D ** -0.5),
        "ffn_w_down": nrm(5, (DEPTH, 2, D_FF, D), D_FF ** -0.5),
        "rwkv_mix": jax.random.uniform(k[6], (na, 6, D), f32),
        "rwkv_w_r": nrm(7, (na, D, D), D ** -0.5),
        "rwkv_w_k": nrm(8, (na, D, D), D ** -0.5),
        "rwkv_w_v": nrm(9, (na, D, D), D ** -0.5),
        "rwkv_w_o": nrm(10, (na, D, D), D ** -0.5),
        "rwkv_w0": jax.random.uniform(k[11], (na, D), f32, -3.0, 0.5),
        "rwkv_w1": nrm(12, (na, D, DECAY_LORA), D ** -0.5),
        "rwkv_w2": nrm(13, (na, DECAY_LORA, D), 0.1 * DECAY_LORA ** -0.5),
        "rwkv_a0": nrm(14, (na, D), 0.1),
        "rwkv_a1": nrm(15, (na, D, AAA_LORA), D ** -0.5),
        "rwkv_a2": nrm(16, (na, AAA_LORA, D), 0.1 * AAA_LORA ** -0.5),
        "rwkv_g1": nrm(17, (na, D, GATE_LORA), D ** -0.5),
        "rwkv_g2": nrm(18, (na, GATE_LORA, D), GATE_LORA ** -0.5),
        "rwkv_k_k": 0.85 + nrm(19, (na, D), 0.05),
        "rwkv_k_a": 1.0 + nrm(20, (na, D), 0.05),
        "rwkv_r_k": nrm(21, (na, H, N), 0.1),
        "rwkv_gn_w": 1.0 + nrm(22, (na, D), 0.02),
        "rwkv_gn_b": nrm(23, (na, D), 0.02),
        "kv_norm_g": 1.0 + nrm(24, (D,), 0.02),
        "mla_w_dkv": nrm(25, (D, KV_LORA_RANK + QK_ROPE_DIM), D ** -0.5),
        "mla_kv_latent_g": 1.0 + nrm(26, (KV_LORA_RANK,), 0.02),
        "mla_w_ukv": nrm(27, (KV_LORA_RANK, MLA_HEADS * (QK_NOPE_DIM + V_HEAD_DIM)), KV_LORA_RANK ** -0.5),
        "mla_w_dq": nrm(28, (nb, D, Q_LORA_RANK), D ** -0.5),
        "mla_q_latent_g": 1.0 + nrm(29, (nb, Q_LORA_RANK), 0.02),
        "mla_w_uq": nrm(30, (nb, Q_LORA_RANK, MLA_HEADS * (QK_NOPE_DIM + QK_ROPE_DIM)), Q_LORA_RANK ** -0.5),
        "mla_w_o": nrm(31, (nb, MLA_HEADS * V_HEAD_DIM, D), (MLA_HEADS * V_HEAD_DIM) ** -0.5),
        "final_norm_g": 1.0 + nrm(32, (D,), 0.02),
    }


def reference(x, positions, norm_g, ffn_w_gate, ffn_w_up, ffn_w_down,
              rwkv_mix, rwkv_w_r, rwkv_w_k, rwkv_w_v, rwkv_w_o, rwkv_w0, rwkv_w1, rwkv_w2,
              rwkv_a0, rwkv_a1, rwkv_a2, rwkv_g1, rwkv_g2, rwkv_k_k, rwkv_k_a, rwkv_r_k,
              rwkv_gn_w, rwkv_gn_b, kv_norm_g, mla_w_dkv, mla_kv_latent_g, mla_w_ukv,
              mla_w_dq, mla_q_latent_g, mla_w_uq, mla_w_o, final_norm_g):
    cos, sin = rope_tables(positions)
    h = x
    shared_kv = None
    for layer in range(DEPTH):
        if layer == N_A_LAYERS:
            shared_kv = mla_shared_kv(h, kv_norm_g, mla_w_dkv, mla_kv_latent_g, mla_w_ukv, cos, sin)
        h = h + 0.5 * swiglu(rmsnorm(h, norm_g[layer, 0]),
                             ffn_w_gate[layer, 0], ffn_w_up[layer, 0], ffn_w_down[layer, 0])
        hn = rmsnorm(h, norm_g[layer, 1])
        if layer < N_A_LAYERS:
            i = layer
            h = h + rwkv7_time_mix(hn, rwkv_mix[i], rwkv_w_r[i], rwkv_w_k[i], rwkv_w_v[i], rwkv_w_o[i],
                                   rwkv_w0[i], rwkv_w1[i], rwkv_w2[i], rwkv_a0[i], rwkv_a1[i], rwkv_a2[i],
                                   rwkv_g1[i], rwkv_g2[i], rwkv_k_k[i], rwkv_k_a[i], rwkv_r_k[i],
                                   rwkv_gn_w[i], rwkv_gn_b[i])
        else:
            j = layer - N_A_LAYERS
            k_nope, k_rope, v = shared_kv
            h = h + mla_attention(hn, mla_w_dq[j], mla_q_latent_g[j], mla_w_uq[j], mla_w_o[j],
                                  k_nope, k_rope, v, cos, sin)
        h = h + 0.5 * swiglu(rmsnorm(h, norm_g[layer, 2]),
                             ffn_w_gate[layer, 1], ffn_w_up[layer, 1], ffn_w_down[layer, 1])
    return rmsnorm(h, final_norm_g)
```

```python
import contextlib
import re
import numpy as np
import concourse.bass as bass
import concourse.mybir as mybir
from concourse.bass_utils import run_bass_kernel_spmd

F32 = mybir.dt.float32
BF16 = mybir.dt.bfloat16
I32 = mybir.dt.int32
AF = mybir.ActivationFunctionType
ALU = mybir.AluOpType
AX = mybir.AxisListType

D = 1024
DFF = 2816
NF = DFF // 128
SEQ = 4096
RMS_EPS = 1e-6

ENGS = ("pe", "act", "dve", "pool", "sp")
_PSUM_KEY = re.compile(r"^(bk|PP|PL|PT|PG|PU|PO|PS|PC|PK|PV|PR)\d*$")
CH = 30000
N_DMA_SEMS = 28
N_SW_SEMS = 8
DEBUG_UNREAD = False


class Prog:
    def __init__(self, nc, stack):
        self.nc = nc
        self.stack = stack
        self.ops = {e: [] for e in ENGS}
        self.count = {e: 0 for e in ENGS}
        self.sems = {e: [] for e in ENGS}
        self.seen = {e: {} for e in ENGS}
        self.dma_sems = [stack.enter_context(nc.semaphore(f"dq{i}")) for i in range(N_DMA_SEMS)]
        self.dma_cnt = [0] * N_DMA_SEMS
        self.dma_rr = 0
        self.dma_rr_sw = 0
        self.writers = {}
        self.readers = {}
        self.n_ops = 0
        self.unread = set()

    def _sem_for(self, e, idx):
        c = idx // CH
        while len(self.sems[e]) <= c:
            self.sems[e].append(self.stack.enter_context(self.nc.semaphore(f"s_{e}{len(self.sems[e])}")))
        return self.sems[e][c], idx % CH + 1

    def _resolve(self, tok):
        if tok[0] == "c":
            return self._sem_for(tok[1], tok[2])
        return self.dma_sems[tok[1]], tok[2]

    def _deps(self, e, reads, writes):
        toks = []
        for k in reads:
            w = self.writers.get(k)
            if w:
                toks.extend(w.values())
        for k in writes:
            w = self.writers.get(k)
            if w:
                toks.extend(w.values())
            r = self.readers.get(k)
            if r:
                toks.extend(r.values())
        best = {}
        for tok in toks:
            if tok[0] == "c" and tok[1] == e and e == "pe":
                continue
            sem, val = self._resolve(tok)
            sid = id(sem)
            if self.seen[e].get(sid, 0) >= val:
                continue
            if sid not in best or best[sid][1] < val:
                best[sid] = (sem, val)
        for sid, (sem, val) in best.items():
            self.seen[e][sid] = val
        return list(best.values())

    def _register(self, tok, slot, reads, writes, partial):
        for k in reads:
            self.readers.setdefault(k, {})[slot] = tok
        for k in writes:
            if DEBUG_UNREAD and not partial and k in self.writers and self.writers[k] and not self.readers.get(k) \
                    and slot not in self.writers[k] and k not in reads:
                self.unread.add(k)
            if partial:
                self.writers.setdefault(k, {})[slot] = tok
            else:
                self.writers[k] = {slot: tok}
            self.readers[k] = {}

    def begin_capture(self):
        self._cap = []

    def end_capture(self):
        lst, self._cap = self._cap, None
        return lst

    def replay(self, lists):
        pos = [0] * len(lists)
        lists = [l for l in lists if l]
        while any(p < len(l) for p, l in zip(pos, lists)):
            if True:
                i = min((j for j in range(len(lists)) if pos[j] < len(lists[j])), key=lambda j: (pos[j] + 0.5) / len(lists[j]))
                l = lists[i]
                if True:
                    kind, args, kw = l[pos[i]]
                    pos[i] += 1
                    if kind == "op":
                        self.op(*args, **dict(kw, sig=True))
                    else:
                        self.dma(*args, **kw)

    def op(self, e, fn, reads=(), writes=(), sig=True, partial=False):
        if getattr(self, "_cap", None) is not None:
            self._cap.append(("op", (e, fn), dict(reads=reads, writes=writes, sig=sig, partial=partial)))
            return None
        rw = [k for k in reads if _PSUM_KEY.match(k) and k not in writes]
        if rw:
            writes = list(writes) + rw
        waits = self._deps(e, reads, writes)
        idx = self.count[e]
        tok = ("c", e, idx)
        inc = None
        if sig:
            inc = self._sem_for(e, idx)[0]
            self.count[e] += 1
        self._register(tok, e, reads, writes, partial)
        self.ops[e].append((fn, waits, inc, 1))
        self.n_ops += 1
        return tok

    def dma(self, e, out, in_, reads=(), writes=(), partial=False, **kw):
        if getattr(self, "_cap", None) is not None:
            self._cap.append(("dma", (e, out, in_), dict(reads=reads, writes=writes, partial=partial, **kw)))
            return None
        if e == "pool":
            k = self.dma_rr_sw
            self.dma_rr_sw = (self.dma_rr_sw + 1) % N_SW_SEMS
        else:
            k = N_SW_SEMS + self.dma_rr
            self.dma_rr = (self.dma_rr + 1) % (N_DMA_SEMS - N_SW_SEMS)
        waits = self._deps(e, reads, writes)
        if self.dma_cnt[k] > 0:
            sem, val = self.dma_sems[k], self.dma_cnt[k]
            if self.seen[e].get(id(sem), 0) < val:
                self.seen[e][id(sem)] = val
                waits.append((sem, val))
        self.dma_cnt[k] += 16
        tok = ("d", k, self.dma_cnt[k])
        self._register(tok, ("d", k), reads, writes, partial)
        fn = lambda eng, out=out, in_=in_, kw=kw: eng.dma_start(out=out, in_=in_, **kw)
        self.ops[e].append((fn, waits, self.dma_sems[k], 16))
        self.n_ops += 1
        return tok

    def wait_all(self, e, keys):
        waits = self._deps(e, keys, ())
        self.ops[e].append((None, waits, None, 0))

    def check(self):
        if not hasattr(self, "simval"):
            self.simval = {}
        pos = {e: 0 for e in ENGS}
        while True:
            prog = False
            for e in ENGS:
                ops = self.ops[e]
                while pos[e] < len(ops):
                    fn, waits, inc, amt = ops[pos[e]]
                    if all(self.simval.get(id(s), 0) >= v for s, v in waits):
                        if inc is not None:
                            self.simval[id(inc)] = self.simval.get(id(inc), 0) + amt
                        pos[e] += 1
                        prog = True
                    else:
                        break
            if all(pos[e] == len(self.ops[e]) for e in ENGS):
                return
            if not prog:
                for e in ENGS:
                    if pos[e] < len(self.ops[e]):
                        fn, waits, inc, amt = self.ops[e][pos[e]]
                        bad = [(s.name if hasattr(s, "name") else str(s), v, self.simval.get(id(s), 0)) for s, v in waits
                               if self.simval.get(id(s), 0) < v]
                        print("DEADLOCK", e, "op#", pos[e], "of", len(self.ops[e]), "waiting", bad)
                raise RuntimeError("deadlock in recorded program")

    def emit(self):
        waits = []
        for k, sem in enumerate(self.dma_sems):
            if self.dma_cnt[k] > self.seen["sp"].get(id(sem), 0):
                waits.append((sem, self.dma_cnt[k]))
                self.seen["sp"][id(sem)] = self.dma_cnt[k]
        if waits:
            self.ops["sp"].append((None, waits, None, 0))
        self.check()
        nc = self.nc
        with nc.Block() as block:
            def mk(e):
                ops = self.ops[e]
                def body(eng):
                    for fn, waits, inc, amt in ops:
                        for sem, val in waits:
                            eng.wait_ge(sem, val)
                        if fn is None:
                            continue
                        ins = fn(eng)
                        if inc is not None:
                            ins.then_inc(inc, amt)
                return body
            block.tensor(mk("pe"))
            block.scalar(mk("act"))
            block.vector(mk("dve"))
            block.gpsimd(mk("pool"))
            block.sync(mk("sp"))
        self.ops = {e: [] for e in ENGS}


class Ctx:
    pass


_UID = [0]


def alloc(st, nc):
    _UID[0] += 1
    u = _UID[0]
    sb = lambda name, shape, dt: st.enter_context(nc.sbuf_tensor(f"{name}_{u}", shape, dt))
    ps = lambda name, shape, dt: st.enter_context(nc.psum_tensor(f"{name}_{u}", shape, dt))
    return sb, ps


def setup_consts(C):
    nc, P = C.nc, C.P
    sb, ps = alloc(C.stack, nc)
    C.ident_f = sb("ident_f", [128, 128], F32)
    C.ident_b = sb("ident_b", [128, 128], BF16)
    C.mhalf = sb("mhalf", [128, 1], F32)
    P.op("pool", lambda e: e.memset(C.ident_f[:], 0.0), writes=["ident_f"])
    P.op("pool", lambda e: e.affine_select(out=C.ident_f[:], in_=C.ident_f[:], pattern=[[-1, 128]],
                                           compare_op=ALU.not_equal, fill=1.0, base=0, channel_multiplier=1),
         reads=["ident_f"], writes=["ident_f"])
    P.op("pool", lambda e: e.tensor_copy(out=C.ident_b[:], in_=C.ident_f[:]), reads=["ident_f"], writes=["ident_b"])
    P.op("pool", lambda e: e.memset(C.mhalf[:], -0.5), writes=["mhalf"])
    P.emit()


def load_col(C, dst, vec_ap, key, nchunk):
    C.P.dma("sp", dst, vec_ap.rearrange("(c p) -> p c", p=128), writes=[key], allow_slow_non_contiguous=True)


def ffn_phase(C, S, src, dst, skey, dkey, g_ap, wg, wu, wd):
    nc, P = C.nc, C.P
    TC = min(1024, S)
    NCH = S // TC
    TT = TC // 128
    NHALF = TC // 512
    FG = 256
    NG = DFF // FG
    with contextlib.ExitStack() as st:
        sb, ps = alloc(st, nc)
        ht = [sb(f"ht{i}", [128, D], F32) for i in range(2)]
        xs = [sb(f"xs{i}", [128, D], BF16) for i in range(2)]
        junk = sb("junk", [128, D], BF16)
        ss = sb("ss", [128, 2], F32)
        vv = sb("vv", [128, 2], F32)
        rstd = sb("rstd", [128, 2], F32)
        hnT = sb("hnT", [128, 8, TC], BF16)
        actT = sb("actT", [128, NF, TC], BF16)
        wgt = [sb(f"wgt{i}", [128, 8, FG], BF16) for i in range(2)]
        wut = [sb(f"wut{i}", [128, 8, FG], BF16) for i in range(2)]
        wdt = sb("wdt", [128, NF, D], BF16)
        ho = [sb(f"ho{i}", [128, D], F32) for i in range(2)]
        hres = [sb(f"hres{i}", [128, D], F32) for i in range(2)]
        gcol = sb("gcol", [128, 8], F32)
        gfull = sb("gfull", [128, 8, 128], F32)
        sg = [sb(f"sg{i}", [128, 512], F32) for i in range(2)]
        PT = [ps(f"PT{i}", [128, D], BF16) for i in range(2)]
        PG = [ps(f"PG{i}", [128, 512], F32) for i in range(2)]
        PU = [ps(f"PU{i}", [128, 512], F32) for i in range(2)]
        PO = [ps(f"PO{i}", [128, 512], F32) for i in range(2)]

        load_col(C, gcol[:, :], g_ap, "gcol", 8)
        P.op("dve", lambda e: e.tensor_copy(out=gfull[:], in_=gcol[:, :].unsqueeze(2).to_broadcast([128, 8, 128])),
             reads=["gcol"], writes=["gfull"])

        cnt = {"a": 0, "gu": 0, "po": 0, "o": 0}

        def stage_a(c, t):
            i = cnt["a"] % 2
            cnt["a"] += 1
            gt = c * TT + t
            rows = slice(gt * 128, (gt + 1) * 128)
            P.dma("sp", ht[i][:], src[rows, :], reads=[f"{skey}{gt}"], writes=[f"ht{i}"])
            P.op("act", lambda e: e.activation(out=junk[:], in_=ht[i][:], func=AF.Square, accum_out=ss[:, i:i + 1]),
                 reads=[f"ht{i}"], writes=["junk", f"ss{i}"])
            P.op("pool", lambda e: e.tensor_scalar(out=vv[:, i:i + 1], in0=ss[:, i:i + 1], scalar1=1.0 / D, scalar2=RMS_EPS,
                                                   op0=ALU.mult, op1=ALU.add), reads=[f"ss{i}"], writes=[f"vv{i}"])
            P.op("pool", lambda e: e.tensor_tensor(out=rstd[:, i:i + 1], in0=vv[:, i:i + 1], in1=C.mhalf[:], op=ALU.pow),
                 reads=[f"vv{i}"], writes=[f"rstd{i}"])
            P.op("dve", lambda e: e.tensor_scalar(out=xs[i][:], in0=ht[i][:], scalar1=rstd[:, i:i + 1], scalar2=None,
                                                  op0=ALU.mult), reads=[f"ht{i}", f"rstd{i}"], writes=[f"xs{i}"])
            return i

        def stage_a2(c, t, i):
            for cc in range(8):
                P.op("pe", lambda e, cc=cc: e.transpose(out=PT[i][:, cc * 128:(cc + 1) * 128],
                                                        in_=xs[i][:, cc * 128:(cc + 1) * 128], identity=C.ident_b[:]),
                     reads=[f"xs{i}", "ident_b"], writes=[f"PT{i}"], sig=(cc == 7), partial=(cc > 0))
            P.op("dve", lambda e: e.tensor_tensor(out=hnT[:, :, t * 128:(t + 1) * 128],
                                                  in0=PT[i][:, :].rearrange("p (c j) -> p c j", c=8), in1=gfull[:],
                                                  op=ALU.mult),
                 reads=[f"PT{i}", "gfull"], writes=["hnT"])

        def stage_b(c):
            for g in range(NG):
                wb = g % 2
                cols = slice(g * FG, (g + 1) * FG)
                P.dma("pool", wgt[wb][:], wg[:, cols].rearrange("(c p) n -> p c n", p=128), writes=[f"wgt{wb}"])
                P.dma("pool", wut[wb][:], wu[:, cols].rearrange("(c p) n -> p c n", p=128), writes=[f"wut{wb}"])
                load_wd_piece(g)
                for fl in range(FG // 128):
                    f = g * (FG // 128) + fl
                    for hf in range(NHALF):
                        k = cnt["gu"] % 2
                        cnt["gu"] += 1
                        tok = slice(hf * 512, (hf + 1) * 512)
                        for cc in range(8):
                            P.op("pe", lambda e, cc=cc, k=k, tok=tok, fl=fl, wb=wb: e.matmul(
                                PG[k][:], lhsT=wgt[wb][:, cc, fl * 128:(fl + 1) * 128], rhs=hnT[:, cc, tok],
                                start=(cc == 0), stop=(cc == 7)),
                                 reads=[f"wgt{wb}", "hnT"], writes=[f"PG{k}"], sig=(cc == 7))
                        for cc in range(8):
                            P.op("pe", lambda e, cc=cc, k=k, tok=tok, fl=fl, wb=wb: e.matmul(
                                PU[k][:], lhsT=wut[wb][:, cc, fl * 128:(fl + 1) * 128], rhs=hnT[:, cc, tok],
                                start=(cc == 0), stop=(cc == 7)),
                                 reads=[f"wut{wb}", "hnT"], writes=[f"PU{k}"], sig=(cc == 7))
                        P.op("act", lambda e, k=k: e.activation(out=sg[k][:], in_=PG[k][:], func=AF.Silu),
                             reads=[f"PG{k}"], writes=[f"sg{k}"])
                        P.op("dve", lambda e, k=k, f=f, tok=tok: e.tensor_tensor(out=actT[:, f, tok], in0=sg[k][:], in1=PU[k][:],
                                                                               op=ALU.mult),
                             reads=[f"sg{k}", f"PU{k}"], writes=["actT"])

        def load_wd_piece(g):
            fr = slice(2 * g, 2 * g + 2)
            P.dma("pool", wdt[:, fr, :], wd[2 * g * 128:(2 * g + 2) * 128, :].rearrange("(f p) n -> p f n", p=128),
                  writes=["wdt"], partial=(g > 0))

        def hres_load(c, t):
            gt = c * TT + t
            j = gt % 2
            P.dma("sp", hres[j][:], src[gt * 128:(gt + 1) * 128, :], reads=[f"{skey}{gt}"], writes=[f"hres{j}"])

        def stage_c(c, t):
            gt = c * TT + t
            rows = slice(gt * 128, (gt + 1) * 128)
            j = gt % 2
            if t + 1 < TT:
                hres_load(c, t + 1)
            for dh in range(2):
                k = cnt["po"] % 2
                cnt["po"] += 1
                for f in range(NF):
                    P.op("pe", lambda e, f=f, k=k, dh=dh: e.matmul(
                        PO[k][:], lhsT=actT[:, f, t * 128:(t + 1) * 128], rhs=wdt[:, f, dh * 512:(dh + 1) * 512],
                        start=(f == 0), stop=(f == NF - 1)),
                         reads=["actT", "wdt"], writes=[f"PO{k}"], sig=(f == NF - 1))
                P.op("dve", lambda e, k=k, dh=dh, j=j: e.scalar_tensor_tensor(
                    out=ho[j][:, dh * 512:(dh + 1) * 512], in0=PO[k][:], scalar=0.5, in1=hres[j][:, dh * 512:(dh + 1) * 512],
                    op0=ALU.mult, op1=ALU.add),
                     reads=[f"PO{k}", f"hres{j}"], writes=[f"ho{j}"], partial=(dh > 0))
            P.dma("sp", dst[rows, :], ho[j][:], reads=[f"ho{j}"], writes=[f"{dkey}{gt}"])

        for t in range(TT):
            stage_a2(0, t, stage_a(0, t))
        for c in range(NCH):
            stage_b(c)
            nxt = stage_a(c + 1, 0) if c + 1 < NCH else None
            hres_load(c, 0)
            for t in range(TT):
                nxt2 = stage_a(c + 1, t + 1) if (c + 1 < NCH and t + 1 < TT) else None
                stage_c(c, t)
                if nxt is not None:
                    stage_a2(c + 1, t, nxt)
                nxt = nxt2
        P.emit()


def final_norm_phase(C, S, src, dst, skey, dkey, g_ap):
    nc, P = C.nc, C.P
    with contextlib.ExitStack() as st:
        sb, ps = alloc(st, nc)
        ht = [sb(f"ht{i}", [128, D], F32) for i in range(2)]
        ot = [sb(f"ot{i}", [128, D], F32) for i in range(2)]
        junk = sb("junk", [128, D], BF16)
        ss = sb("ss", [128, 2], F32)
        vv = sb("vv", [128, 2], F32)
        rstd = sb("rstd", [128, 2], F32)
        gb = sb("gb", [128, D], F32)
        P.dma("sp", gb[:], g_ap.partition_broadcast(128), writes=["gb"])
        def fload(t):
            P.dma("sp", ht[t % 2][:], src[t * 128:(t + 1) * 128, :], reads=[f"{skey}{t}"], writes=[f"ht{t % 2}"])

        fload(0)
        for t in range(S // 128):
            i = t % 2
            rows = slice(t * 128, (t + 1) * 128)
            if t + 1 < S // 128:
                fload(t + 1)
            P.op("act", lambda e, i=i: e.activation(out=junk[:], in_=ht[i][:], func=AF.Square, accum_out=ss[:, i:i + 1]),
                 reads=[f"ht{i}"], writes=["junk", f"ss{i}"])
            P.op("pool", lambda e, i=i: e.tensor_scalar(out=vv[:, i:i + 1], in0=ss[:, i:i + 1], scalar1=1.0 / D, scalar2=RMS_EPS,
                                                        op0=ALU.mult, op1=ALU.add), reads=[f"ss{i}"], writes=[f"vv{i}"])
            P.op("pool", lambda e, i=i: e.tensor_tensor(out=rstd[:, i:i + 1], in0=vv[:, i:i + 1], in1=C.mhalf[:], op=ALU.pow),
                 reads=[f"vv{i}"], writes=[f"rstd{i}"])
            P.op("dve", lambda e, i=i: e.scalar_tensor_tensor(out=ot[i][:], in0=ht[i][:], scalar=rstd[:, i:i + 1], in1=gb[:],
                                                              op0=ALU.mult, op1=ALU.mult),
                 reads=[f"ht{i}", f"rstd{i}", "gb"], writes=[f"ot{i}"])
            P.dma("sp", dst[rows, :], ot[i][:], reads=[f"ot{i}"], writes=[f"{dkey}{t}"])
        P.wait_all("sp", [f"{dkey}{t}" for t in range(S // 128)])
        P.emit()


HS = 64
NH = 16
GN_EPS = 64e-5


def bcast_load(C, dst, vec_ap, key):
    C.P.dma("sp", dst, vec_ap.partition_broadcast(128), writes=[key])


def rwkv_pass_a_v1(C, S, h, hkey, W, scr):
    nc, P = C.nc, C.P
    NT = S // 128
    with contextlib.ExitStack() as st:
        sb, ps = alloc(st, nc)
        wr = sb("wr", [128, 8, D], BF16)
        wk = sb("wk", [128, 8, D], BF16)
        wv = sb("wv", [128, 8, D], BF16)
        w1 = sb("w1", [128, 8, 64], BF16)
        a1 = sb("a1", [128, 8, 64], BF16)
        g1 = sb("g1", [128, 8, 128], BF16)
        w2 = sb("w2", [64, D], BF16)
        a2 = sb("a2", [64, D], BF16)
        g2 = sb("g2", [128, D], BF16)
        w0b = sb("w0b", [128, D], F32)
        a0b = sb("a0b", [128, D], F32)
        kkb = sb("kkb", [128, D], F32)
        kab = sb("kab", [128, D], F32)
        rkb = sb("rkb", [128, D], F32)
        gcol = sb("gcol", [128, 8], F32)
        gfull = sb("gfull", [128, 8, 128], F32)
        mixc = sb("mixc", [128, 6, 8], F32)
        trif = sb("trif", [128, 128], F32)
        ones = sb("ones", [128, 1], F32)
        ht = [sb(f"ht{i}", [128, D], F32) for i in range(2)]
        xs = sb("xs", [128, D], BF16)
        junk = sb("junk", [128, D], BF16)
        st4 = sb("st4", [128, 8], F32)
        hnTe = [sb(f"hnTe{i}", [128, 8, 130], BF16) for i in range(2)]
        dxT = sb("dxT", [128, 8, 128], F32)
        tmpT = [sb(f"tmpT{i}", [128, 8, 128], F32) for i in range(2)]
        xT = [sb(f"xT{i}", [128, 8, 128], BF16) for i in range(2)]
        l1 = [sb(f"l1{i}", [128, 128], BF16) for i in range(3)]
        T = [sb(f"T{i}", [128, D], F32) for i in range(10)]
        ob = [sb(f"ob{i}", [128, D], BF16) for i in range(5)]
        s16 = sb("s16", [128, 4, 16], F32)
        gct = sb("gct", [128, 8], F32)
        PT = ps("PT", [128, D], BF16)
        PP = [ps(f"PP{i}", [128, D], F32) for i in range(2)]
        PL = ps("PL", [128, 512], F32)

        for wt, nm in ((wr, "rwkv_w_r"), (wk, "rwkv_w_k"), (wv, "rwkv_w_v")):
            for q in range(2):
                P.dma("pool", wt[:, :, q * 512:(q + 1) * 512], W[nm][:, q * 512:(q + 1) * 512].rearrange("(c p) n -> p c n", p=128),
                      writes=[nm], partial=(q > 0))
        for wt, nm in ((w1, "rwkv_w1"), (a1, "rwkv_a1"), (g1, "rwkv_g1")):
            P.dma("pool", wt[:], W[nm].rearrange("(c p) n -> p c n", p=128), writes=[nm])
        for wt, nm in ((w2, "rwkv_w2"), (a2, "rwkv_a2"), (g2, "rwkv_g2")):
            P.dma("pool", wt[:], W[nm], writes=[nm])
        for wt, nm in ((w0b, "rwkv_w0"), (a0b, "rwkv_a0"), (kkb, "rwkv_k_k"), (kab, "rwkv_k_a"), (rkb, "rwkv_r_k")):
            bcast_load(C, wt[:], W[nm], nm)
        load_col(C, gcol[:, :], W["norm_g"], "gcol", 8)
        P.op("dve", lambda e: e.tensor_copy(out=gfull[:], in_=gcol[:, :].unsqueeze(2).to_broadcast([128, 8, 128])),
             reads=["gcol"], writes=["gfull"])
        P.dma("sp", mixc[:], W["rwkv_mix"].rearrange("i (c p) -> p i c", p=128), writes=["mixc"], allow_slow_non_contiguous=True)
        P.op("pool", lambda e: e.memset(trif[:], 1.0), writes=["trif"])
        P.op("pool", lambda e: e.affine_select(out=trif[:], in_=trif[:], pattern=[[1, 128]], compare_op=ALU.is_ge, fill=0.0,
                                               base=0, channel_multiplier=-1), reads=["trif"], writes=["trif"])
        P.op("pool", lambda e: e.memset(ones[:], 1.0), writes=["ones"])
        P.op("pool", lambda e: e.memset(hnTe[0][:, :, 0:2], 0.0), writes=["hnTe0"])

        def proj(xbuf, wt, wkey, dstf, evac_key):
            k = proj.n % 2
            proj.n += 1
            for hf in range(2):
                for cc in range(8):
                    P.op("pe", lambda e, cc=cc, hf=hf, k=k: e.matmul(PP[k][:, hf * 512:(hf + 1) * 512], lhsT=xT[xbuf][:, cc, :],
                                                                    rhs=wt[:, cc, hf * 512:(hf + 1) * 512], start=(cc == 0), stop=(cc == 7)),
                         reads=[f"xT{xbuf}", wkey], writes=[f"PP{k}"], sig=(cc == 7 and hf == 1), partial=(hf > 0 or cc > 0))
            dstf(k)
        proj.n = 0

        def lora(xbuf, wt1, k1key, width, li, func, wt2, k2key, dstf):
            for cc in range(8):
                P.op("pe", lambda e, cc=cc: e.matmul(PL[0:width, li * 128:(li + 1) * 128], lhsT=wt1[:, cc, :], rhs=xT[xbuf][:, cc, :],
                                                     start=(cc == 0), stop=(cc == 7)),
                     reads=[f"xT{xbuf}", k1key], writes=[f"PL{li}"], sig=(cc == 7))
            P.op("act", lambda e: e.activation(out=l1[li][0:width, :], in_=PL[0:width, li * 128:(li + 1) * 128], func=func),
                 reads=[f"PL{li}"], writes=[f"l1{li}"])
            k = proj.n % 2
            proj.n += 1
            for hf in range(2):
                P.op("pe", lambda e, hf=hf, k=k: e.matmul(PP[k][:, hf * 512:(hf + 1) * 512], lhsT=l1[li][0:width, :],
                                                          rhs=wt2[0:width, hf * 512:(hf + 1) * 512], start=True, stop=True),
                     reads=[f"l1{li}", k2key], writes=[f"PP{k}"], sig=(hf == 1), partial=(hf > 0))
            dstf(k)

        mixn = [0]

        def mix(i):
            b = mixn[0] % 2
            mixn[0] += 1
            e1 = "dve" if b == 0 else "pool"
            P.op(e1, lambda e: e.tensor_tensor(out=tmpT[b][:], in0=dxT[:], in1=mixc[:, i, :].unsqueeze(2).to_broadcast([128, 8, 128]),
                                               op=ALU.mult), reads=["dxT", "mixc"], writes=[f"tmpT{b}"])
            cur = cur_ap[0]
            P.op(e1, lambda e: e.tensor_tensor(out=xT[b][:], in0=tmpT[b][:], in1=cur, op=ALU.add),
                 reads=[f"tmpT{b}", cur_key[0]], writes=[f"xT{b}"])
            return b

        cur_ap = [None]
        cur_key = [None]

        for t in range(NT):
            i = t % 2
            rows = slice(t * 128, (t + 1) * 128)
            hb = hnTe[i]
            P.dma("sp", ht[i][:], h[rows, :], reads=[f"{hkey}{t}"], writes=[f"ht{i}"])
            P.op("act", lambda e, i=i: e.activation(out=junk[:], in_=ht[i][:], func=AF.Square, accum_out=st4[:, 0:1]),
                 reads=[f"ht{i}"], writes=["junk", "st0"])
            P.op("pool", lambda e: e.tensor_scalar(out=st4[:, 1:2], in0=st4[:, 0:1], scalar1=1.0 / D, scalar2=RMS_EPS,
                                                   op0=ALU.mult, op1=ALU.add), reads=["st0"], writes=["st1"])
            P.op("pool", lambda e: e.tensor_tensor(out=st4[:, 2:3], in0=st4[:, 1:2], in1=C.mhalf[:], op=ALU.pow),
                 reads=["st1"], writes=["st2"])
            P.op("dve", lambda e, i=i: e.tensor_scalar(out=xs[:], in0=ht[i][:], scalar1=st4[:, 2:3], scalar2=None, op0=ALU.mult),
                 reads=[f"ht{i}", "st2"], writes=["xs"])
            for cc in range(8):
                P.op("pe", lambda e, cc=cc: e.transpose(out=PT[:, cc * 128:(cc + 1) * 128], in_=xs[:, cc * 128:(cc + 1) * 128],
                                                        identity=C.ident_b[:]),
                     reads=["xs", "ident_b"], writes=["PT"], sig=(cc == 7), partial=(cc > 0))
            P.op("dve", lambda e, hb=hb: e.tensor_tensor(out=hb[:, :, 2:130], in0=PT[:, :].rearrange("p (c j) -> p c j", c=8),
                                                         in1=gfull[:], op=ALU.mult),
                 reads=["PT", "gfull"], writes=[f"hnTe{i}"])
            if t + 1 < NT:
                P.op("pool", lambda e, hb=hb, i=i: e.tensor_copy(out=hnTe[1 - i][:, :, 0:2], in_=hb[:, :, 128:130]),
                     reads=[f"hnTe{i}"], writes=[f"hnTe{1 - i}"])
            cur_ap[0] = hb[:, :, 2:130]
            cur_key[0] = f"hnTe{i}"
            P.op("dve", lambda e, hb=hb: e.tensor_tensor(out=dxT[:], in0=hb[:, :, 1:129], in1=hb[:, :, 2:130], op=ALU.subtract),
                 reads=[f"hnTe{i}"], writes=["dxT"])
            R, KR, V, WP, AL, G, KK, KM, GA, TM = T
            RT, AT, BT, KT, VB = ob
            b = mix(0)
            proj(b, wr, "rwkv_w_r", lambda k: P.op("act", lambda e: e.activation(out=R[:], in_=PP[k][:], func=AF.Copy),
                                                   reads=[f"PP{k}"], writes=["R"]), "R")
            b = mix(1)
            lora(b, w1, "rwkv_w1", 64, 0, AF.Tanh, w2, "rwkv_w2",
                 lambda k: P.op("dve", lambda e: e.tensor_tensor(out=WP[:], in0=PP[k][:], in1=w0b[:], op=ALU.add),
                                reads=[f"PP{k}", "rwkv_w0"], writes=["WP"]))
            b = mix(2)
            proj(b, wk, "rwkv_w_k", lambda k: P.op("act", lambda e: e.activation(out=KR[:], in_=PP[k][:], func=AF.Copy),
                                                   reads=[f"PP{k}"], writes=["KR"]), "KR")
            b = mix(3)

            def evac_v(k):
                P.op("act", lambda e: e.activation(out=V[:], in_=PP[k][:], func=AF.Copy), reads=[f"PP{k}"], writes=["V"])
                P.op("pool", lambda e: e.tensor_copy(out=VB[:], in_=V[:]), reads=["V"], writes=["VB"])
            proj(b, wv, "rwkv_w_v", evac_v, "V")
            b = mix(4)

            def evac_a(k):
                P.op("dve", lambda e: e.tensor_tensor(out=AL[:], in0=PP[k][:], in1=a0b[:], op=ALU.add),
                     reads=[f"PP{k}", "rwkv_a0"], writes=["AL"])
                P.op("act", lambda e: e.activation(out=AL[:], in_=AL[:], func=AF.Sigmoid), reads=["AL"], writes=["AL"])
            lora(b, a1, "rwkv_a1", 64, 1, AF.Copy, a2, "rwkv_a2", evac_a)
            b = mix(5)
            lora(b, g1, "rwkv_g1", 128, 2, AF.Sigmoid, g2, "rwkv_g2",
                 lambda k: P.op("act", lambda e: e.activation(out=G[:], in_=PP[k][:], func=AF.Copy),
                                reads=[f"PP{k}"], writes=["G"]))
            P.dma("sp", scr["G"][rows, :], G[:], reads=["G"], writes=[f"sG{t}"])
            v3 = lambda ap: ap.rearrange("p (h n) -> p h n", h=NH)
            bc = lambda col: col.unsqueeze(2).to_broadcast([128, NH, HS])
            P.op("act", lambda e: e.activation(out=WP[:], in_=WP[:], func=AF.Exp, scale=-1.0), reads=["WP"], writes=["WP"])
            P.op("act", lambda e: e.activation(out=WP[:], in_=WP[:], func=AF.Ln, bias=1.0), reads=["WP"], writes=["WP"])
            P.op("act", lambda e: e.activation(out=WP[:], in_=WP[:], func=AF.Exp, scale=-1.0, bias=-0.5), reads=["WP"], writes=["WP"])
            P.op("dve", lambda e: e.tensor_tensor(out=KK[:], in0=KR[:], in1=kkb[:], op=ALU.mult), reads=["KR", "rwkv_k_k"], writes=["KK"])
            P.op("pool", lambda e: e.tensor_tensor(out=TM[:], in0=KK[:], in1=KK[:], op=ALU.mult), reads=["KK"], writes=["TM"])
            P.op("dve", lambda e: e.tensor_reduce(out=s16[:, 0, :], in_=v3(TM[:]), axis=AX.X, op=ALU.add), reads=["TM"], writes=["s16a"])
            P.op("pool", lambda e: e.tensor_scalar(out=s16[:, 1, :], in0=s16[:, 0, :], scalar1=1e-24, scalar2=None, op0=ALU.max),
                 reads=["s16a"], writes=["s16b"])
            P.op("pool", lambda e: e.tensor_tensor(out=s16[:, 1, :], in0=s16[:, 1, :], in1=C.mhalf[:, 0:1].to_broadcast([128, NH]),
                                                   op=ALU.pow), reads=["s16b"], writes=["s16b"])
            P.op("dve", lambda e: e.tensor_tensor(out=v3(KK[:]), in0=v3(KK[:]), in1=bc(s16[:, 1, :]), op=ALU.mult),
                 reads=["KK", "s16b"], writes=["KK"])
            P.op("dve", lambda e: e.scalar_tensor_tensor(out=KM[:], in0=AL[:], scalar=-1.0, in1=kab[:], op0=ALU.add, op1=ALU.mult),
                 reads=["AL", "rwkv_k_a"], writes=["KM"])
            P.op("dve", lambda e: e.scalar_tensor_tensor(out=KM[:], in0=KM[:], scalar=1.0, in1=KR[:], op0=ALU.add, op1=ALU.mult),
                 reads=["KM", "KR"], writes=["KM"])
            P.op("pool", lambda e: e.tensor_tensor(out=TM[:], in0=R[:], in1=rkb[:], op=ALU.mult), reads=["R", "rwkv_r_k"], writes=["TM"])
            P.op("pool", lambda e: e.tensor_tensor(out=TM[:], in0=TM[:], in1=KM[:], op=ALU.mult), reads=["TM", "KM"], writes=["TM"])
            P.op("dve", lambda e: e.tensor_reduce(out=s16[:, 2, :], in_=v3(TM[:]), axis=AX.X, op=ALU.add), reads=["TM"], writes=["s16c"])
            P.op("pool", lambda e: e.tensor_tensor(out=v3(TM[:]), in0=v3(V[:]), in1=bc(s16[:, 2, :]), op=ALU.mult),
                 reads=["V", "s16c"], writes=["TM"])
            P.dma("sp", scr["BON"][rows, :], TM[:], reads=["TM"], writes=[f"sBON{t}"])
            k = proj.n % 2
            proj.n += 1
            for hf in range(2):
                P.op("pe", lambda e, hf=hf, k=k: e.matmul(PP[k][:, hf * 512:(hf + 1) * 512], lhsT=trif[:], rhs=WP[:, hf * 512:(hf + 1) * 512],
                                                          start=True, stop=True),
                     reads=["trif", "WP"], writes=[f"PP{k}"], sig=(hf == 1), partial=(hf > 0))
            for cc in range(8):
                P.op("pe", lambda e, cc=cc: e.matmul(PL[:, 384 + cc:385 + cc], lhsT=WP[:, cc * 128:(cc + 1) * 128], rhs=ones[:],
                                                     start=True, stop=True),
                     reads=["WP", "ones"], writes=["PL3"], sig=(cc == 7), partial=(cc > 0))
            P.op("act", lambda e: e.activation(out=gct[:], in_=PL[:, 384:392], func=AF.Exp, scale=-1.0), reads=["PL3"], writes=["gct"])
            P.dma("sp", scr["GC"][t], gct[:], reads=["gct"], writes=[f"sGC{t}"])
            P.op("act", lambda e, k=k: e.activation(out=GA[:], in_=PP[k][:], func=AF.Exp, scale=-1.0), reads=[f"PP{k}"], writes=["GA"])
            P.op("pool", lambda e: e.tensor_tensor(out=RT[:], in0=R[:], in1=GA[:], op=ALU.mult), reads=["R", "GA"], writes=["RT"])
            P.dma("sp", scr["RT"][rows, :], RT[:], reads=["RT"], writes=[f"sRT{t}"])
            P.op("act", lambda e, k=k: e.activation(out=GA[:], in_=PP[k][:], func=AF.Exp), reads=[f"PP{k}"], writes=["GA"])
            P.op("dve", lambda e: e.tensor_tensor(out=KT[:], in0=KM[:], in1=GA[:], op=ALU.mult), reads=["KM", "GA"], writes=["KT"])
            P.dma("sp", scr["KT"][rows, :], KT[:], reads=["KT"], writes=[f"sKT{t}"])
            P.op("pool", lambda e: e.tensor_tensor(out=TM[:], in0=KK[:], in1=AL[:], op=ALU.mult), reads=["KK", "AL"], writes=["TM"])
            P.op("dve", lambda e: e.tensor_tensor(out=BT[:], in0=TM[:], in1=GA[:], op=ALU.mult), reads=["TM", "GA"], writes=["BT"])
            P.dma("sp", scr["BT"][rows, :], BT[:], reads=["BT"], writes=[f"sBT{t}"])
            P.op("dve", lambda e, k=k: e.tensor_tensor(out=TM[:], in0=PP[k][:], in1=WP[:], op=ALU.subtract),
                 reads=[f"PP{k}", "WP"], writes=["TM"])
            P.op("act", lambda e: e.activation(out=TM[:], in_=TM[:], func=AF.Exp, scale=-1.0), reads=["TM"], writes=["TM"])
            P.op("dve", lambda e: e.scalar_tensor_tensor(out=AT[:], in0=KK[:], scalar=-1.0, in1=TM[:], op0=ALU.mult, op1=ALU.mult),
                 reads=["KK", "TM"], writes=["AT"])
            P.dma("sp", scr["AT"][rows, :], AT[:], reads=["AT"], writes=[f"sAT{t}"])
            P.dma("sp", scr["VB"][rows, :], VB[:], reads=["VB"], writes=[f"sVB{t}"])
        P.emit()


def rwkv_pass_a(C, S, h, hkey, W, scr):
    nc, P = C.nc, C.P
    NT = S // 128
    HW = 512
    with contextlib.ExitStack() as st:
        sb, ps = alloc(st, nc)
        wr = sb("wr", [128, 8, D], BF16)
        wk = sb("wk", [128, 8, D], BF16)
        wv = sb("wv", [128, 8, D], BF16)
        w1 = sb("w1", [128, 2, 8, 64], BF16)
        a1 = sb("a1", [128, 2, 8, 64], BF16)
        g1 = sb("g1", [128, 2, 8, 128], BF16)
        w2 = sb("w2", [64, D], BF16)
        a2 = sb("a2", [64, D], BF16)
        g2 = sb("g2", [128, D], BF16)
        w0b = sb("w0b", [128, D], F32)
        a0b = sb("a0b", [128, D], F32)
        kkb = sb("kkb", [128, D], F32)
        kab = sb("kab", [128, D], F32)
        rkb = sb("rkb", [128, D], F32)
        gcol = sb("gcol", [128, 8], F32)
        gfull = sb("gfull", [128, 8, 128], F32)
        mixc = sb("mixc", [128, 6, 8], F32)
        trif = sb("trif", [128, 128], F32)
        ones = sb("ones", [128, 1], F32)
        ht = [sb(f"ht{i}", [128, D], F32) for i in range(2)]
        xs = sb("xs", [128, D], BF16)
        junk = sb("junk", [128, D], BF16)
        st4 = sb("st4", [128, 8], F32)
        hnTe = [sb(f"hnTe{i}", [128, 8, 130], BF16) for i in range(2)]
        dxT = sb("dxT", [128, 8, 128], F32)
        dxb = sb("dxb", [128, 8, 128], BF16)
        tmpT = sb("tmpT", [128, 8, 128], F32)
        xT = [[sb(f"xT{j}_{i}", [128, 8, 128], BF16) for i in range(3)] for j in range(2)]
        l1 = [[sb(f"l1{j}_{i}", [128, 128], BF16) for i in range(3)] for j in range(2)]
        TS = [[sb(f"T{u}_{i}", [128, HW], F32) for i in range(10)] for u in range(2)]
        OB = [[sb(f"ob{u}_{i}", [128, HW], BF16) for i in range(5)] for u in range(2)]
        s16 = [sb(f"s16_{u}", [128, 4, 8], F32) for u in range(2)]
        gct = [sb(f"gct{u}", [128, 4], F32) for u in range(2)]
        PT = ps("PT", [128, D], BF16)
        PPn = 4
        PP = [ps(f"PP{i}", [128, HW], F32) for i in range(PPn)]
        PCm = [ps(f"PC{i}", [128, HW], F32) for i in range(2)]
        PL = ps("PL", [128, 512], F32)

        for wt, nm in ((wr, "rwkv_w_r"), (wk, "rwkv_w_k"), (wv, "rwkv_w_v")):
            for q in range(2):
                P.dma("pool", wt[:, :, q * 512:(q + 1) * 512], W[nm][:, q * 512:(q + 1) * 512].rearrange("(c p) n -> p c n", p=128),
                      writes=[nm], partial=(q > 0))
        for wt, nm in ((w1, "rwkv_w1"), (a1, "rwkv_a1"), (g1, "rwkv_g1")):
            P.dma("pool", wt[:, 0, :, :], W[nm].rearrange("(c p) n -> p c n", p=128), writes=[nm])
        for wt, nm in ((w2, "rwkv_w2"), (a2, "rwkv_a2"), (g2, "rwkv_g2")):
            P.dma("pool", wt[:], W[nm], writes=[nm])
        for wt, nm in ((w0b, "rwkv_w0"), (a0b, "rwkv_a0"), (kkb, "rwkv_k_k"), (kab, "rwkv_k_a"), (rkb, "rwkv_r_k")):
            bcast_load(C, wt[:], W[nm], nm)
        load_col(C, gcol[:, :], W["norm_g"], "gcol", 8)
        P.op("dve", lambda e: e.tensor_copy(out=gfull[:], in_=gcol[:, :].unsqueeze(2).to_broadcast([128, 8, 128])),
             reads=["gcol"], writes=["gfull"])
        P.dma("sp", mixc[:], W["rwkv_mix"].rearrange("i (c p) -> p i c", p=128), writes=["mixc"], allow_slow_non_contiguous=True)
        for wt, nm, mi, wd_ in ((w1, "rwkv_w1", 1, 64), (a1, "rwkv_a1", 4, 64), (g1, "rwkv_g1", 5, 128)):
            P.op("dve", lambda e, wt=wt, mi=mi, wd_=wd_: e.tensor_tensor(out=wt[:, 1, :, :], in0=wt[:, 0, :, :],
                                                                         in1=mixc[:, mi, :].unsqueeze(2).to_broadcast([128, 8, wd_]), op=ALU.mult),
                 reads=[nm, "mixc"], writes=[nm])
        P.op("pool", lambda e: e.memset(trif[:], 1.0), writes=["trif"])
        P.op("pool", lambda e: e.affine_select(out=trif[:], in_=trif[:], pattern=[[1, 128]], compare_op=ALU.is_ge, fill=0.0,
                                               base=0, channel_multiplier=-1), reads=["trif"], writes=["trif"])
        P.op("pool", lambda e: e.memset(ones[:], 1.0), writes=["ones"])
        P.op("pool", lambda e: e.memset(hnTe[0][:, :, 0:2], 0.0), writes=["hnTe0"])
        ppn = [0]

        def next_pp():
            k = ppn[0] % PPn
            ppn[0] += 1
            return k

        def do_tile(t):
            i = t % 2
            rows = slice(t * 128, (t + 1) * 128)
            hb = hnTe[i]
            hk = f"hnTe{i}"
            cur = hb[:, :, 2:130]
            P.dma("sp", ht[i][:], h[rows, :], reads=[f"{hkey}{t}"], writes=[f"ht{i}"])
            P.op("act", lambda e: e.activation(out=junk[:], in_=ht[i][:], func=AF.Square, accum_out=st4[:, 0:1]),
                 reads=[f"ht{i}"], writes=["junk", "st0"])
            P.op("pool", lambda e: e.tensor_scalar(out=st4[:, 1:2], in0=st4[:, 0:1], scalar1=1.0 / D, scalar2=RMS_EPS,
                                                   op0=ALU.mult, op1=ALU.add), reads=["st0"], writes=["st1"])
            P.op("pool", lambda e: e.tensor_tensor(out=st4[:, 2:3], in0=st4[:, 1:2], in1=C.mhalf[:], op=ALU.pow),
                 reads=["st1"], writes=["st2"])
            P.op("dve", lambda e: e.tensor_scalar(out=xs[:], in0=ht[i][:], scalar1=st4[:, 2:3], scalar2=None, op0=ALU.mult),
                 reads=[f"ht{i}", "st2"], writes=["xs"])
            for cc in range(8):
                P.op("pe", lambda e, cc=cc: e.transpose(out=PT[:, cc * 128:(cc + 1) * 128], in_=xs[:, cc * 128:(cc + 1) * 128],
                                                        identity=C.ident_b[:]),
                     reads=["xs", "ident_b"], writes=["PT"], sig=(cc == 7), partial=(cc > 0))
            P.op("dve", lambda e: e.tensor_tensor(out=hb[:, :, 2:130], in0=PT[:, :].rearrange("p (c j) -> p c j", c=8),
                                                  in1=gfull[:], op=ALU.mult),
                 reads=["PT", "gfull"], writes=[hk])
            if t + 1 < NT:
                P.op("pool", lambda e: e.tensor_copy(out=hnTe[1 - i][:, :, 0:2], in_=hb[:, :, 128:130]),
                     reads=[hk], writes=[f"hnTe{1 - i}"])
            P.op("dve", lambda e: e.tensor_tensor(out=dxT[:], in0=hb[:, :, 1:129], in1=hb[:, :, 2:130], op=ALU.subtract),
                 reads=[hk], writes=["dxT"])
            P.op("act", lambda e: e.activation(out=dxb[:], in_=dxT[:], func=AF.Copy), reads=["dxT"], writes=["dxb"])
            for xi, mi in ((0, 0), (1, 2), (2, 3)):
                e1 = "dve" if xi != 1 else "pool"
                P.op(e1, lambda e, mi=mi: e.tensor_tensor(out=tmpT[:], in0=dxT[:], in1=mixc[:, mi, :].unsqueeze(2).to_broadcast([128, 8, 128]),
                                                          op=ALU.mult), reads=["dxT", "mixc"], writes=["tmpT"])
                P.op(e1, lambda e, xi=xi: e.tensor_tensor(out=xT[i][xi][:], in0=tmpT[:], in1=cur, op=ALU.add),
                     reads=["tmpT", hk], writes=[f"xT{i}_{xi}"])
            for li, (wt, nm, width, func) in enumerate(((w1, "rwkv_w1", 64, AF.Tanh), (a1, "rwkv_a1", 64, AF.Copy), (g1, "rwkv_g1", 128, AF.Sigmoid))):
                for cc in range(8):
                    P.op("pe", lambda e, cc=cc, wt=wt, width=width, li=li: e.matmul(PL[0:width, li * 128:(li + 1) * 128], lhsT=wt[:, 0, cc, :], rhs=hb[:, cc, 2:130],
                                                                                    start=(cc == 0), stop=False),
                         reads=[hk, nm], writes=["PL"], sig=False)
                for cc in range(8):
                    P.op("pe", lambda e, cc=cc, wt=wt, width=width, li=li: e.matmul(PL[0:width, li * 128:(li + 1) * 128], lhsT=wt[:, 1, cc, :], rhs=dxb[:, cc, :],
                                                                                    start=False, stop=(cc == 7)),
                         reads=["dxb", nm], writes=["PL"], sig=(cc == 7), partial=True)
                P.op("act", lambda e, li=li, width=width, func=func: e.activation(out=l1[i][li][0:width, :], in_=PL[0:width, li * 128:(li + 1) * 128], func=func),
                     reads=["PL"], writes=[f"l1{i}_{li}"])

        def unit_x(t, hf):
            rows = slice(t * 128, (t + 1) * 128)
            u = (2 * t + hf) % 2
            cs = slice(hf * HW, (hf + 1) * HW)
            R, KR, V, WP, AL, G, KK, KM, GA, TM = TS[u]
            RT, AT, BT, KT, VB = OB[u]
            T_ = lambda n: f"T{u}_{n}"
            O_ = lambda n: f"ob{u}_{n}"
            v3 = lambda ap: ap.rearrange("p (h n) -> p h n", h=8)
            bc = lambda col: col.unsqueeze(2).to_broadcast([128, 8, HS])
            sx = s16[u]

            ti = t % 2
            kn = [0]

            def next_k():
                kn[0] += 1
                return 2 * hf + kn[0] % 2

            def proj(xi, wt, wkey):
                k = next_k()
                for cc in range(8):
                    P.op("pe", lambda e, cc=cc: e.matmul(PP[k][:, :], lhsT=xT[ti][xi][:, cc, :], rhs=wt[:, cc, cs], start=(cc == 0), stop=(cc == 7)),
                         reads=[f"xT{ti}_{xi}", wkey], writes=[f"PP{k}"], sig=(cc == 7))
                return k

            def lora2(li, width, wt2, k2key):
                k = next_k()
                P.op("pe", lambda e: e.matmul(PP[k][:, :], lhsT=l1[ti][li][0:width, :], rhs=wt2[0:width, cs], start=True, stop=True),
                     reads=[f"l1{ti}_{li}", k2key], writes=[f"PP{k}"])
                return k

            kr_ = proj(0, wr, "rwkv_w_r")
            P.op("act", lambda e: e.activation(out=R[:], in_=PP[kr_][:], func=AF.Copy), reads=[f"PP{kr_}"], writes=[T_("R")])
            kw_ = lora2(0, 64, w2, "rwkv_w2")
            P.op("dve", lambda e: e.tensor_tensor(out=WP[:], in0=PP[kw_][:], in1=w0b[:, cs], op=ALU.add), reads=[f"PP{kw_}", "rwkv_w0"], writes=[T_("WP")])
            kk_ = proj(1, wk, "rwkv_w_k")
            P.op("act", lambda e: e.activation(out=KR[:], in_=PP[kk_][:], func=AF.Copy), reads=[f"PP{kk_}"], writes=[T_("KR")])
            ka_ = lora2(1, 64, a2, "rwkv_a2")
            P.op("dve", lambda e: e.tensor_tensor(out=AL[:], in0=PP[ka_][:], in1=a0b[:, cs], op=ALU.add), reads=[f"PP{ka_}", "rwkv_a0"], writes=[T_("AL")])
            kv_ = proj(2, wv, "rwkv_w_v")
            P.op("act", lambda e: e.activation(out=V[:], in_=PP[kv_][:], func=AF.Copy), reads=[f"PP{kv_}"], writes=[T_("V")])
            P.op("act", lambda e: e.activation(out=VB[:], in_=PP[kv_][:], func=AF.Copy), reads=[f"PP{kv_}"], writes=[O_("VB")])
            kg_ = lora2(2, 128, g2, "rwkv_g2")
            P.op("act", lambda e: e.activation(out=G[:], in_=PP[kg_][:], func=AF.Copy), reads=[f"PP{kg_}"], writes=[T_("G")])
            P.dma("sp", scr["VB"][rows, cs], VB[:], reads=[O_("VB")], writes=[f"sVB{t}"], partial=True)
            P.dma("sp", scr["G"][rows, cs], G[:], reads=[T_("G")], writes=[f"sG{t}"], partial=True)
            P.op("act", lambda e: e.activation(out=AL[:], in_=AL[:], func=AF.Sigmoid), reads=[T_("AL")], writes=[T_("AL")])
            P.op("act", lambda e: e.activation(out=WP[:], in_=WP[:], func=AF.Exp, scale=-1.0), reads=[T_("WP")], writes=[T_("WP")])
            P.op("act", lambda e: e.activation(out=WP[:], in_=WP[:], func=AF.Ln, bias=1.0), reads=[T_("WP")], writes=[T_("WP")])
            P.op("act", lambda e: e.activation(out=WP[:], in_=WP[:], func=AF.Exp, scale=-1.0, bias=-0.5), reads=[T_("WP")], writes=[T_("WP")])
            return dict(t=t, hf=hf)

        def unit_y(stt):
            t, hf = stt["t"], stt["hf"]
            rows = slice(t * 128, (t + 1) * 128)
            u = (2 * t + hf) % 2
            cs = slice(hf * HW, (hf + 1) * HW)
            R, KR, V, WP, AL, G, KK, KM, GA, TM = TS[u]
            RT, AT, BT, KT, VB = OB[u]
            T_ = lambda n: f"T{u}_{n}"
            O_ = lambda n: f"ob{u}_{n}"
            v3 = lambda ap: ap.rearrange("p (h n) -> p h n", h=8)
            bc = lambda col: col.unsqueeze(2).to_broadcast([128, 8, HS])
            sx = s16[u]
            PCu = PCm[u]
            P.op("pe", lambda e: e.matmul(PCu[:, :], lhsT=trif[:], rhs=WP[:], start=True, stop=True),
                 reads=["trif", T_("WP")], writes=[f"PC{u}"])
            P.op("dve", lambda e: e.tensor_tensor(out=KK[:], in0=KR[:], in1=kkb[:, cs], op=ALU.mult), reads=[T_("KR"), "rwkv_k_k"], writes=[T_("KK")])
            P.op("act", lambda e: e.activation(out=TM[:], in_=KK[:], func=AF.Square), reads=[T_("KK")], writes=[T_("TM")])
            P.op("dve", lambda e: e.tensor_reduce(out=sx[:, 0, :], in_=v3(TM[:]), axis=AX.X, op=ALU.add), reads=[T_("TM")], writes=[f"sa{u}"])
            P.op("pool", lambda e: e.tensor_scalar(out=sx[:, 1, :], in0=sx[:, 0, :], scalar1=1e-24, scalar2=None, op0=ALU.max),
                 reads=[f"sa{u}"], writes=[f"sb{u}"])
            P.op("pool", lambda e: e.tensor_tensor(out=sx[:, 1, :], in0=sx[:, 1, :], in1=C.mhalf[:, 0:1].to_broadcast([128, 8]), op=ALU.pow),
                 reads=[f"sb{u}"], writes=[f"sb{u}"])
            P.op("pool", lambda e: e.tensor_tensor(out=v3(KK[:]), in0=v3(KK[:]), in1=bc(sx[:, 1, :]), op=ALU.mult),
                 reads=[T_("KK"), f"sb{u}"], writes=[T_("KK")])
            P.op("dve", lambda e: e.scalar_tensor_tensor(out=KM[:], in0=AL[:], scalar=-1.0, in1=kab[:, cs], op0=ALU.add, op1=ALU.mult),
                 reads=[T_("AL"), "rwkv_k_a"], writes=[T_("KM")])
            P.op("dve", lambda e: e.scalar_tensor_tensor(out=KM[:], in0=KM[:], scalar=1.0, in1=KR[:], op0=ALU.add, op1=ALU.mult),
                 reads=[T_("KM"), T_("KR")], writes=[T_("KM")])
            P.op("pool", lambda e: e.tensor_tensor(out=TM[:], in0=R[:], in1=rkb[:, cs], op=ALU.mult), reads=[T_("R"), "rwkv_r_k", f"sa{u}"], writes=[T_("TM")])
            P.op("dve", lambda e: e.tensor_tensor(out=TM[:], in0=TM[:], in1=KM[:], op=ALU.mult), reads=[T_("TM"), T_("KM")], writes=[T_("TM")])
            P.op("dve", lambda e: e.tensor_reduce(out=sx[:, 2, :], in_=v3(TM[:]), axis=AX.X, op=ALU.add), reads=[T_("TM")], writes=[f"sc{u}"])
            P.op("pool", lambda e: e.tensor_tensor(out=v3(TM[:]), in0=v3(V[:]), in1=bc(sx[:, 2, :]), op=ALU.mult),
                 reads=[T_("V"), f"sc{u}"], writes=[T_("TM")])
            P.dma("sp", scr["BON"][rows, cs], TM[:], reads=[T_("TM")], writes=[f"sBON{t}"], partial=True)
            P.op("act", lambda e: e.activation(out=GA[:], in_=PCu[:], func=AF.Exp, scale=-1.0), reads=[f"PC{u}"], writes=[T_("GA")])
            P.dma("sp", scr["GC"][t:t + 1, cs], GA[127:128, :], reads=[T_("GA")], writes=[f"sGC{t}"], partial=True)
            P.op("pool", lambda e: e.tensor_tensor(out=RT[:], in0=R[:], in1=GA[:], op=ALU.mult), reads=[T_("R"), T_("GA")], writes=[O_("RT")])
            P.dma("sp", scr["RT"][rows, cs], RT[:], reads=[O_("RT")], writes=[f"sRT{t}"], partial=True)
            P.op("act", lambda e: e.activation(out=GA[:], in_=PCu[:], func=AF.Exp), reads=[f"PC{u}", O_("RT")], writes=[T_("GA")])
            P.op("dve", lambda e: e.tensor_tensor(out=KT[:], in0=KM[:], in1=GA[:], op=ALU.mult), reads=[T_("KM"), T_("GA")], writes=[O_("KT")])
            P.dma("sp", scr["KT"][rows, cs], KT[:], reads=[O_("KT")], writes=[f"sKT{t}"], partial=True)
            P.op("pool", lambda e: e.tensor_tensor(out=KM[:], in0=KK[:], in1=AL[:], op=ALU.mult), reads=[T_("KK"), T_("AL"), O_("KT")], writes=[T_("KM")])
            P.op("dve", lambda e: e.tensor_tensor(out=BT[:], in0=KM[:], in1=GA[:], op=ALU.mult), reads=[T_("KM"), T_("GA")], writes=[O_("BT")])
            P.dma("sp", scr["BT"][rows, cs], BT[:], reads=[O_("BT")], writes=[f"sBT{t}"], partial=True)
            P.op("dve", lambda e: e.tensor_tensor(out=G[:], in0=PCu[:], in1=WP[:], op=ALU.subtract),
                 reads=[f"PC{u}", T_("WP"), f"sG{t}"], writes=[T_("G")])
            P.op("act", lambda e: e.activation(out=G[:], in_=G[:], func=AF.Exp, scale=-1.0), reads=[T_("G")], writes=[T_("G")])
            P.op("dve", lambda e: e.scalar_tensor_tensor(out=AT[:], in0=KK[:], scalar=-1.0, in1=G[:], op0=ALU.mult, op1=ALU.mult),
                 reads=[T_("KK"), T_("G")], writes=[O_("AT")])
            P.dma("sp", scr["AT"][rows, cs], AT[:], reads=[O_("AT")], writes=[f"sAT{t}"], partial=True)

        do_tile(0)
        for t in range(NT):
            streams = []
            for hf in range(2):
                P.begin_capture()
                unit_y(unit_x(t, hf))
                streams.append(P.end_capture())
            if t + 1 < NT:
                P.begin_capture()
                do_tile(t + 1)
                streams.append(P.end_capture())
            P.replay(streams)
        P.emit()


def rwkv_pass_b_v2(C, S, h, hkey, W, scr):
    nc, P = C.nc, C.P
    NT = S // 128
    with contextlib.ExitStack() as st:
        sb, ps = alloc(st, nc)
        wo = sb("wo", [128, 8, D], BF16)
        gnw = sb("gnw", [128, D], F32)
        gnb = sb("gnb", [128, D], F32)
        mk1 = sb("mk1", [128, 256], F32)
        mksl = sb("mksl", [128, 128], F32)
        inb = [[sb(f"in{n}_{i}", [128, D], BF16) for n in range(5)] for i in range(2)]
        gb = [sb(f"gb{i}", [128, D], F32) for i in range(2)]
        bon = [sb(f"bon{i}", [128, D], F32) for i in range(2)]
        gc = [sb(f"gc{i}", [128, 8], F32) for i in range(2)]
        hres = [sb(f"hres{i}", [128, D], F32) for i in range(2)]
        ARt = sb("ARt", [128, 8, 2, 128], BF16)
        BtT = sb("BtT", [128, 8, 128], BF16)
        KtT = sb("KtT", [128, 8, 128], BF16)
        AB1 = sb("AB1", [128, NH, 256], BF16)
        AK1 = sb("AK1", [128, NH, 256], BF16)
        M0 = sb("M0", [128, 8, 128], BF16)
        Mb = [sb(f"Mb{i}", [128, 8, 128], BF16) for i in range(2)]
        MTb = [sb(f"MTb{i}", [128, 8, 128], BF16) for i in range(2)]
        X32 = sb("X32", [128, NH, 128], F32)
        Xb = sb("Xb", [128, 8, 128], BF16)
        W1c = sb("W1c", [128, NH, 64], BF16)
        W1T = sb("W1T", [128, 8, 128], BF16)
        S32 = sb("S32", [128, 8, 64], F32)
        Sb = sb("Sb", [128, 8, 64], BF16)
        Ub = sb("Ub", [128, D], BF16)
        Ysb = sb("Ysb", [128, D], F32)
        Ysq = sb("Ysq", [128, D], F32)
        s16 = sb("s16", [128, 6, 16], F32)
        otm = sb("otm", [128, D], BF16)
        oT = sb("oT", [128, 8, 128], BF16)
        hout = [sb(f"hout{i}", [128, D], F32) for i in range(2)]
        BK = [ps(f"BK{i}", [128, 512], F32) for i in range(8)]
        bkb = lambda b: BK[b][:, :].bitcast(BF16)

        for q in range(2):
            P.dma("pool", wo[:, :, q * 512:(q + 1) * 512], W["rwkv_w_o"][:, q * 512:(q + 1) * 512].rearrange("(c p) n -> p c n", p=128),
                  writes=["wo"], partial=(q > 0))
        bcast_load(C, gnw[:], W["rwkv_gn_w"], "gnw")
        bcast_load(C, gnb[:], W["rwkv_gn_b"], "gnb")
        P.op("pool", lambda e: e.memset(mk1[:], 1.0), writes=["mk1"])
        P.op("pool", lambda e: e.affine_select(out=mk1[:, 0:128], in_=mk1[:, 0:128], pattern=[[1, 128]], compare_op=ALU.is_gt, fill=0.0,
                                               base=0, channel_multiplier=-1), reads=["mk1"], writes=["mk1"])
        P.op("pool", lambda e: e.affine_select(out=mk1[:, 128:256], in_=mk1[:, 128:256], pattern=[[1, 128]], compare_op=ALU.is_ge, fill=0.0,
                                               base=0, channel_multiplier=-1), reads=["mk1"], writes=["mk1"])
        P.op("pool", lambda e: e.memset(mksl[:], 1.0), writes=["mksl"])
        P.op("pool", lambda e: e.affine_select(out=mksl[:], in_=mksl[:], pattern=[[-1, 128]], compare_op=ALU.is_gt, fill=0.0,
                                               base=0, channel_multiplier=1), reads=["mksl"], writes=["mksl"])
        P.op("pool", lambda e: e.memset(S32[:], 0.0), writes=["S32"])
        P.op("pool", lambda e: e.memset(Sb[:], 0.0), writes=["Sb"])

        v3 = lambda ap: ap.rearrange("p (h n) -> p h n", h=NH)
        bc = lambda col: col.unsqueeze(2).to_broadcast([128, NH, HS])
        names = ["RT", "AT", "BT", "KT", "VB"]
        evn = [0]

        def evac_copy(out, in_, reads, writes):
            e1 = "act" if evn[0] % 2 == 0 else "dve"
            evn[0] += 1
            if e1 == "act":
                P.op("act", lambda e: e.activation(out=out, in_=in_, func=AF.Copy), reads=reads, writes=writes)
            else:
                P.op("dve", lambda e: e.tensor_copy(out=out, in_=in_), reads=reads, writes=writes)

        def load(t):
            i = t % 2
            rows = slice(t * 128, (t + 1) * 128)
            for n in range(5):
                P.dma("act" if n % 2 else "sp", inb[i][n][:], scr[names[n]][rows, :], reads=[f"s{names[n]}{t}"], writes=[f"in{n}_{i}"])
            P.dma("sp", gb[i][:], scr["G"][rows, :], reads=[f"sG{t}"], writes=[f"gb{i}"])
            P.dma("act", bon[i][:], scr["BON"][rows, :], reads=[f"sBON{t}"], writes=[f"bon{i}"])
            P.dma("sp", gc[i][:], scr["GC"][t].rearrange("(c p) -> p c", p=128), reads=[f"sGC{t}"], writes=[f"gc{i}"],
                  allow_slow_non_contiguous=True)
            P.dma("act", hres[i][:], h[rows, :], reads=[f"{hkey}{t}"], writes=[f"hres{i}"])

        load(0)

        def do_tile(t):
            i = t % 2
            rows = slice(t * 128, (t + 1) * 128)
            if t + 1 < NT:
                load(t + 1)
            RT, AT, BT, KT, VB = inb[i]
            kin = [f"in{n}_{i}" for n in range(5)]
            for n, (src_t, dst_ap, dkey) in enumerate(((AT, ARt[:, :, 0, :], "ARt"), (RT, ARt[:, :, 1, :], "ARt"),
                                                       (BT, BtT[:, :, :], "BtT"), (KT, KtT[:, :, :], "KtT"))):
                b = 6 + n % 2
                for cc in range(8):
                    P.op("pe", lambda e, cc=cc, b=b, src_t=src_t: e.transpose(out=bkb(b)[:, cc * 128:(cc + 1) * 128],
                                                                             in_=src_t[:, cc * 128:(cc + 1) * 128], identity=C.ident_b[:]),
                         reads=[kin[[1, 0, 2, 3][n]], "ident_b"], writes=[f"bk{b}"], sig=(cc == 7), partial=(cc > 0))
                evac_copy(dst_ap, bkb(b).rearrange("p (c j) -> p c j", c=8), [f"bk{b}"], [dkey] if n != 1 else ["ARt"])
            for rd in range(2):
                for cl in range(4):
                    c = rd * 4 + cl
                    base = (cl % 2) * 4
                    for hh in range(2):
                        h_ = 2 * c + hh
                        hl = h_ - rd * 8
                        p0 = 64 * hh
                        ps_ = slice(p0, p0 + 64)
                        b1 = base + 2 * hh
                        b2 = base + 2 * hh + 1
                        P.op("pe", lambda e, c=c, ps_=ps_, b1=b1: e.matmul(BK[b1][:, 0:256], lhsT=BtT[ps_, c, :],
                                                                          rhs=ARt[ps_, c, :, :], start=True, stop=True),
                             reads=["BtT", "ARt"], writes=[f"bk{b1}"], sig=False)
                        P.op("pe", lambda e, c=c, ps_=ps_, b1=b1: e.matmul(BK[b1][:, 256:384], lhsT=ARt[ps_, c, 0, :],
                                                                          rhs=BtT[ps_, c, :], start=True, stop=True),
                             reads=["BtT", "ARt"], writes=[f"bk{b1}"], partial=True)
                        P.op("pe", lambda e, c=c, ps_=ps_, b2=b2: e.matmul(BK[b2][:, 0:256], lhsT=KtT[ps_, c, :],
                                                                          rhs=ARt[ps_, c, :, :], start=True, stop=True),
                             reads=["KtT", "ARt"], writes=[f"bk{b2}"])
                        P.op("dve", lambda e, h_=h_, b1=b1: e.tensor_tensor(out=AB1[:, h_, :], in0=BK[b1][:, 0:256], in1=mk1[:], op=ALU.mult),
                             reads=[f"bk{b1}", "mk1"], writes=["AB1"])
                        P.op("dve", lambda e, hl=hl, b1=b1: e.tensor_tensor(out=M0[:, hl, :], in0=BK[b1][:, 256:384], in1=mksl[:], op=ALU.mult),
                             reads=[f"bk{b1}", "mksl"], writes=["M0"])
                        P.op("dve", lambda e, h_=h_, b2=b2: e.tensor_tensor(out=AK1[:, h_, :], in0=BK[b2][:, 0:256], in1=mk1[:], op=ALU.mult),
                             reads=[f"bk{b2}", "mk1"], writes=["AK1"])
                hs_ = slice(rd * 8, rd * 8 + 8)
                P.op("pool", lambda e, hs_=hs_, rd=rd: e.tensor_copy(out=X32[:, hs_, 0:64],
                                                                    in_=AT[:, rd * 512:(rd + 1) * 512].rearrange("p (h n) -> p h n", h=8)),
                     reads=[kin[1]], writes=["X32_0", "X32_1"])
                for hl in range(8):
                    h_ = rd * 8 + hl
                    bX = 3 * (hl // 4)
                    P.op("pe", lambda e, hl=hl, h_=h_, bX=bX: e.matmul(BK[bX][:, (hl % 4) * 128 + 64:(hl % 4) * 128 + 128], lhsT=AK1[:, h_, 0:128],
                                                                        rhs=VB[:, h_ * 64:(h_ + 1) * 64], start=True, stop=True),
                         reads=["AK1", kin[4]], writes=[f"bk{bX}"], sig=(hl % 4 == 3), partial=(hl % 4 > 0))
                for sq in range(2):
                    hsq = slice(rd * 8 + sq * 4, rd * 8 + sq * 4 + 4)
                    P.op("act", lambda e, sq=sq, hsq=hsq: e.activation(out=X32[:, hsq, 64:128],
                                                                       in_=BK[3 * sq][:, :].rearrange("p (h n) -> p h n", h=4)[:, :, 64:128], func=AF.Copy),
                         reads=[f"bk{3 * sq}", f"X32_{sq}"], writes=[f"X32_{sq}"])
                    P.op("act", lambda e, sq=sq, hsq=hsq: e.activation(out=Xb[:, sq * 4:sq * 4 + 4, :], in_=X32[:, hsq, :], func=AF.Copy),
                         reads=[f"X32_{sq}"], writes=[f"Xb{sq}"])
                for L in range(7):
                    pp = L % 2
                    for sq in range(2):
                        bX, bM, bMT = 3 * sq, 3 * sq + 1, 3 * sq + 2
                        hsq = slice(rd * 8 + sq * 4, rd * 8 + sq * 4 + 4)
                        mk_ = [f"MTb{pp}_{sq}", f"Mb{pp}_{sq}"]
                        for hq in range(4):
                            hl = sq * 4 + hq
                            h_ = rd * 8 + hl
                            mt = AB1[:, h_, 0:128] if L == 0 else MTb[pp][:, hl, :]
                            P.op("pe", lambda e, hl=hl, hq=hq, mt=mt, bX=bX: e.matmul(BK[bX][:, hq * 128:(hq + 1) * 128], lhsT=mt, rhs=Xb[:, hl, :],
                                                                                      start=True, stop=True),
                                 reads=["AB1" if L == 0 else mk_[0], f"Xb{sq}"], writes=[f"bk{bX}"], sig=(hq == 3), partial=(hq > 0))
                        if L < 6:
                            for hq in range(4):
                                hl = sq * 4 + hq
                                h_ = rd * 8 + hl
                                mt = AB1[:, h_, 0:128] if L == 0 else MTb[pp][:, hl, :]
                                m = M0[:, hl, :] if L == 0 else Mb[pp][:, hl, :]
                                rk = ["AB1", "M0"] if L == 0 else mk_
                                P.op("pe", lambda e, hq=hq, mt=mt, m=m, bM=bM: e.matmul(BK[bM][:, hq * 128:(hq + 1) * 128], lhsT=mt, rhs=m,
                                                                                        start=True, stop=True),
                                     reads=rk, writes=[f"bk{bM}"], sig=(hq == 3), partial=(hq > 0))
                                P.op("pe", lambda e, hq=hq, mt=mt, m=m, bMT=bMT: e.matmul(BK[bMT][:, hq * 128:(hq + 1) * 128], lhsT=m, rhs=mt,
                                                                                          start=True, stop=True),
                                     reads=rk, writes=[f"bk{bMT}"], sig=(hq == 3), partial=(hq > 0))
                        P.op("dve", lambda e, hsq=hsq, bX=bX: e.tensor_tensor(out=X32[:, hsq, :], in0=BK[bX][:, :].rearrange("p (h n) -> p h n", h=4),
                                                                             in1=X32[:, hsq, :], op=ALU.add),
                             reads=[f"bk{bX}", f"X32_{sq}"], writes=[f"X32_{sq}"])
                        if L < 6:
                            P.op("act", lambda e, sq=sq, hsq=hsq: e.activation(out=Xb[:, sq * 4:sq * 4 + 4, :], in_=X32[:, hsq, :], func=AF.Copy),
                                 reads=[f"X32_{sq}"], writes=[f"Xb{sq}"])
                            P.op("act", lambda e, sq=sq, pp=pp, bM=bM: e.activation(out=Mb[1 - pp][:, sq * 4:sq * 4 + 4, :],
                                                                                    in_=BK[bM][:, :].rearrange("p (h n) -> p h n", h=4), func=AF.Copy),
                                 reads=[f"bk{bM}"], writes=[f"Mb{1 - pp}_{sq}"])
                            P.op("dve", lambda e, sq=sq, pp=pp, bMT=bMT: e.tensor_copy(out=MTb[1 - pp][:, sq * 4:sq * 4 + 4, :],
                                                                                       in_=BK[bMT][:, :].rearrange("p (h n) -> p h n", h=4)),
                                 reads=[f"bk{bMT}"], writes=[f"MTb{1 - pp}_{sq}"])
                P.op("pool", lambda e, hs_=hs_: e.tensor_copy(out=W1c[:, hs_, :], in_=X32[:, hs_, 0:64]), reads=["X32_0", "X32_1"], writes=["W1c"], partial=(rd > 0))
            for cc in range(8):
                P.op("pe", lambda e, cc=cc: e.transpose(out=bkb(6)[:, cc * 128:(cc + 1) * 128],
                                                        in_=W1c[:, 2 * cc:2 * cc + 2, :].rearrange("p h n -> p (h n)"), identity=C.ident_b[:]),
                     reads=["W1c", "ident_b"], writes=["bk6"], sig=(cc == 7), partial=(cc > 0))
            evac_copy(W1T[:, :, :], bkb(6).rearrange("p (c j) -> p c j", c=8), ["bk6"], ["W1T"])
            for h_ in range(NH):
                c, p0 = h_ // 2, 64 * (h_ % 2)
                P.op("pe", lambda e, h_=h_, c=c, p0=p0: e.matmul(BK[h_ % 2][:, c * 64:(c + 1) * 64], lhsT=W1T[p0:p0 + 64, c, :],
                                                                  rhs=Sb[p0:p0 + 64, c, :], start=True, stop=True),
                     reads=["W1T", "Sb"], writes=[f"bk{h_ % 2}"], sig=(h_ >= 14), partial=(h_ >= 2))
            for bq in range(2):
                P.op("dve", lambda e, bq=bq: e.tensor_tensor(out=Ub[:, :].rearrange("p (c two n) -> p c two n", two=2, n=64)[:, :, bq, :],
                                                             in0=BK[bq][:, :].rearrange("p (h n) -> p h n", h=8),
                                                             in1=X32[:, :, :].rearrange("p (c two) n -> p c two n", two=2)[:, :, bq, 64:128], op=ALU.add),
                     reads=[f"bk{bq}", "X32_0", "X32_1"], writes=["Ub"], partial=(bq > 0))
            for h_ in range(NH):
                c, p0 = h_ // 2, 64 * (h_ % 2)
                ob_ = BK[2 + h_ % 2][:, c * 64:(c + 1) * 64]
                P.op("pe", lambda e, c=c, p0=p0, ob_=ob_: e.matmul(ob_, lhsT=ARt[p0:p0 + 64, c, 1, :], rhs=Sb[p0:p0 + 64, c, :], start=True, stop=False),
                     reads=["ARt", "Sb"], writes=[f"bk{2 + h_ % 2}"], sig=False, partial=(h_ >= 2))
                P.op("pe", lambda e, h_=h_, ob_=ob_: e.matmul(ob_, lhsT=AB1[:, h_, 128:256], rhs=Ub[:, h_ * 64:(h_ + 1) * 64], start=False, stop=False),
                     reads=["AB1", "Ub"], writes=[f"bk{2 + h_ % 2}"], sig=False, partial=True)
                P.op("pe", lambda e, h_=h_, ob_=ob_: e.matmul(ob_, lhsT=AK1[:, h_, 128:256], rhs=VB[:, h_ * 64:(h_ + 1) * 64], start=False, stop=True),
                     reads=["AK1", kin[4]], writes=[f"bk{2 + h_ % 2}"], sig=(h_ >= 14), partial=True)
            for c in range(8):
                ob_ = BK[4 + c // 4][:, (c % 4) * 128:(c % 4 + 1) * 128]
                P.op("pe", lambda e, c=c, ob_=ob_: e.matmul(ob_, lhsT=BT[:, c * 128:(c + 1) * 128], rhs=Ub[:, c * 128:(c + 1) * 128], start=True, stop=False),
                     reads=[kin[2], "Ub"], writes=[f"bk{4 + c // 4}"], sig=False, partial=(c % 4 > 0))
                P.op("pe", lambda e, c=c, ob_=ob_: e.matmul(ob_, lhsT=KT[:, c * 128:(c + 1) * 128], rhs=VB[:, c * 128:(c + 1) * 128], start=False, stop=True),
                     reads=[kin[3], kin[4]], writes=[f"bk{4 + c // 4}"], sig=(c % 4 == 3), partial=True)
            for bq in range(2):
                for hh in range(2):
                    ps_ = slice(64 * hh, 64 * hh + 64)
                    P.op("dve", lambda e, bq=bq, hh=hh, ps_=ps_: e.tensor_tensor(
                        out=S32[ps_, bq * 4:bq * 4 + 4, :], in0=BK[4 + bq][ps_, :].rearrange("p (c n) -> p c n", c=4)[:, :, hh * 64:(hh + 1) * 64],
                        in1=S32[ps_, bq * 4:bq * 4 + 4, :], op=ALU.add),
                         reads=[f"bk{4 + bq}", "S32", "Sb"], writes=["S32"], partial=True)
            P.op("dve", lambda e, i=i: e.tensor_tensor(out=S32[:], in0=S32[:], in1=gc[i][:, :].unsqueeze(2).to_broadcast([128, 8, 64]), op=ALU.mult),
                 reads=["S32", f"gc{i}"], writes=["S32"])
            P.op("pool", lambda e: e.tensor_copy(out=Sb[:], in_=S32[:]), reads=["S32"], writes=["Sb"])
            for bq in range(2):
                yv = lambda ap, bq=bq: ap.rearrange("p (c two n) -> p c two n", two=2, n=64)[:, :, bq, :]
                P.op("act", lambda e, bq=bq, yv=yv: e.activation(out=yv(Ysb[:, :]), in_=BK[2 + bq][:, :].rearrange("p (c n) -> p c n", c=8), func=AF.Copy),
                     reads=[f"bk{2 + bq}"], writes=["Ysb"], partial=(bq > 0))
                P.op("act", lambda e, bq=bq, yv=yv: e.activation(out=yv(Ysq[:, :]), in_=BK[2 + bq][:, :].rearrange("p (c n) -> p c n", c=8), func=AF.Square),
                     reads=[f"bk{2 + bq}"], writes=["Ysq"], partial=(bq > 0))
            P.op("dve", lambda e: e.tensor_reduce(out=s16[:, 0, :], in_=v3(Ysb[:]), axis=AX.X, op=ALU.add), reads=["Ysb"], writes=["q0"])
            P.op("dve", lambda e: e.tensor_reduce(out=s16[:, 1, :], in_=v3(Ysq[:]), axis=AX.X, op=ALU.add), reads=["Ysq"], writes=["q1"])
            P.op("pool", lambda e: e.tensor_scalar(out=s16[:, 2, :], in0=s16[:, 0, :], scalar1=1.0 / HS, scalar2=None, op0=ALU.mult),
                 reads=["q0"], writes=["q2"])
            P.op("pool", lambda e: e.tensor_tensor(out=s16[:, 3, :], in0=s16[:, 2, :], in1=s16[:, 2, :], op=ALU.mult), reads=["q2"], writes=["q3"])
            P.op("dve", lambda e: e.scalar_tensor_tensor(out=s16[:, 4, :], in0=s16[:, 1, :], scalar=1.0 / HS, in1=s16[:, 3, :],
                                                         op0=ALU.mult, op1=ALU.subtract), reads=["q1", "q3"], writes=["q4"])
            P.op("pool", lambda e: e.tensor_scalar(out=s16[:, 4, :], in0=s16[:, 4, :], scalar1=GN_EPS, scalar2=None, op0=ALU.add),
                 reads=["q4"], writes=["q4"])
            P.op("pool", lambda e: e.tensor_tensor(out=s16[:, 5, :], in0=s16[:, 4, :], in1=C.mhalf[:, 0:1].to_broadcast([128, NH]), op=ALU.pow),
                 reads=["q4"], writes=["q5"])
            P.op("dve", lambda e: e.tensor_tensor(out=v3(Ysb[:]), in0=v3(Ysb[:]), in1=bc(s16[:, 2, :]), op=ALU.subtract),
                 reads=["Ysb", "q2"], writes=["Ysb"])
            P.op("pool", lambda e: e.tensor_tensor(out=v3(Ysb[:]), in0=v3(Ysb[:]), in1=bc(s16[:, 5, :]), op=ALU.mult),
                 reads=["Ysb", "q5"], writes=["Ysb"])
            P.op("dve", lambda e: e.tensor_tensor(out=Ysb[:], in0=Ysb[:], in1=gnw[:], op=ALU.mult), reads=["Ysb", "gnw"], writes=["Ysb"])
            P.op("pool", lambda e: e.tensor_tensor(out=Ysb[:], in0=Ysb[:], in1=gnb[:], op=ALU.add), reads=["Ysb", "gnb"], writes=["Ysb"])
            P.op("dve", lambda e, i=i: e.tensor_tensor(out=Ysb[:], in0=Ysb[:], in1=bon[i][:], op=ALU.add), reads=["Ysb", f"bon{i}"], writes=["Ysb"])
            P.op("pool", lambda e, i=i: e.tensor_tensor(out=otm[:], in0=Ysb[:], in1=gb[i][:], op=ALU.mult), reads=["Ysb", f"gb{i}"], writes=["otm"])
            for cc in range(8):
                P.op("pe", lambda e, cc=cc: e.transpose(out=bkb(7)[:, cc * 128:(cc + 1) * 128], in_=otm[:, cc * 128:(cc + 1) * 128],
                                                        identity=C.ident_b[:]),
                     reads=["otm", "ident_b"], writes=["bk7"], sig=(cc == 7), partial=(cc > 0))
            evac_copy(oT[:, :, :], bkb(7).rearrange("p (c j) -> p c j", c=8), ["bk7"], ["oT"])
            for hf in range(2):
                for cc in range(8):
                    P.op("pe", lambda e, cc=cc, hf=hf: e.matmul(BK[hf][:, :], lhsT=oT[:, cc, :], rhs=wo[:, cc, hf * 512:(hf + 1) * 512],
                                                                start=(cc == 0), stop=(cc == 7)),
                         reads=["oT", "wo"], writes=[f"bk{hf}"], sig=(cc == 7))
                P.op("dve", lambda e, hf=hf, i=i: e.tensor_tensor(out=hout[i][:, hf * 512:(hf + 1) * 512], in0=BK[hf][:, :],
                                                                  in1=hres[i][:, hf * 512:(hf + 1) * 512], op=ALU.add),
                     reads=[f"bk{hf}", f"hres{i}"], writes=[f"hout{i}"], partial=(hf > 0))
            P.dma("sp", h[rows, :], hout[i][:], reads=[f"hout{i}"], writes=[f"{hkey}{t}"])

        for t in range(NT):
            do_tile(t)
        P.emit()


def rwkv_pass_b(C, S, h, hkey, W, scr):
    nc, P = C.nc, C.P
    NT = S // 128
    with contextlib.ExitStack() as st:
        sb, ps = alloc(st, nc)
        wo = sb("wo", [128, 8, D], BF16)
        gnw = sb("gnw", [128, D], F32)
        gnb = sb("gnb", [128, D], F32)
        mk1 = sb("mk1", [128, 256], F32)
        mksl = sb("mksl", [128, 128], F32)
        inb = [[sb(f"in{n}_{i}", [128, D], BF16) for n in range(5)] for i in range(2)]
        gb = sb("gb", [128, D], F32)
        bon = sb("bon", [128, D], F32)
        gc = [sb(f"gc{i}", [128, 8], F32) for i in range(2)]
        hres = sb("hres", [128, D], F32)
        ARt = [sb(f"ARt{i}", [128, 8, 2, 128], BF16) for i in range(2)]
        BtT = sb("BtT", [128, 8, 128], BF16)
        KtT = sb("KtT", [128, 8, 128], BF16)
        AB1 = [sb(f"AB1{i}", [128, NH, 256], BF16) for i in range(2)]
        AK1 = [sb(f"AK1{i}", [128, NH, 256], BF16) for i in range(2)]
        M0 = sb("M0", [128, 8, 128], BF16)
        Mb = [sb(f"Mb{i}", [128, 8, 128], BF16) for i in range(2)]
        MTb = [sb(f"MTb{i}", [128, 8, 128], BF16) for i in range(2)]
        X32 = [sb(f"X32{i}", [128, NH, 128], F32) for i in range(2)]
        Xb = sb("Xb", [128, 8, 128], BF16)
        W1c = sb("W1c", [128, NH, 64], BF16)
        W1T = [sb(f"W1T{i}", [128, 8, 128], BF16) for i in range(2)]
        S32 = sb("S32", [128, 8, 64], F32)
        Sb = sb("Sb", [128, 8, 64], BF16)
        Ub = sb("Ub", [128, D], BF16)
        Ysb = sb("Ysb", [128, D], F32)
        Ysq = sb("Ysq", [128, D], F32)
        s16 = sb("s16", [128, 6, 16], F32)
        otm = sb("otm", [128, D], BF16)
        oT = sb("oT", [128, 8, 128], BF16)
        BK = [ps(f"BK{i}", [128, 512], F32) for i in range(8)]
        bkb = lambda b: BK[b][:, :].bitcast(BF16)

        for q in range(2):
            P.dma("pool", wo[:, :, q * 512:(q + 1) * 512], W["rwkv_w_o"][:, q * 512:(q + 1) * 512].rearrange("(c p) n -> p c n", p=128),
                  writes=["wo"], partial=(q > 0))
        bcast_load(C, gnw[:], W["rwkv_gn_w"], "gnw")
        bcast_load(C, gnb[:], W["rwkv_gn_b"], "gnb")
        P.op("pool", lambda e: e.memset(mk1[:], 1.0), writes=["mk1"])
        P.op("pool", lambda e: e.affine_select(out=mk1[:, 0:128], in_=mk1[:, 0:128], pattern=[[1, 128]], compare_op=ALU.is_gt, fill=0.0,
                                               base=0, channel_multiplier=-1), reads=["mk1"], writes=["mk1"])
        P.op("pool", lambda e: e.affine_select(out=mk1[:, 128:256], in_=mk1[:, 128:256], pattern=[[1, 128]], compare_op=ALU.is_ge, fill=0.0,
                                               base=0, channel_multiplier=-1), reads=["mk1"], writes=["mk1"])
        P.op("pool", lambda e: e.memset(mksl[:], 1.0), writes=["mksl"])
        P.op("pool", lambda e: e.affine_select(out=mksl[:], in_=mksl[:], pattern=[[-1, 128]], compare_op=ALU.is_gt, fill=0.0,
                                               base=0, channel_multiplier=1), reads=["mksl"], writes=["mksl"])
        P.op("pool", lambda e: e.memset(S32[:], 0.0), writes=["S32"])
        P.op("pool", lambda e: e.memset(Sb[:], 0.0), writes=["Sb"])

        v3 = lambda ap: ap.rearrange("p (h n) -> p h n", h=NH)
        bc = lambda col: col.unsqueeze(2).to_broadcast([128, NH, HS])
        names = ["RT", "AT", "BT", "KT", "VB"]

        def head(t):
            i = t % 2
            rows = slice(t * 128, (t + 1) * 128)
            RT, AT, BT, KT, VB = inb[i]
            kin = [f"in{n}_{i}" for n in range(5)]
            ARt_, AB1_, AK1_, X32_, W1T_ = ARt[i], AB1[i], AK1[i], X32[i], W1T[i]
            kA, kAB, kAK, kW = f"ARt{i}", f"AB1{i}", f"AK1{i}", f"W1T{i}"
            kX = lambda sq: f"X32{i}_{sq}"
            for n in range(5):
                P.dma("act" if n % 2 else "sp", inb[i][n][:], scr[names[n]][rows, :], reads=[f"s{names[n]}{t}"], writes=[kin[n]])
            P.dma("sp", gc[i][:], scr["GC"][t].rearrange("(c p) -> p c", p=128), reads=[f"sGC{t}"], writes=[f"gc{i}"],
                  allow_slow_non_contiguous=True)
            for n, (src_t, dst_ap, dkey, skey) in enumerate(((AT, ARt_[:, :, 0, :], kA, kin[1]), (RT, ARt_[:, :, 1, :], kA, kin[0]),
                                                             (BT, BtT[:, :, :], "BtT", kin[2]), (KT, KtT[:, :, :], "KtT", kin[3]))):
                b = 4 + n % 2
                for cc in range(8):
                    P.op("pe", lambda e, cc=cc, b=b, src_t=src_t: e.transpose(out=bkb(b)[:, cc * 128:(cc + 1) * 128],
                                                                             in_=src_t[:, cc * 128:(cc + 1) * 128], identity=C.ident_b[:]),
                         reads=[skey, "ident_b"], writes=[f"bk{b}"], sig=(cc == 7), partial=(cc > 0))
                if n % 2 == 0:
                    P.op("act", lambda e, dst_ap=dst_ap, b=b: e.activation(out=dst_ap, in_=bkb(b).rearrange("p (c j) -> p c j", c=8), func=AF.Copy),
                         reads=[f"bk{b}"], writes=[dkey], partial=(n == 1))
                else:
                    P.op("dve", lambda e, dst_ap=dst_ap, b=b: e.tensor_copy(out=dst_ap, in_=bkb(b).rearrange("p (c j) -> p c j", c=8)),
                         reads=[f"bk{b}"], writes=[dkey], partial=(n == 1))
            for rd in range(2):
                for cl in range(4):
                    c = rd * 4 + cl
                    for hh in range(2):
                        h_ = 2 * c + hh
                        hl = h_ - rd * 8
                        ps_ = slice(64 * hh, 64 * hh + 64)
                        b1 = 2 * (h_ % 3)
                        b2 = b1 + 1
                        P.op("pe", lambda e, c=c, ps_=ps_, b1=b1: e.matmul(BK[b1][:, 0:256], lhsT=BtT[ps_, c, :], rhs=ARt_[ps_, c, :, :], start=True, stop=True),
                             reads=["BtT", kA], writes=[f"bk{b1}"], sig=False)
                        P.op("pe", lambda e, c=c, ps_=ps_, b1=b1: e.matmul(BK[b1][:, 256:384], lhsT=ARt_[ps_, c, 0, :], rhs=BtT[ps_, c, :], start=True, stop=True),
                             reads=["BtT", kA], writes=[f"bk{b1}"], partial=True)
                        P.op("pe", lambda e, c=c, ps_=ps_, b2=b2: e.matmul(BK[b2][:, 0:256], lhsT=KtT[ps_, c, :], rhs=ARt_[ps_, c, :, :], start=True, stop=True),
                             reads=["KtT", kA], writes=[f"bk{b2}"])
                        P.op("dve", lambda e, h_=h_, b1=b1: e.tensor_tensor(out=AB1_[:, h_, :], in0=BK[b1][:, 0:256], in1=mk1[:], op=ALU.mult),
                             reads=[f"bk{b1}", "mk1"], writes=[kAB], partial=True)
                        P.op("dve", lambda e, hl=hl, b1=b1: e.tensor_tensor(out=M0[:, hl, :], in0=BK[b1][:, 256:384], in1=mksl[:], op=ALU.mult),
                             reads=[f"bk{b1}", "mksl"], writes=["M0"], partial=True)
                        P.op("dve", lambda e, h_=h_, b2=b2: e.tensor_tensor(out=AK1_[:, h_, :], in0=BK[b2][:, 0:256], in1=mk1[:], op=ALU.mult),
                             reads=[f"bk{b2}", "mk1"], writes=[kAK], partial=True)
                hs_ = slice(rd * 8, rd * 8 + 8)
                P.op("pool", lambda e, hs_=hs_, rd=rd: e.tensor_copy(out=X32_[:, hs_, 0:64],
                                                                    in_=AT[:, rd * 512:(rd + 1) * 512].rearrange("p (h n) -> p h n", h=8)),
                     reads=[kin[1]], writes=[kX(0), kX(1)])
                for hl in range(8):
                    h_ = rd * 8 + hl
                    bX = 3 * (hl // 4)
                    P.op("pe", lambda e, hl=hl, h_=h_, bX=bX: e.matmul(BK[bX][:, (hl % 4) * 128 + 64:(hl % 4) * 128 + 128], lhsT=AK1_[:, h_, 0:128],
                                                                        rhs=VB[:, h_ * 64:(h_ + 1) * 64], start=True, stop=True),
                         reads=[kAK, kin[4]], writes=[f"bk{bX}"], sig=(hl % 4 == 3), partial=(hl % 4 > 0))
                for sq in range(2):
                    hsq = slice(rd * 8 + sq * 4, rd * 8 + sq * 4 + 4)
                    P.op("act", lambda e, sq=sq, hsq=hsq: e.activation(out=X32_[:, hsq, 64:128],
                                                                       in_=BK[3 * sq][:, :].rearrange("p (h n) -> p h n", h=4)[:, :, 64:128], func=AF.Copy),
                         reads=[f"bk{3 * sq}", kX(sq)], writes=[kX(sq)])
                    P.op("act", lambda e, sq=sq, hsq=hsq: e.activation(out=Xb[:, sq * 4:sq * 4 + 4, :], in_=X32_[:, hsq, :], func=AF.Copy),
                         reads=[kX(sq)], writes=[f"Xb{sq}"])
                for L in range(7):
                    pp = L % 2
                    for sq in range(2):
                        bX, bM, bMT = 3 * sq, 3 * sq + 1, 3 * sq + 2
                        hsq = slice(rd * 8 + sq * 4, rd * 8 + sq * 4 + 4)
                        mk_ = [f"MTb{pp}_{sq}", f"Mb{pp}_{sq}"]
                        for hq in range(4):
                            hl = sq * 4 + hq
                            h_ = rd * 8 + hl
                            mt = AB1_[:, h_, 0:128] if L == 0 else MTb[pp][:, hl, :]
                            P.op("pe", lambda e, hl=hl, hq=hq, mt=mt, bX=bX: e.matmul(BK[bX][:, hq * 128:(hq + 1) * 128], lhsT=mt, rhs=Xb[:, hl, :],
                                                                                      start=True, stop=True),
                                 reads=[kAB if L == 0 else mk_[0], f"Xb{sq}"], writes=[f"bk{bX}"], sig=(hq == 3), partial=(hq > 0))
                        if L < 6:
                            for hq in range(4):
                                hl = sq * 4 + hq
                                h_ = rd * 8 + hl
                                mt = AB1_[:, h_, 0:128] if L == 0 else MTb[pp][:, hl, :]
                                m = M0[:, hl, :] if L == 0 else Mb[pp][:, hl, :]
                                rk = [kAB, "M0"] if L == 0 else mk_
                                P.op("pe", lambda e, hq=hq, mt=mt, m=m, bM=bM: e.matmul(BK[bM][:, hq * 128:(hq + 1) * 128], lhsT=mt, rhs=m, start=True, stop=True),
                                     reads=rk, writes=[f"bk{bM}"], sig=(hq == 3), partial=(hq > 0))
                                P.op("pe", lambda e, hq=hq, mt=mt, m=m, bMT=bMT: e.matmul(BK[bMT][:, hq * 128:(hq + 1) * 128], lhsT=m, rhs=mt, start=True, stop=True),
                                     reads=rk, writes=[f"bk{bMT}"], sig=(hq == 3), partial=(hq > 0))
                        P.op("dve", lambda e, hsq=hsq, bX=bX: e.tensor_tensor(out=X32_[:, hsq, :], in0=BK[bX][:, :].rearrange("p (h n) -> p h n", h=4),
                                                                             in1=X32_[:, hsq, :], op=ALU.add),
                             reads=[f"bk{bX}", kX(sq)], writes=[kX(sq)])
                        if L < 6:
                            P.op("act", lambda e, sq=sq, hsq=hsq: e.activation(out=Xb[:, sq * 4:sq * 4 + 4, :], in_=X32_[:, hsq, :], func=AF.Copy),
                                 reads=[kX(sq)], writes=[f"Xb{sq}"])
                            P.op("act", lambda e, sq=sq, pp=pp, bM=bM: e.activation(out=Mb[1 - pp][:, sq * 4:sq * 4 + 4, :],
                                                                                    in_=BK[bM][:, :].rearrange("p (h n) -> p h n", h=4), func=AF.Copy),
                                 reads=[f"bk{bM}"], writes=[f"Mb{1 - pp}_{sq}"])
                            if sq == 0 and L % 2 == 0:
                                P.op("act", lambda e, sq=sq, pp=pp, bMT=bMT: e.activation(out=MTb[1 - pp][:, sq * 4:sq * 4 + 4, :],
                                                                                          in_=BK[bMT][:, :].rearrange("p (h n) -> p h n", h=4), func=AF.Copy),
                                     reads=[f"bk{bMT}"], writes=[f"MTb{1 - pp}_{sq}"])
                            else:
                                P.op("dve", lambda e, sq=sq, pp=pp, bMT=bMT: e.tensor_copy(out=MTb[1 - pp][:, sq * 4:sq * 4 + 4, :],
                                                                                           in_=BK[bMT][:, :].rearrange("p (h n) -> p h n", h=4)),
                                     reads=[f"bk{bMT}"], writes=[f"MTb{1 - pp}_{sq}"])
                P.op("pool", lambda e, hs_=hs_: e.tensor_copy(out=W1c[:, hs_, :], in_=X32_[:, hs_, 0:64]), reads=[kX(0), kX(1)], writes=["W1c"], partial=(rd > 0))
            for cc in range(8):
                P.op("pe", lambda e, cc=cc: e.transpose(out=bkb(4)[:, cc * 128:(cc + 1) * 128],
                                                        in_=W1c[:, 2 * cc:2 * cc + 2, :].rearrange("p h n -> p (h n)"), identity=C.ident_b[:]),
                     reads=["W1c", "ident_b"], writes=["bk4"], sig=(cc == 7), partial=(cc > 0))
            P.op("act", lambda e: e.activation(out=W1T_[:, :, :], in_=bkb(4).rearrange("p (c j) -> p c j", c=8), func=AF.Copy),
                 reads=["bk4"], writes=[kW])

        def tail(t):
            i = t % 2
            rows = slice(t * 128, (t + 1) * 128)
            RT, AT, BT, KT, VB = inb[i]
            kin = [f"in{n}_{i}" for n in range(5)]
            ARt_, AB1_, AK1_, X32_, W1T_ = ARt[i], AB1[i], AK1[i], X32[i], W1T[i]
            kA, kAB, kAK, kW = f"ARt{i}", f"AB1{i}", f"AK1{i}", f"W1T{i}"
            kXs = [f"X32{i}_0", f"X32{i}_1"]
            P.dma("sp", gb[:], scr["G"][rows, :], reads=[f"sG{t}"], writes=["gb"])
            P.dma("act", bon[:], scr["BON"][rows, :], reads=[f"sBON{t}"], writes=["bon"])
            P.dma("act", hres[:], h[rows, :], reads=[f"{hkey}{t}"], writes=["hres"])
            for h_ in range(NH):
                c, p0 = h_ // 2, 64 * (h_ % 2)
                P.op("pe", lambda e, h_=h_, c=c, p0=p0: e.matmul(BK[6 + h_ % 2][:, c * 64:(c + 1) * 64], lhsT=W1T_[p0:p0 + 64, c, :],
                                                                  rhs=Sb[p0:p0 + 64, c, :], start=True, stop=True),
                     reads=[kW, "Sb"], writes=[f"bk{6 + h_ % 2}"], sig=(h_ >= 14), partial=(h_ >= 2))
            for bq in range(2):
                P.op("dve", lambda e, bq=bq: e.tensor_tensor(out=Ub[:, :].rearrange("p (c two n) -> p c two n", two=2, n=64)[:, :, bq, :],
                                                             in0=BK[6 + bq][:, :].rearrange("p (h n) -> p h n", h=8),
                                                             in1=X32_[:, :, :].rearrange("p (c two) n -> p c two n", two=2)[:, :, bq, 64:128], op=ALU.add),
                     reads=[f"bk{6 + bq}"] + kXs, writes=["Ub"], partial=(bq > 0))
            for h_ in range(NH):
                c, p0 = h_ // 2, 64 * (h_ % 2)
                ob_ = BK[6 + h_ % 2][:, c * 64:(c + 1) * 64]
                P.op("pe", lambda e, c=c, p0=p0, ob_=ob_: e.matmul(ob_, lhsT=ARt_[p0:p0 + 64, c, 1, :], rhs=Sb[p0:p0 + 64, c, :], start=True, stop=False),
                     reads=[kA, "Sb"], writes=[f"bk{6 + h_ % 2}"], sig=False, partial=(h_ >= 2))
                P.op("pe", lambda e, h_=h_, ob_=ob_: e.matmul(ob_, lhsT=AB1_[:, h_, 128:256], rhs=Ub[:, h_ * 64:(h_ + 1) * 64], start=False, stop=False),
                     reads=[kAB, "Ub"], writes=[f"bk{6 + h_ % 2}"], sig=False, partial=True)
                P.op("pe", lambda e, h_=h_, ob_=ob_: e.matmul(ob_, lhsT=AK1_[:, h_, 128:256], rhs=VB[:, h_ * 64:(h_ + 1) * 64], start=False, stop=True),
                     reads=[kAK, kin[4]], writes=[f"bk{6 + h_ % 2}"], sig=(h_ >= 14), partial=True)
            for bq in range(2):
                yv = lambda ap, bq=bq: ap.rearrange("p (c two n) -> p c two n", two=2, n=64)[:, :, bq, :]
                P.op("act", lambda e, bq=bq, yv=yv: e.activation(out=yv(Ysb[:, :]), in_=BK[6 + bq][:, :].rearrange("p (c n) -> p c n", c=8), func=AF.Copy),
                     reads=[f"bk{6 + bq}"], writes=["Ysb"], partial=(bq > 0))
                P.op("act", lambda e, bq=bq, yv=yv: e.activation(out=yv(Ysq[:, :]), in_=BK[6 + bq][:, :].rearrange("p (c n) -> p c n", c=8), func=AF.Square),
                     reads=[f"bk{6 + bq}"], writes=["Ysq"], partial=(bq > 0))
            for c in range(8):
                ob_ = BK[6 + c // 4][:, (c % 4) * 128:(c % 4 + 1) * 128]
                P.op("pe", lambda e, c=c, ob_=ob_: e.matmul(ob_, lhsT=BT[:, c * 128:(c + 1) * 128], rhs=Ub[:, c * 128:(c + 1) * 128], start=True, stop=False),
                     reads=[kin[2], "Ub"], writes=[f"bk{6 + c // 4}"], sig=False, partial=(c % 4 > 0))
                P.op("pe", lambda e, c=c, ob_=ob_: e.matmul(ob_, lhsT=KT[:, c * 128:(c + 1) * 128], rhs=VB[:, c * 128:(c + 1) * 128], start=False, stop=True),
                     reads=[kin[3], kin[4]], writes=[f"bk{6 + c // 4}"], sig=(c % 4 == 3), partial=True)
            for bq in range(2):
                for hh in range(2):
                    ps_ = slice(64 * hh, 64 * hh + 64)
                    P.op("dve", lambda e, bq=bq, hh=hh, ps_=ps_: e.tensor_tensor(
                        out=S32[ps_, bq * 4:bq * 4 + 4, :], in0=BK[6 + bq][ps_, :].rearrange("p (c n) -> p c n", c=4)[:, :, hh * 64:(hh + 1) * 64],
                        in1=S32[ps_, bq * 4:bq * 4 + 4, :], op=ALU.add),
                         reads=[f"bk{6 + bq}", "S32", "Sb"], writes=["S32"], partial=True)
            P.op("dve", lambda e: e.tensor_tensor(out=S32[:], in0=S32[:], in1=gc[i][:, :].unsqueeze(2).to_broadcast([128, 8, 64]), op=ALU.mult),
                 reads=["S32", f"gc{i}"], writes=["S32"])
            P.op("pool", lambda e: e.tensor_copy(out=Sb[:], in_=S32[:]), reads=["S32"], writes=["Sb"])
            P.op("dve", lambda e: e.tensor_reduce(out=s16[:, 0, :], in_=v3(Ysb[:]), axis=AX.X, op=ALU.add), reads=["Ysb"], writes=["q0"])
            P.op("dve", lambda e: e.tensor_reduce(out=s16[:, 1, :], in_=v3(Ysq[:]), axis=AX.X, op=ALU.add), reads=["Ysq"], writes=["q1"])
            P.op("pool", lambda e: e.tensor_scalar(out=s16[:, 2, :], in0=s16[:, 0, :], scalar1=1.0 / HS, scalar2=None, op0=ALU.mult),
                 reads=["q0"], writes=["q2"])
            P.op("pool", lambda e: e.tensor_tensor(out=s16[:, 3, :], in0=s16[:, 2, :], in1=s16[:, 2, :], op=ALU.mult), reads=["q2"], writes=["q3"])
            P.op("dve", lambda e: e.scalar_tensor_tensor(out=s16[:, 4, :], in0=s16[:, 1, :], scalar=1.0 / HS, in1=s16[:, 3, :],
                                                         op0=ALU.mult, op1=ALU.subtract), reads=["q1", "q3"], writes=["q4"])
            P.op("pool", lambda e: e.tensor_scalar(out=s16[:, 4, :], in0=s16[:, 4, :], scalar1=GN_EPS, scalar2=None, op0=ALU.add),
                 reads=["q4"], writes=["q4"])
            P.op("pool", lambda e: e.tensor_tensor(out=s16[:, 5, :], in0=s16[:, 4, :], in1=C.mhalf[:, 0:1].to_broadcast([128, NH]), op=ALU.pow),
                 reads=["q4"], writes=["q5"])
            P.op("dve", lambda e: e.tensor_tensor(out=v3(Ysb[:]), in0=v3(Ysb[:]), in1=bc(s16[:, 2, :]), op=ALU.subtract),
                 reads=["Ysb", "q2"], writes=["Ysb"])
            P.op("pool", lambda e: e.tensor_tensor(out=v3(Ysb[:]), in0=v3(Ysb[:]), in1=bc(s16[:, 5, :]), op=ALU.mult),
                 reads=["Ysb", "q5"], writes=["Ysb"])
            P.op("dve", lambda e: e.tensor_tensor(out=Ysb[:], in0=Ysb[:], in1=gnw[:], op=ALU.mult), reads=["Ysb", "gnw"], writes=["Ysb"])
            P.op("pool", lambda e: e.tensor_tensor(out=Ysb[:], in0=Ysb[:], in1=gnb[:], op=ALU.add), reads=["Ysb", "gnb"], writes=["Ysb"])
            P.op("dve", lambda e: e.tensor_tensor(out=Ysb[:], in0=Ysb[:], in1=bon[:], op=ALU.add), reads=["Ysb", "bon"], writes=["Ysb"])
            P.op("pool", lambda e: e.tensor_tensor(out=otm[:], in0=Ysb[:], in1=gb[:], op=ALU.mult), reads=["Ysb", "gb"], writes=["otm"])
            for cc in range(8):
                P.op("pe", lambda e, cc=cc: e.transpose(out=bkb(6)[:, cc * 128:(cc + 1) * 128], in_=otm[:, cc * 128:(cc + 1) * 128],
                                                        identity=C.ident_b[:]),
                     reads=["otm", "ident_b"], writes=["bk6"], sig=(cc == 7), partial=(cc > 0))
            P.op("act", lambda e: e.activation(out=oT[:, :, :], in_=bkb(6).rearrange("p (c j) -> p c j", c=8), func=AF.Copy),
                 reads=["bk6"], writes=["oT"])
            for hf in range(2):
                for cc in range(8):
                    P.op("pe", lambda e, cc=cc, hf=hf: e.matmul(BK[6 + hf][:, :], lhsT=oT[:, cc, :], rhs=wo[:, cc, hf * 512:(hf + 1) * 512],
                                                                start=(cc == 0), stop=(cc == 7)),
                         reads=["oT", "wo"], writes=[f"bk{6 + hf}"], sig=(cc == 7))
                P.op("dve", lambda e, hf=hf: e.tensor_tensor(out=hres[:, hf * 512:(hf + 1) * 512], in0=BK[6 + hf][:, :],
                                                             in1=hres[:, hf * 512:(hf + 1) * 512], op=ALU.add),
                     reads=[f"bk{6 + hf}", "hres"], writes=["hres"])
            P.dma("sp", h[rows, :], hres[:], reads=["hres"], writes=[f"{hkey}{t}"])

        head(0)
        for t in range(NT):
            P.begin_capture()
            tail(t)
            s_tail = P.end_capture()
            streams = [s_tail]
            if t + 1 < NT:
                P.begin_capture()
                head(t + 1)
                streams.append(P.end_capture())
            P.replay(streams)
        P.emit()


NHM = 8
DN = 128
DR = 64
VX_W = 130
PI = float(np.pi)


def norm_tile_fm(C, P, ht_ap, hkey_r, bufs, gfull, dstT, dkey, tcols):
    junk, st4, xs, PT = bufs["junk"], bufs["st4"], bufs["xs"], bufs["PT"]
    P.op("act", lambda e: e.activation(out=junk[:], in_=ht_ap, func=AF.Square, accum_out=st4[:, 0:1]),
         reads=[hkey_r], writes=["junk", "st0"])
    P.op("pool", lambda e: e.tensor_scalar(out=st4[:, 1:2], in0=st4[:, 0:1], scalar1=1.0 / D, scalar2=RMS_EPS,
                                           op0=ALU.mult, op1=ALU.add), reads=["st0"], writes=["st1"])
    P.op("pool", lambda e: e.tensor_tensor(out=st4[:, 2:3], in0=st4[:, 1:2], in1=C.mhalf[:], op=ALU.pow),
         reads=["st1"], writes=["st2"])
    P.op("dve", lambda e: e.tensor_scalar(out=xs[:], in0=ht_ap, scalar1=st4[:, 2:3], scalar2=None, op0=ALU.mult),
         reads=[hkey_r, "st2"], writes=["xs"])
    for cc in range(8):
        P.op("pe", lambda e, cc=cc: e.transpose(out=PT[:, cc * 128:(cc + 1) * 128], in_=xs[:, cc * 128:(cc + 1) * 128],
                                                identity=C.ident_b[:]),
             reads=["xs", "ident_b"], writes=["PT"], sig=(cc == 7), partial=(cc > 0))
    P.op("dve", lambda e: e.tensor_tensor(out=dstT[:, :, tcols], in0=PT[:, :].rearrange("p (c j) -> p c j", c=8),
                                          in1=gfull[:], op=ALU.mult),
         reads=["PT", "gfull"], writes=[dkey])


def rope_tables(C, S, sb, positions, invf_ap, sgn_ap, scr=None, reload=False):
    P = C.P
    if reload:
        cosT = sb("cosT", [64, S], F32)
        sinT = sb("sinT", [64, S], F32)
        P.dma("sp", cosT[:], scr["COS"], reads=["sCOS"], writes=["cosT"])
        P.dma("act", sinT[:], scr["SIN"], reads=["sSIN"], writes=["sinT"])
        return cosT, sinT
    posi = sb("posi", [64, S], I32)
    ang = sb("ang", [64, S], F32)
    tmp = sb("rtmp", [64, S], F32)
    tmi = sb("rtmi", [64, S], I32)
    cosT = sb("cosT", [64, S], F32)
    sinT = sb("sinT", [64, S], F32)
    invf = sb("invf", [64, 1], F32)
    sgn = sb("sgn", [64, 1], F32)
    P.dma("sp", posi[:], positions.partition_broadcast(64), writes=["posi"])
    P.dma("sp", invf[:], invf_ap.rearrange("(p o) -> p o", o=1), writes=["invf"])
    P.dma("sp", sgn[:], sgn_ap.rearrange("(p o) -> p o", o=1), writes=["sgn"])
    P.op("dve", lambda e: e.tensor_copy(out=ang[:], in_=posi[:]), reads=["posi"], writes=["ang"])
    P.op("dve", lambda e: e.tensor_scalar(out=ang[:], in0=ang[:], scalar1=invf[:, 0:1], scalar2=None, op0=ALU.mult),
         reads=["ang", "invf"], writes=["ang"])

    def reduce_sin(dst, shift, dkey):
        P.op("dve", lambda e: e.tensor_scalar(out=tmp[:], in0=ang[:], scalar1=shift, scalar2=1.0 / (2 * PI), op0=ALU.add, op1=ALU.mult),
             reads=["ang"], writes=["rtmp"])
        P.op("dve", lambda e: e.tensor_copy(out=tmi[:], in_=tmp[:]), reads=["rtmp"], writes=["rtmi"])
        P.op("dve", lambda e: e.tensor_copy(out=tmp[:], in_=tmi[:]), reads=["rtmi"], writes=["rtmp"])
        P.op("dve", lambda e: e.tensor_scalar(out=tmp[:], in0=tmp[:], scalar1=-2 * PI, scalar2=shift, op0=ALU.mult, op1=ALU.add),
             reads=["rtmp"], writes=["rtmp"])
        P.op("dve", lambda e: e.tensor_tensor(out=dst[:], in0=tmp[:], in1=ang[:], op=ALU.add), reads=["rtmp", "ang"], writes=[dkey])
        P.op("dve", lambda e: e.tensor_scalar(out=tmp[:], in0=dst[:], scalar1=PI, scalar2=-2 * PI, op0=ALU.is_gt, op1=ALU.mult),
             reads=[dkey], writes=["rtmp"])
        P.op("dve", lambda e: e.tensor_tensor(out=dst[:], in0=dst[:], in1=tmp[:], op=ALU.add), reads=[dkey, "rtmp"], writes=[dkey])
        P.op("dve", lambda e: e.tensor_scalar(out=tmp[:], in0=dst[:], scalar1=-PI, scalar2=2 * PI, op0=ALU.is_lt, op1=ALU.mult),
             reads=[dkey], writes=["rtmp"])
        P.op("dve", lambda e: e.tensor_tensor(out=dst[:], in0=dst[:], in1=tmp[:], op=ALU.add), reads=[dkey, "rtmp"], writes=[dkey])
        P.op("act", lambda e: e.activation(out=dst[:], in_=dst[:], func=AF.Sin), reads=[dkey], writes=[dkey])

    reduce_sin(sinT, 0.0, "sinT")
    reduce_sin(cosT, PI / 2, "cosT")
    P.op("dve", lambda e: e.tensor_scalar(out=sinT[:], in0=sinT[:], scalar1=sgn[:, 0:1], scalar2=None, op0=ALU.mult),
         reads=["sinT", "sgn"], writes=["sinT"])
    if scr is not None:
        P.dma("sp", scr["COS"], cosT[:], reads=["cosT"], writes=["sCOS"])
        P.dma("act", scr["SIN"], sinT[:], reads=["sinT"], writes=["sSIN"])
    return cosT, sinT


def mla_kv_phase(C, S, h, hkey, W, scr):
    nc, P = C.nc, C.P
    NT = S // 128
    with contextlib.ExitStack() as st:
        sb, ps = alloc(st, nc)
        cosT, sinT = rope_tables(C, S, sb, W["positions"], W["rope_invf"], W["rope_sgn"], scr=scr)
        wdkv = sb("wdkv", [128, 8, 320], BF16)
        wdsw = sb("wdsw", [128, 8, 64], BF16)
        wukv = sb("wukv", [128, 2, 2048], BF16)
        gcol = sb("gcol", [128, 8], F32)
        gfull = sb("gfull", [128, 8, 128], F32)
        glat = sb("glat", [128, 2], F32)
        ht = [sb(f"ht{i}", [128, D], F32) for i in range(2)]
        bufs = {"junk": sb("junk", [128, D], BF16), "st4": sb("st4", [128, 8], F32), "xs": sb("xs", [128, D], BF16),
                "PT": ps("PT", [128, D], BF16)}
        hT = sb("hT", [128, 8, 128], BF16)
        cs = sb("cs", [128, 256], BF16)
        cT = sb("cT", [128, 2, 128], BF16)
        knT = [sb(f"knT{i}", [128, 8, 128], BF16) for i in range(2)]
        vx = [sb(f"vx{i}", [128, 8, VX_W], BF16) for i in range(2)]
        kr = [sb(f"kr{i}", [64, 128], BF16) for i in range(2)]
        t1 = sb("t1", [64, 128], F32)
        t2 = sb("t2", [64, 128], F32)
        PC = ps("PC", [128, 512], F32)
        PK = [ps(f"PK{i}", [128, 512], F32) for i in range(2)]
        PV = [ps(f"PV{i}", [128, 512], F32) for i in range(2)]
        PR = ps("PR", [128, 512], F32)

        P.dma("pool", wdkv[:], W["mla_w_dkv"].rearrange("(c p) n -> p c n", p=128), writes=["wdkv"])
        P.dma("pool", wdsw[:, :, 0:32], W["mla_w_dkv"][:, 288:320].rearrange("(c p) n -> p c n", p=128), writes=["wdsw"])
        P.dma("pool", wdsw[:, :, 32:64], W["mla_w_dkv"][:, 256:288].rearrange("(c p) n -> p c n", p=128), writes=["wdsw"], partial=True)
        for q in range(2):
            P.dma("pool", wukv[:, :, q * 1024:(q + 1) * 1024], W["mla_w_ukv"][:, q * 1024:(q + 1) * 1024].rearrange("(c p) n -> p c n", p=128),
                  writes=["wukv"], partial=(q > 0))
        load_col(C, gcol[:, :], W["kv_norm_g"], "gcol", 8)
        load_col(C, glat[:, :], W["mla_kv_latent_g"], "glat", 2)
        P.op("dve", lambda e: e.tensor_copy(out=gfull[:], in_=gcol[:, :].unsqueeze(2).to_broadcast([128, 8, 128])),
             reads=["gcol"], writes=["gfull"])
        for i in range(2):
            P.op("pool", lambda e, i=i: e.memset(vx[i][:], 1.0), writes=[f"vx{i}"])
        wuv = wukv[:, :, :].rearrange("p c (h x) -> p c h x", h=NHM)

        def hload(t):
            P.dma("sp", ht[t % 2][:], h[t * 128:(t + 1) * 128, :], reads=[f"{hkey}{t}"], writes=[f"ht{t % 2}"])

        hload(0)

        def do_tile(t):
            i = t % 2
            rows = slice(t * 128, (t + 1) * 128)
            tc_ = slice(t * 128, (t + 1) * 128)
            if t + 1 < NT:
                hload(t + 1)
            norm_tile_fm(C, P, ht[i][:], f"ht{i}", bufs, gfull, hT, "hT", slice(0, 128))
            for cc in range(8):
                P.op("pe", lambda e, cc=cc: e.matmul(PC[:, 0:320], lhsT=hT[:, cc, :], rhs=wdkv[:, cc, :], start=(cc == 0), stop=(cc == 7)),
                     reads=["hT", "wdkv"], writes=["PC"], sig=(cc == 7))
            st4 = bufs["st4"]
            P.op("act", lambda e: e.activation(out=bufs["junk"][:, 0:256], in_=PC[:, 0:256], func=AF.Square, accum_out=st4[:, 4:5]),
                 reads=["PC"], writes=["junk", "st4"])
            P.op("pool", lambda e: e.tensor_scalar(out=st4[:, 5:6], in0=st4[:, 4:5], scalar1=1.0 / 256, scalar2=RMS_EPS, op0=ALU.mult, op1=ALU.add),
                 reads=["st4"], writes=["st5"])
            P.op("pool", lambda e: e.tensor_tensor(out=st4[:, 6:7], in0=st4[:, 5:6], in1=C.mhalf[:], op=ALU.pow), reads=["st5"], writes=["st6"])
            P.op("dve", lambda e: e.tensor_scalar(out=cs[:], in0=PC[:, 0:256], scalar1=st4[:, 6:7], scalar2=None, op0=ALU.mult),
                 reads=["PC", "st6"], writes=["cs"])
            PT = bufs["PT"]
            for cc in range(2):
                P.op("pe", lambda e, cc=cc: e.transpose(out=PT[:, cc * 128:(cc + 1) * 128], in_=cs[:, cc * 128:(cc + 1) * 128], identity=C.ident_b[:]),
                     reads=["cs", "ident_b"], writes=["PT"], sig=(cc == 1), partial=(cc > 0))
            for cc in range(2):
                P.op("act", lambda e, cc=cc: e.activation(out=cT[:, cc, :], in_=PT[:, cc * 128:(cc + 1) * 128], func=AF.Copy, scale=glat[:, cc:cc + 1]),
                     reads=["PT", "glat"], writes=["cT"], partial=(cc > 0))
            for hh in range(NHM):
                for cc in range(2):
                    P.op("pe", lambda e, hh=hh, cc=cc: e.matmul(PK[hh // 4][:, (hh % 4) * 128:(hh % 4 + 1) * 128], lhsT=wuv[:, cc, hh, 0:128],
                                                                rhs=cT[:, cc, :], start=(cc == 0), stop=(cc == 1)),
                         reads=["wukv", "cT"], writes=[f"PK{hh // 4}"], sig=(cc == 1 and hh % 4 == 3), partial=not (hh % 4 == 0 and cc == 0))
            for q in range(2):
                P.op("act" if q == 0 else "dve",
                     (lambda e, q=q: e.activation(out=knT[i][:, q * 4:q * 4 + 4, :], in_=PK[q][:, :].rearrange("p (h n) -> p h n", h=4), func=AF.Copy)) if q == 0 else
                     (lambda e, q=q: e.tensor_copy(out=knT[i][:, q * 4:q * 4 + 4, :], in_=PK[q][:, :].rearrange("p (h n) -> p h n", h=4))),
                     reads=[f"PK{q}"], writes=[f"knT{i}"], partial=(q > 0))
            P.dma("sp", scr["KN"][:, :, tc_].rearrange("h d t -> d h t"), knT[i][:], reads=[f"knT{i}"], writes=[f"sKN{t}"])
            for q in range(2):
                for cc in range(2):
                    P.op("pe", lambda e, q=q, cc=cc: e.matmul(PV[q][:, :], lhsT=cT[:, cc, :], rhs=wuv[:, cc, q * 4:q * 4 + 4, 128:256],
                                                              start=(cc == 0), stop=(cc == 1)),
                         reads=["wukv", "cT"], writes=[f"PV{q}"], sig=(cc == 1))
                P.op("act" if q == 0 else "dve",
                     (lambda e, q=q: e.activation(out=vx[i][:, q * 4:q * 4 + 4, 0:128], in_=PV[q][:, :].rearrange("p (h n) -> p h n", h=4), func=AF.Copy)) if q == 0 else
                     (lambda e, q=q: e.tensor_copy(out=vx[i][:, q * 4:q * 4 + 4, 0:128], in_=PV[q][:, :].rearrange("p (h n) -> p h n", h=4))),
                     reads=[f"PV{q}"], writes=[f"vx{i}"], partial=True)
            P.dma("sp", scr["VX"][rows, :, :], vx[i][:], reads=[f"vx{i}"], writes=[f"sVX{t}"])
            for cc in range(8):
                P.op("pe", lambda e, cc=cc: e.matmul(PR[0:64, 0:128], lhsT=wdkv[:, cc, 256:320], rhs=hT[:, cc, :], start=(cc == 0), stop=(cc == 7)),
                     reads=["hT", "wdkv"], writes=["PR"], sig=False)
            for cc in range(8):
                P.op("pe", lambda e, cc=cc: e.matmul(PR[0:64, 128:256], lhsT=wdsw[:, cc, :], rhs=hT[:, cc, :], start=(cc == 0), stop=(cc == 7)),
                     reads=["hT", "wdsw"], writes=["PR"], sig=(cc == 7), partial=True)
            P.op("dve", lambda e: e.tensor_tensor(out=t1[:], in0=PR[0:64, 0:128], in1=cosT[:, tc_], op=ALU.mult), reads=["PR", "cosT"], writes=["t1"])
            P.op("dve", lambda e: e.tensor_tensor(out=t2[:], in0=PR[0:64, 128:256], in1=sinT[:, tc_], op=ALU.mult), reads=["PR", "sinT"], writes=["t2"])
            P.op("pool", lambda e: e.tensor_tensor(out=kr[i][:], in0=t1[:], in1=t2[:], op=ALU.add), reads=["t1", "t2"], writes=[f"kr{i}"])
            P.dma("sp", scr["KR"][:, tc_], kr[i][:], reads=[f"kr{i}"], writes=[f"sKR{t}"])

        for t in range(NT):
            do_tile(t)
        P.emit()


def mla_q_phase(C, S, h, hkey, W, scr):
    nc, P = C.nc, C.P
    NT = S // 128
    with contextlib.ExitStack() as st:
        sb, ps = alloc(st, nc)
        cosT, sinT = rope_tables(C, S, sb, W["positions"], W["rope_invf"], W["rope_sgn"], scr=scr, reload=True)
        wdq = sb("wdq", [128, 8, 512], BF16)
        wuq = sb("wuq", [128, 4, 1536], BF16)
        wusw = sb("wusw", [128, 4, NHM, 64], BF16)
        gcol = sb("gcol", [128, 8], F32)
        gfull = sb("gfull", [128, 8, 128], F32)
        glat = sb("glat", [128, 4], F32)
        ht = [sb(f"ht{i}", [128, D], F32) for i in range(2)]
        bufs = {"junk": sb("junk", [128, D], BF16), "st4": sb("st4", [128, 8], F32), "xs": sb("xs", [128, D], BF16),
                "PT": ps("PT", [128, D], BF16)}
        hT = sb("hT", [128, 8, 128], BF16)
        qs = sb("qs", [128, 512], BF16)
        qlT = sb("qlT", [128, 4, 128], BF16)
        qnT = [sb(f"qnT{i}", [128, 8, 128], BF16) for i in range(2)]
        qr = [sb(f"qr{i}", [64, 8, 128], BF16) for i in range(2)]
        t1 = sb("t1", [64, 8, 128], F32)
        t2 = sb("t2", [64, 8, 128], F32)
        PC = ps("PC", [128, 512], F32)
        PK = [ps(f"PK{i}", [128, 512], F32) for i in range(2)]
        PR = [ps(f"PR{i}", [128, 1024], F32) for i in range(2)]

        P.dma("pool", wdq[:], W["mla_w_dq"].rearrange("(c p) n -> p c n", p=128), writes=["wdq"])
        P.dma("pool", wuq[:], W["mla_w_uq"].rearrange("(c p) n -> p c n", p=128), writes=["wuq"])
        wq4 = W["mla_w_uq"].rearrange("(c p) (h x) -> p c h x", p=128, h=NHM)
        for cc in range(4):
            P.dma("pool", wusw[:, cc, :, 0:32], wq4[:, cc, :, 160:192], writes=["wusw"], partial=True)
            P.dma("pool", wusw[:, cc, :, 32:64], wq4[:, cc, :, 128:160], writes=["wusw"], partial=True)
        load_col(C, gcol[:, :], W["norm_g"], "gcol", 8)
        load_col(C, glat[:, :], W["mla_q_latent_g"], "glat", 4)
        P.op("dve", lambda e: e.tensor_copy(out=gfull[:], in_=gcol[:, :].unsqueeze(2).to_broadcast([128, 8, 128])),
             reads=["gcol"], writes=["gfull"])
        wu4 = wuq[:, :, :].rearrange("p c (h x) -> p c h x", h=NHM)

        def hload(t):
            P.dma("sp", ht[t % 2][:], h[t * 128:(t + 1) * 128, :], reads=[f"{hkey}{t}"], writes=[f"ht{t % 2}"])

        hload(0)

        def do_tile(t):
            i = t % 2
            rows = slice(t * 128, (t + 1) * 128)
            tc_ = slice(t * 128, (t + 1) * 128)
            if t + 1 < NT:
                hload(t + 1)
            norm_tile_fm(C, P, ht[i][:], f"ht{i}", bufs, gfull, hT, "hT", slice(0, 128))
            for cc in range(8):
                P.op("pe", lambda e, cc=cc: e.matmul(PC[:, :], lhsT=hT[:, cc, :], rhs=wdq[:, cc, :], start=(cc == 0), stop=(cc == 7)),
                     reads=["hT", "wdq"], writes=["PC"], sig=(cc == 7))
            st4 = bufs["st4"]
            P.op("act", lambda e: e.activation(out=bufs["junk"][:, 0:512], in_=PC[:, :], func=AF.Square, accum_out=st4[:, 4:5]),
                 reads=["PC"], writes=["junk", "st4"])
            P.op("pool", lambda e: e.tensor_scalar(out=st4[:, 5:6], in0=st4[:, 4:5], scalar1=1.0 / 512, scalar2=RMS_EPS, op0=ALU.mult, op1=ALU.add),
                 reads=["st4"], writes=["st5"])
            P.op("pool", lambda e: e.tensor_tensor(out=st4[:, 6:7], in0=st4[:, 5:6], in1=C.mhalf[:], op=ALU.pow), reads=["st5"], writes=["st6"])
            P.op("dve", lambda e: e.tensor_scalar(out=qs[:], in0=PC[:, :], scalar1=st4[:, 6:7], scalar2=None, op0=ALU.mult),
                 reads=["PC", "st6"], writes=["qs"])
            PT = bufs["PT"]
            for cc in range(4):
                P.op("pe", lambda e, cc=cc: e.transpose(out=PT[:, cc * 128:(cc + 1) * 128], in_=qs[:, cc * 128:(cc + 1) * 128], identity=C.ident_b[:]),
                     reads=["qs", "ident_b"], writes=["PT"], sig=(cc == 3), partial=(cc > 0))
            for cc in range(4):
                P.op("act", lambda e, cc=cc: e.activation(out=qlT[:, cc, :], in_=PT[:, cc * 128:(cc + 1) * 128], func=AF.Copy, scale=glat[:, cc:cc + 1]),
                     reads=["PT", "glat"], writes=["qlT"], partial=(cc > 0))
            for hh in range(NHM):
                for cc in range(4):
                    P.op("pe", lambda e, hh=hh, cc=cc: e.matmul(PK[hh // 4][:, (hh % 4) * 128:(hh % 4 + 1) * 128], lhsT=wu4[:, cc, hh, 0:128],
                                                                rhs=qlT[:, cc, :], start=(cc == 0), stop=(cc == 3)),
                         reads=["wuq", "qlT"], writes=[f"PK{hh // 4}"], sig=(cc == 3 and hh % 4 == 3), partial=not (hh % 4 == 0 and cc == 0))
            for q in range(2):
                P.op("act" if q == 0 else "dve",
                     (lambda e, q=q: e.activation(out=qnT[i][:, q * 4:q * 4 + 4, :], in_=PK[q][:, :].rearrange("p (h n) -> p h n", h=4), func=AF.Copy)) if q == 0 else
                     (lambda e, q=q: e.tensor_copy(out=qnT[i][:, q * 4:q * 4 + 4, :], in_=PK[q][:, :].rearrange("p (h n) -> p h n", h=4))),
                     reads=[f"PK{q}"], writes=[f"qnT{i}"], partial=(q > 0))
            P.dma("sp", scr["QN"][:, :, tc_].rearrange("h d t -> d h t"), qnT[i][:], reads=[f"qnT{i}"], writes=[f"sQN{t}"])
            for hh in range(NHM):
                for cc in range(4):
                    P.op("pe", lambda e, hh=hh, cc=cc: e.matmul(PR[0][0:64, hh * 128:(hh + 1) * 128], lhsT=wu4[:, cc, hh, 128:192], rhs=qlT[:, cc, :],
                                                                start=(cc == 0), stop=(cc == 3)),
                         reads=["wuq", "qlT"], writes=["PR0"], sig=False, partial=True)
            for hh in range(NHM):
                for cc in range(4):
                    P.op("pe", lambda e, hh=hh, cc=cc: e.matmul(PR[1][0:64, hh * 128:(hh + 1) * 128], lhsT=wusw[:, cc, hh, :], rhs=qlT[:, cc, :],
                                                                start=(cc == 0), stop=(cc == 3)),
                         reads=["wusw", "qlT"], writes=["PR1", "PR0"], sig=(hh == 7 and cc == 3), partial=True)
            cb = cosT[:, tc_].unsqueeze(1).to_broadcast([64, NHM, 128])
            sbb = sinT[:, tc_].unsqueeze(1).to_broadcast([64, NHM, 128])
            P.op("dve", lambda e: e.tensor_tensor(out=t1[:], in0=PR[0][0:64, :].rearrange("p (h n) -> p h n", h=NHM), in1=cb, op=ALU.mult),
                 reads=["PR0", "cosT"], writes=["t1"])
            P.op("dve", lambda e: e.tensor_tensor(out=t2[:], in0=PR[1][0:64, :].rearrange("p (h n) -> p h n", h=NHM), in1=sbb, op=ALU.mult),
                 reads=["PR1", "sinT"], writes=["t2"])
            P.op("pool", lambda e: e.tensor_tensor(out=qr[i][:], in0=t1[:], in1=t2[:], op=ALU.add), reads=["t1", "t2"], writes=[f"qr{i}"])
            P.dma("sp", scr["QR"][:, :, tc_].rearrange("h d t -> d h t"), qr[i][:], reads=[f"qr{i}"], writes=[f"sQR{t}"])

        for t in range(NT):
            do_tile(t)
        P.emit()


def mla_attn_phase(C, S, scr):
    nc, P = C.nc, C.P
    NT = S // 128
    NQC = S // 512
    scale = float((DN + DR) ** -0.5)
    with contextlib.ExitStack() as st:
        sb, ps = alloc(st, nc)
        krT = sb("krT", [128, S], BF16)
        knT = [sb(f"knT{i}", [128, S], BF16) for i in range(2)]
        qnT = [sb(f"qnT{i}", [128, S], BF16) for i in range(2)]
        qrT = [sb(f"qrT{i}", [128, S], BF16) for i in range(2)]
        vx = [sb(f"vx{i}", [128, NT, VX_W], BF16) for i in range(2)]
        pt = [sb(f"pt{i}", [128, 512], BF16) for i in range(4)]
        rs = sb("rs", [128, 4], F32)
        ao = [sb(f"ao{i}", [128, 128], BF16) for i in range(4)]
        PS = [ps(f"PS{i}", [128, 512], F32) for i in range(2)]
        PO = [ps(f"PO{i}", [128, 512], F32) for i in range(4)]
        allk = lambda n: [f"s{n}{t}" for t in range(NT)]
        P.dma("sp", krT[0:64, :], scr["KR"][:, :], reads=allk("KR"), writes=["krT"])
        P.dma("act", krT[64:128, :], scr["KR"][:, :], reads=allk("KR"), writes=["krT"], partial=True)
        cnt = {"s": 0, "p": 0, "a": 0}

        def load_head(hh):
            b = hh % 2
            P.dma("sp", knT[b][:], scr["KN"][hh], reads=allk("KN"), writes=[f"knT{b}"])
            P.dma("act", qnT[b][:], scr["QN"][hh], reads=allk("QN"), writes=[f"qnT{b}"])
            P.dma("sp", qrT[b][0:64, :], scr["QR"][hh], reads=allk("QR"), writes=[f"qrT{b}"])
            P.dma("sp", qrT[b][64:128, :], scr["QR"][hh], reads=allk("QR"), writes=[f"qrT{b}"], partial=True)
            P.dma("act", vx[b][:], scr["VX"][:, hh, :].rearrange("(t p) x -> p t x", p=128), reads=allk("VX"), writes=[f"vx{b}"])

        load_head(0)
        for hh in range(NHM):
            b = hh % 2
            if hh + 1 < NHM:
                load_head(hh + 1)
            for qc in range(NQC):
                nkt = 4 * qc + 4

                def qk(kt, qc=qc, b=b):
                    q0 = max(qc * 512, kt * 128)
                    q1 = (qc + 1) * 512
                    n = q1 - q0
                    s_ = cnt["s"] % 2
                    cnt["s"] += 1
                    p_ = cnt["p"] % 4
                    cnt["p"] += 1
                    ks = slice(kt * 128, (kt + 1) * 128)
                    P.op("pe", lambda e: e.matmul(PS[s_][:, 0:n], lhsT=knT[b][:, ks], rhs=qnT[b][:, q0:q1], start=True, stop=False),
                         reads=[f"knT{b}", f"qnT{b}"], writes=[f"PS{s_}"], sig=False)
                    return (kt, p_, q0, q1, n, s_, ks)

                def qk_rope(st_, b=b):
                    kt, p_, q0, q1, n, s_, ks = st_
                    rp = slice(64 * (kt % 2), 64 * (kt % 2) + 64)
                    P.op("pe", lambda e: e.matmul(PS[s_][:, 0:n], lhsT=krT[rp, ks], rhs=qrT[b][rp, q0:q1], start=False, stop=True),
                         reads=["krT", f"qrT{b}"], writes=[f"PS{s_}"], partial=True)

                def qk_exp(st_, qc=qc):
                    kt, p_, q0, q1, n, s_, ks = st_
                    P.op("act", lambda e: e.activation(out=pt[p_][:, 0:n], in_=PS[s_][:, 0:n], func=AF.Exp, scale=scale),
                         reads=[f"PS{s_}"], writes=[f"pt{p_}"])
                    if kt >= 4 * qc:
                        P.op("pool", lambda e: e.affine_select(out=pt[p_][:, 0:128], in_=pt[p_][:, 0:128], pattern=[[1, 128]],
                                                               compare_op=ALU.is_ge, fill=0.0, base=0, channel_multiplier=-1),
                             reads=[f"pt{p_}"], writes=[f"pt{p_}"])
                    return (kt, p_, q0)

                def pv(st_, qc=qc, b=b):
                    kt, p_, q0 = st_
                    for j in range(4):
                        qt = qc * 4 + j
                        if qt < kt:
                            continue
                        c0 = qt * 128 - q0
                        P.op("pe", lambda e, j=j, c0=c0, qt=qt: e.matmul(PO[j][:, 0:VX_W], lhsT=pt[p_][:, c0:c0 + 128], rhs=vx[b][:, kt, :],
                                                                        start=(kt == 0), stop=(kt == qt)),
                             reads=[f"pt{p_}", f"vx{b}"], writes=[f"PO{j}"], sig=(kt == qt))

                pend = []
                for k2 in range(0, nkt, 2):
                    a_ = qk(k2)
                    b_ = qk(k2 + 1)
                    qk_rope(a_)
                    qk_rope(b_)
                    sa_ = qk_exp(a_)
                    sb_ = qk_exp(b_)
                    for pr in pend:
                        pv(pr)
                    pend = [sa_, sb_]
                for pr in pend:
                    pv(pr)
                for j in range(4):
                    qt = qc * 4 + j
                    a_ = cnt["a"] % 4
                    cnt["a"] += 1
                    P.op("dve", lambda e, j=j: e.reciprocal(out=rs[:, j:j + 1], in_=PO[j][:, 128:129]), reads=[f"PO{j}"], writes=[f"rs{j}"])
                    P.op("dve", lambda e, j=j, a_=a_: e.tensor_scalar(out=ao[a_][:], in0=PO[j][:, 0:128], scalar1=rs[:, j:j + 1], scalar2=None, op0=ALU.mult),
                         reads=[f"PO{j}", f"rs{j}"], writes=[f"ao{a_}"])
                    P.dma("sp", scr["AO"][qt * 128:(qt + 1) * 128, hh * 128:(hh + 1) * 128], ao[a_][:], reads=[f"ao{a_}"], writes=[f"sAO{qt}"], partial=True)
        P.emit()


def mla_out_phase(C, S, h, hkey, W, scr):
    nc, P = C.nc, C.P
    NT = S // 128
    with contextlib.ExitStack() as st:
        sb, ps = alloc(st, nc)
        wo = sb("wo", [128, 8, D], BF16)
        aot = [sb(f"aot{i}", [128, D], BF16) for i in range(2)]
        hres = [sb(f"hres{i}", [128, D], F32) for i in range(2)]
        hout = [sb(f"hout{i}", [128, D], F32) for i in range(2)]
        oT = sb("oT", [128, 8, 128], BF16)
        PT = ps("PT", [128, D], BF16)
        PO = [ps(f"PO{i}", [128, 512], F32) for i in range(2)]
        for q in range(2):
            P.dma("pool", wo[:, :, q * 512:(q + 1) * 512], W["mla_w_o"][:, q * 512:(q + 1) * 512].rearrange("(c p) n -> p c n", p=128),
                  writes=["wo"], partial=(q > 0))

        def oload(t):
            rows = slice(t * 128, (t + 1) * 128)
            P.dma("sp", aot[t % 2][:], scr["AO"][rows, :], reads=[f"sAO{t}"], writes=[f"aot{t % 2}"])
            P.dma("sp", hres[t % 2][:], h[rows, :], reads=[f"{hkey}{t}"], writes=[f"hres{t % 2}"])

        oload(0)

        def do_tile(t):
            i = t % 2
            rows = slice(t * 128, (t + 1) * 128)
            if t + 1 < NT:
                oload(t + 1)
            for cc in range(8):
                P.op("pe", lambda e, cc=cc: e.transpose(out=PT[:, cc * 128:(cc + 1) * 128], in_=aot[i][:, cc * 128:(cc + 1) * 128], identity=C.ident_b[:]),
                     reads=[f"aot{i}", "ident_b"], writes=["PT"], sig=(cc == 7), partial=(cc > 0))
            P.op("act", lambda e: e.activation(out=oT[:, :, :], in_=PT[:, :].rearrange("p (c j) -> p c j", c=8), func=AF.Copy),
                 reads=["PT"], writes=["oT"])
            for hf in range(2):
                for cc in range(8):
                    P.op("pe", lambda e, cc=cc, hf=hf: e.matmul(PO[hf][:, :], lhsT=oT[:, cc, :], rhs=wo[:, cc, hf * 512:(hf + 1) * 512],
                                                                start=(cc == 0), stop=(cc == 7)),
                         reads=["oT", "wo"], writes=[f"PO{hf}"], sig=(cc == 7))
                P.op("dve", lambda e, hf=hf: e.tensor_tensor(out=hout[i][:, hf * 512:(hf + 1) * 512], in0=PO[hf][:, :],
                                                             in1=hres[i][:, hf * 512:(hf + 1) * 512], op=ALU.add),
                     reads=[f"PO{hf}", f"hres{i}"], writes=[f"hout{i}"], partial=(hf > 0))
            P.dma("sp", h[rows, :], hout[i][:], reads=[f"hout{i}"], writes=[f"{hkey}{t}"])

        for t in range(NT):
            do_tile(t)
        P.emit()


PARAM_SHAPES = {
    "norm_g": [2, 3, D], "ffn_w_gate": [2, 2, D, DFF], "ffn_w_up": [2, 2, D, DFF], "ffn_w_down": [2, 2, DFF, D],
    "rwkv_mix": [1, 6, D], "rwkv_w_r": [1, D, D], "rwkv_w_k": [1, D, D], "rwkv_w_v": [1, D, D], "rwkv_w_o": [1, D, D],
    "rwkv_w0": [1, D], "rwkv_w1": [1, D, 64], "rwkv_w2": [1, 64, D], "rwkv_a0": [1, D], "rwkv_a1": [1, D, 64],
    "rwkv_a2": [1, 64, D], "rwkv_g1": [1, D, 128], "rwkv_g2": [1, 128, D], "rwkv_k_k": [1, D], "rwkv_k_a": [1, D],
    "rwkv_r_k": [1, 16, 64], "rwkv_gn_w": [1, D], "rwkv_gn_b": [1, D], "kv_norm_g": [D], "mla_w_dkv": [D, 320],
    "mla_kv_latent_g": [256], "mla_w_ukv": [256, 2048], "mla_w_dq": [1, D, 512], "mla_q_latent_g": [1, 512],
    "mla_w_uq": [1, 512, 1536], "mla_w_o": [1, D, D], "final_norm_g": [D],
}


def rope_consts():
    invf = (np.float32(10000.0) ** (-(np.arange(0, DR, 2, dtype=np.float32)) / np.float32(DR))).astype(np.float32)
    invf2 = np.concatenate([invf, invf]).astype(np.float32)
    sgn = np.concatenate([-np.ones(32, np.float32), np.ones(32, np.float32)])
    return invf2, sgn


def build_program(S):
    nc = bass.Bass("TRN2", target_bir_lowering=False)
    NT = S // 128
    x = nc.dram_tensor("x", [S, D], F32, kind="ExternalInput").ap()
    pos = nc.dram_tensor("positions", [S], I32, kind="ExternalInput").ap()
    A = {n: nc.dram_tensor(n, shp, F32, kind="ExternalInput").ap() for n, shp in PARAM_SHAPES.items()}
    invf = nc.dram_tensor("rope_invf", [DR], F32, kind="ExternalInput").ap()
    sgn = nc.dram_tensor("rope_sgn", [DR], F32, kind="ExternalInput").ap()
    out = nc.dram_tensor("out", [S, D], F32, kind="ExternalOutput").ap()
    h = nc.dram_tensor("h_scr", [S, D], F32, kind="Internal").ap()
    scr = {}
    for n in ("RT", "AT", "BT", "KT", "VB", "AO"):
        scr[n] = nc.dram_tensor("scr_" + n, [S, D], BF16, kind="Internal").ap()
    for n in ("G", "BON"):
        scr[n] = nc.dram_tensor("scr_" + n, [S, D], F32, kind="Internal").ap()
    scr["GC"] = nc.dram_tensor("scr_GC", [NT, D], F32, kind="Internal").ap()
    scr["KN"] = nc.dram_tensor("scr_KN", [NHM, DN, S], BF16, kind="Internal").ap()
    scr["QN"] = nc.dram_tensor("scr_QN", [NHM, DN, S], BF16, kind="Internal").ap()
    scr["QR"] = nc.dram_tensor("scr_QR", [NHM, DR, S], BF16, kind="Internal").ap()
    scr["KR"] = nc.dram_tensor("scr_KR", [DR, S], BF16, kind="Internal").ap()
    scr["VX"] = nc.dram_tensor("scr_VX", [S, NHM, VX_W], BF16, kind="Internal").ap()
    scr["COS"] = nc.dram_tensor("scr_COS", [DR, S], F32, kind="Internal").ap()
    scr["SIN"] = nc.dram_tensor("scr_SIN", [DR, S], F32, kind="Internal").ap()
    with contextlib.ExitStack() as st:
        C = Ctx()
        C.nc = nc
        C.stack = st
        C.P = Prog(nc, st)
        setup_consts(C)
        ffn = lambda src, dst, sk, l, j: ffn_phase(C, S, src, dst, sk, "h", A["norm_g"][l, 2 * j], A["ffn_w_gate"][l, j],
                                                   A["ffn_w_up"][l, j], A["ffn_w_down"][l, j])
        ffn(x, h, "x", 0, 0)
        Wr = {n: A[n][0] for n in PARAM_SHAPES if n.startswith("rwkv")}
        Wr["rwkv_r_k"] = A["rwkv_r_k"][0].rearrange("h n -> (h n)")
        Wr["norm_g"] = A["norm_g"][0, 1]
        rwkv_pass_a(C, S, h, "h", Wr, scr)
        rwkv_pass_b(C, S, h, "h", Wr, scr)
        ffn(h, h, "h", 0, 1)
        Wm = {"positions": pos, "rope_invf": invf, "rope_sgn": sgn, "kv_norm_g": A["kv_norm_g"], "mla_w_dkv": A["mla_w_dkv"],
              "mla_kv_latent_g": A["mla_kv_latent_g"], "mla_w_ukv": A["mla_w_ukv"], "mla_w_dq": A["mla_w_dq"][0],
              "mla_q_latent_g": A["mla_q_latent_g"][0], "mla_w_uq": A["mla_w_uq"][0], "mla_w_o": A["mla_w_o"][0],
              "norm_g": A["norm_g"][1, 1]}
        mla_kv_phase(C, S, h, "h", Wm, scr)
        ffn(h, h, "h", 1, 0)
        mla_q_phase(C, S, h, "h", Wm, scr)
        mla_attn_phase(C, S, scr)
        mla_out_phase(C, S, h, "h", Wm, scr)
        ffn(h, h, "h", 1, 1)
        final_norm_phase(C, S, h, out, "h", "o", A["final_norm_g"])
    return nc


def kernel(**inputs):
    x = np.ascontiguousarray(np.asarray(inputs["x"], dtype=np.float32))
    B, S, _ = x.shape
    positions = np.ascontiguousarray(np.asarray(inputs["positions"]).astype(np.int32))
    params = {n: np.ascontiguousarray(np.asarray(inputs[n], dtype=np.float32)) for n in PARAM_SHAPES}
    invf2, sgn = rope_consts()
    nc = build_program(S)
    in_maps = []
    for b in range(B):
        m = dict(params)
        m["x"] = x[b]
        m["positions"] = positions[b]
        m["rope_invf"] = invf2
        m["rope_sgn"] = sgn
        in_maps.append(m)
    res = run_bass_kernel_spmd(nc, in_maps, core_ids=list(range(B)))
    return np.stack([np.asarray(r["out"]) for r in res.results], axis=0).astype(np.float32)
```

```python
import contextlib
import re
import numpy as np
import concourse.bass as bass
import concourse.mybir as mybir
from concourse.bass_utils import run_bass_kernel_spmd

F32 = mybir.dt.float32
BF16 = mybir.dt.bfloat16
I32 = mybir.dt.int32
AF = mybir.ActivationFunctionType
ALU = mybir.AluOpType
AX = mybir.AxisListType

D = 1024
DFF = 2816
NF = DFF // 128
SEQ = 4096
RMS_EPS = 1e-6

ENGS = ("pe", "act", "dve", "pool", "sp")
_PSUM_KEY = re.compile(r"^(bk|PP|PL|PT|PG|PU|PO|PS|PC|PK|PV|PR)\d*$")
CH = 30000
N_DMA_SEMS = 28
N_SW_SEMS = 8
DEBUG_UNREAD = False


class Prog:
    def __init__(self, nc, stack):
        self.nc = nc
        self.stack = stack
        self.ops = {e: [] for e in ENGS}
        self.count = {e: 0 for e in ENGS}
        self.sems = {e: [] for e in ENGS}
        self.seen = {e: {} for e in ENGS}
        self.dma_sems = [stack.enter_context(nc.semaphore(f"dq{i}")) for i in range(N_DMA_SEMS)]
        self.dma_cnt = [0] * N_DMA_SEMS
        self.dma_rr = 0
        self.dma_rr_sw = 0
        self.writers = {}
        self.readers = {}
        self.n_ops = 0
        self.unread = set()

    def _sem_for(self, e, idx):
        c = idx // CH
        while len(self.sems[e]) <= c:
            self.sems[e].append(self.stack.enter_context(self.nc.semaphore(f"s_{e}{len(self.sems[e])}")))
        return self.sems[e][c], idx % CH + 1

    def _resolve(self, tok):
        if tok[0] == "c":
            return self._sem_for(tok[1], tok[2])
        return self.dma_sems[tok[1]], tok[2]

    def _deps(self, e, reads, writes):
        toks = []
        for k in reads:
            w = self.writers.get(k)
            if w:
                toks.extend(w.values())
        for k in writes:
            w = self.writers.get(k)
            if w:
                toks.extend(w.values())
            r = self.readers.get(k)
            if r:
                toks.extend(r.values())
        best = {}
        for tok in toks:
            if tok[0] == "c" and tok[1] == e and e == "pe":
                continue
            sem, val = self._resolve(tok)
            sid = id(sem)
            if self.seen[e].get(sid, 0) >= val:
                continue
            if sid not in best or best[sid][1] < val:
                best[sid] = (sem, val)
        for sid, (sem, val) in best.items():
            self.seen[e][sid] = val
        return list(best.values())

    def _register(self, tok, slot, reads, writes, partial):
        for k in reads:
            self.readers.setdefault(k, {})[slot] = tok
        for k in writes:
            if DEBUG_UNREAD and not partial and k in self.writers and self.writers[k] and not self.readers.get(k) \
                    and slot not in self.writers[k] and k not in reads:
                self.unread.add(k)
            if partial:
                self.writers.setdefault(k, {})[slot] = tok
            else:
                self.writers[k] = {slot: tok}
            self.readers[k] = {}

    def begin_capture(self):
        self._cap = []

    def end_capture(self):
        lst, self._cap = self._cap, None
        return lst

    def replay(self, lists):
        pos = [0] * len(lists)
        lists = [l for l in lists if l]
        while any(p < len(l) for p, l in zip(pos, lists)):
            if True:
                i = min((j for j in range(len(lists)) if pos[j] < len(lists[j])), key=lambda j: (pos[j] + 0.5) / len(lists[j]))
                l = lists[i]
                if True:
                    kind, args, kw = l[pos[i]]
                    pos[i] += 1
                    if kind == "op":
                        self.op(*args, **dict(kw, sig=True))
                    else:
                        self.dma(*args, **kw)

    def op(self, e, fn, reads=(), writes=(), sig=True, partial=False):
        if getattr(self, "_cap", None) is not None:
            self._cap.append(("op", (e, fn), dict(reads=reads, writes=writes, sig=sig, partial=partial)))
            return None
        rw = [k for k in reads if _PSUM_KEY.match(k) and k not in writes]
        if rw:
            writes = list(writes) + rw
        waits = self._deps(e, reads, writes)
        idx = self.count[e]
        tok = ("c", e, idx)
        inc = None
        if sig:
            inc = self._sem_for(e, idx)[0]
            self.count[e] += 1
        self._register(tok, e, reads, writes, partial)
        self.ops[e].append((fn, waits, inc, 1))
        self.n_ops += 1
        return tok

    def dma(self, e, out, in_, reads=(), writes=(), partial=False, **kw):
        if getattr(self, "_cap", None) is not None:
            self._cap.append(("dma", (e, out, in_), dict(reads=reads, writes=writes, partial=partial, **kw)))
            return None
        if e == "pool":
            k = self.dma_rr_sw
            self.dma_rr_sw = (self.dma_rr_sw + 1) % N_SW_SEMS
        else:
            k = N_SW_SEMS + self.dma_rr
            self.dma_rr = (self.dma_rr + 1) % (N_DMA_SEMS - N_SW_SEMS)
        waits = self._deps(e, reads, writes)
        if self.dma_cnt[k] > 0:
            sem, val = self.dma_sems[k], self.dma_cnt[k]
            if self.seen[e].get(id(sem), 0) < val:
                self.seen[e][id(sem)] = val
                waits.append((sem, val))
        self.dma_cnt[k] += 16
        tok = ("d", k, self.dma_cnt[k])
        self._register(tok, ("d", k), reads, writes, partial)
        fn = lambda eng, out=out, in_=in_, kw=kw: eng.dma_start(out=out, in_=in_, **kw)
        self.ops[e].append((fn, waits, self.dma_sems[k], 16))
        self.n_ops += 1
        return tok

    def wait_all(self, e, keys):
        waits = self._deps(e, keys, ())
        self.ops[e].append((None, waits, None, 0))

    def check(self):
        if not hasattr(self, "simval"):
            self.simval = {}
        pos = {e: 0 for e in ENGS}
        while True:
            prog = False
            for e in ENGS:
                ops = self.ops[e]
                while pos[e] < len(ops):
                    fn, waits, inc, amt = ops[pos[e]]
                    if all(self.simval.get(id(s), 0) >= v for s, v in waits):
                        if inc is not None:
                            self.simval[id(inc)] = self.simval.get(id(inc), 0) + amt
                        pos[e] += 1
                        prog = True
                    else:
                        break
            if all(pos[e] == len(self.ops[e]) for e in ENGS):
                return
            if not prog:
                for e in ENGS:
                    if pos[e] < len(self.ops[e]):
                        fn, waits, inc, amt = self.ops[e][pos[e]]
                        bad = [(s.name if hasattr(s, "name") else str(s), v, self.simval.get(id(s), 0)) for s, v in waits
                               if self.simval.get(id(s), 0) < v]
                        print("DEADLOCK", e, "op#", pos[e], "of", len(self.ops[e]), "waiting", bad)
                raise RuntimeError("deadlock in recorded program")

    def emit(self):
        waits = []
        for k, sem in enumerate(self.dma_sems):
            if self.dma_cnt[k] > self.seen["sp"].get(id(sem), 0):
                waits.append((sem, self.dma_cnt[k]))
                self.seen["sp"][id(sem)] = self.dma_cnt[k]
        if waits:
            self.ops["sp"].append((None, waits, None, 0))
        self.check()
        nc = self.nc
        with nc.Block() as block:
            def mk(e):
                ops = self.ops[e]
                def body(eng):
                    for fn, waits, inc, amt in ops:
                        for sem, val in waits:
                            eng.wait_ge(sem, val)
                        if fn is None:
                            continue
                        ins = fn(eng)
                        if inc is not None:
                            ins.then_inc(inc, amt)
                return body
            block.tensor(mk("pe"))
            block.scalar(mk("act"))
            block.vector(mk("dve"))
            block.gpsimd(mk("pool"))
            block.sync(mk("sp"))
        self.ops = {e: [] for e in ENGS}


class Ctx:
    pass


_UID = [0]


def alloc(st, nc):
    _UID[0] += 1
    u = _UID[0]
    sb = lambda name, shape, dt: st.enter_context(nc.sbuf_tensor(f"{name}_{u}", shape, dt))
    ps = lambda name, shape, dt: st.enter_context(nc.psum_tensor(f"{name}_{u}", shape, dt))
    return sb, ps


def setup_consts(C):
    nc, P = C.nc, C.P
    sb, ps = alloc(C.stack, nc)
    C.ident_f = sb("ident_f", [128, 128], F32)
    C.ident_b = sb("ident_b", [128, 128], BF16)
    C.mhalf = sb("mhalf", [128, 1], F32)
    P.op("pool", lambda e: e.memset(C.ident_f[:], 0.0), writes=["ident_f"])
    P.op("pool", lambda e: e.affine_select(out=C.ident_f[:], in_=C.ident_f[:], pattern=[[-1, 128]],
                                           compare_op=ALU.not_equal, fill=1.0, base=0, channel_multiplier=1),
         reads=["ident_f"], writes=["ident_f"])
    P.op("pool", lambda e: e.tensor_copy(out=C.ident_b[:], in_=C.ident_f[:]), reads=["ident_f"], writes=["ident_b"])
    P.op("pool", lambda e: e.memset(C.mhalf[:], -0.5), writes=["mhalf"])
    P.emit()


def load_col(C, dst, vec_ap, key, nchunk):
    C.P.dma("sp", dst, vec_ap.rearrange("(c p) -> p c", p=128), writes=[key], allow_slow_non_contiguous=True)


def ffn_phase(C, S, src, dst, skey, dkey, g_ap, wg, wu, wd):
    nc, P = C.nc, C.P
    TC = min(1024, S)
    NCH = S // TC
    TT = TC // 128
    NHALF = TC // 512
    FG = 256
    NG = DFF // FG
    with contextlib.ExitStack() as st:
        sb, ps = alloc(st, nc)
        ht = [sb(f"ht{i}", [128, D], F32) for i in range(2)]
        xs = [sb(f"xs{i}", [128, D], BF16) for i in range(2)]
        junk = sb("junk", [128, D], BF16)
        ss = sb("ss", [128, 2], F32)
        vv = sb("vv", [128, 2], F32)
        rstd = sb("rstd", [128, 2], F32)
        hnT = sb("hnT", [128, 8, TC], BF16)
        actT = sb("actT", [128, NF, TC], BF16)
        wgt = [sb(f"wgt{i}", [128, 8, FG], BF16) for i in range(2)]
        wut = [sb(f"wut{i}", [128, 8, FG], BF16) for i in range(2)]
        wdt = sb("wdt", [128, NF, D], BF16)
        ho = [sb(f"ho{i}", [128, D], F32) for i in range(2)]
        hres = [sb(f"hres{i}", [128, D], F32) for i in range(2)]
        gcol = sb("gcol", [128, 8], F32)
        gfull = sb("gfull", [128, 8, 128], F32)
        sg = [sb(f"sg{i}", [128, 512], F32) for i in range(2)]
        PT = [ps(f"PT{i}", [128, D], BF16) for i in range(2)]
        PG = [ps(f"PG{i}", [128, 512], F32) for i in range(2)]
        PU = [ps(f"PU{i}", [128, 512], F32) for i in range(2)]
        PO = [ps(f"PO{i}", [128, 512], F32) for i in range(2)]

        load_col(C, gcol[:, :], g_ap, "gcol", 8)
        P.op("dve", lambda e: e.tensor_copy(out=gfull[:], in_=gcol[:, :].unsqueeze(2).to_broadcast([128, 8, 128])),
             reads=["gcol"], writes=["gfull"])

        cnt = {"a": 0, "gu": 0, "po": 0, "o": 0}

        def stage_a(c, t):
            i = cnt["a"] % 2
            cnt["a"] += 1
            gt = c * TT + t
            rows = slice(gt * 128, (gt + 1) * 128)
            P.dma("sp", ht[i][:], src[rows, :], reads=[f"{skey}{gt}"], writes=[f"ht{i}"])
            P.op("act", lambda e: e.activation(out=junk[:], in_=ht[i][:], func=AF.Square, accum_out=ss[:, i:i + 1]),
                 reads=[f"ht{i}"], writes=["junk", f"ss{i}"])
            P.op("pool", lambda e: e.tensor_scalar(out=vv[:, i:i + 1], in0=ss[:, i:i + 1], scalar1=1.0 / D, scalar2=RMS_EPS,
                                                   op0=ALU.mult, op1=ALU.add), reads=[f"ss{i}"], writes=[f"vv{i}"])
            P.op("pool", lambda e: e.tensor_tensor(out=rstd[:, i:i + 1], in0=vv[:, i:i + 1], in1=C.mhalf[:], op=ALU.pow),
                 reads=[f"vv{i}"], writes=[f"rstd{i}"])
            P.op("dve", lambda e: e.tensor_scalar(out=xs[i][:], in0=ht[i][:], scalar1=rstd[:, i:i + 1], scalar2=None,
                                                  op0=ALU.mult), reads=[f"ht{i}", f"rstd{i}"], writes=[f"xs{i}"])
            return i

        def stage_a2(c, t, i):
            for cc in range(8):
                P.op("pe", lambda e, cc=cc: e.transpose(out=PT[i][:, cc * 128:(cc + 1) * 128],
                                                        in_=xs[i][:, cc * 128:(cc + 1) * 128], identity=C.ident_b[:]),
                     reads=[f"xs{i}", "ident_b"], writes=[f"PT{i}"], sig=(cc == 7), partial=(cc > 0))
            P.op("dve", lambda e: e.tensor_tensor(out=hnT[:, :, t * 128:(t + 1) * 128],
                                                  in0=PT[i][:, :].rearrange("p (c j) -> p c j", c=8), in1=gfull[:],
                                                  op=ALU.mult),
                 reads=[f"PT{i}", "gfull"], writes=["hnT"])

        def stage_b(c):
            for g in range(NG):
                wb = g % 2
                cols = slice(g * FG, (g + 1) * FG)
                P.dma("pool", wgt[wb][:], wg[:, cols].rearrange("(c p) n -> p c n", p=128), writes=[f"wgt{wb}"])
                P.dma("pool", wut[wb][:], wu[:, cols].rearrange("(c p) n -> p c n", p=128), writes=[f"wut{wb}"])
                load_wd_piece(g)
                for fl in range(FG // 128):
                    f = g * (FG // 128) + fl
                    for hf in range(NHALF):
                        k = cnt["gu"] % 2
                        cnt["gu"] += 1
                        tok = slice(hf * 512, (hf + 1) * 512)
                        for cc in range(8):
                            P.op("pe", lambda e, cc=cc, k=k, tok=tok, fl=fl, wb=wb: e.matmul(
                                PG[k][:], lhsT=wgt[wb][:, cc, fl * 128:(fl + 1) * 128], rhs=hnT[:, cc, tok],
                                start=(cc == 0), stop=(cc == 7)),
                                 reads=[f"wgt{wb}", "hnT"], writes=[f"PG{k}"], sig=(cc == 7))
                        for cc in range(8):
                            P.op("pe", lambda e, cc=cc, k=k, tok=tok, fl=fl, wb=wb: e.matmul(
                                PU[k][:], lhsT=wut[wb][:, cc, fl * 128:(fl + 1) * 128], rhs=hnT[:, cc, tok],
                                start=(cc == 0), stop=(cc == 7)),
                                 reads=[f"wut{wb}", "hnT"], writes=[f"PU{k}"], sig=(cc == 7))
                        P.op("act", lambda e, k=k: e.activation(out=sg[k][:], in_=PG[k][:], func=AF.Silu),
                             reads=[f"PG{k}"], writes=[f"sg{k}"])
                        P.op("dve", lambda e, k=k, f=f, tok=tok: e.tensor_tensor(out=actT[:, f, tok], in0=sg[k][:], in1=PU[k][:],
                                                                               op=ALU.mult),
                             reads=[f"sg{k}", f"PU{k}"], writes=["actT"])

        def load_wd_piece(g):
            fr = slice(2 * g, 2 * g + 2)
            P.dma("pool", wdt[:, fr, :], wd[2 * g * 128:(2 * g + 2) * 128, :].rearrange("(f p) n -> p f n", p=128),
                  writes=["wdt"], partial=(g > 0))

        def hres_load(c, t):
            gt = c * TT + t
            j = gt % 2
            P.dma("sp", hres[j][:], src[gt * 128:(gt + 1) * 128, :], reads=[f"{skey}{gt}"], writes=[f"hres{j}"])

        def stage_c(c, t):
            gt = c * TT + t
            rows = slice(gt * 128, (gt + 1) * 128)
            j = gt % 2
            if t + 1 < TT:
                hres_load(c, t + 1)
            for dh in range(2):
                k = cnt["po"] % 2
                cnt["po"] += 1
                for f in range(NF):
                    P.op("pe", lambda e, f=f, k=k, dh=dh: e.matmul(
                        PO[k][:], lhsT=actT[:, f, t * 128:(t + 1) * 128], rhs=wdt[:, f, dh * 512:(dh + 1) * 512],
                        start=(f == 0), stop=(f == NF - 1)),
                         reads=["actT", "wdt"], writes=[f"PO{k}"], sig=(f == NF - 1))
                P.op("dve", lambda e, k=k, dh=dh, j=j: e.scalar_tensor_tensor(
                    out=ho[j][:, dh * 512:(dh + 1) * 512], in0=PO[k][:], scalar=0.5, in1=hres[j][:, dh * 512:(dh + 1) * 512],
                    op0=ALU.mult, op1=ALU.add),
                     reads=[f"PO{k}", f"hres{j}"], writes=[f"ho{j}"], partial=(dh > 0))
            P.dma("sp", dst[rows, :], ho[j][:], reads=[f"ho{j}"], writes=[f"{dkey}{gt}"])

        for t in range(TT):
            stage_a2(0, t, stage_a(0, t))
        for c in range(NCH):
            stage_b(c)
            nxt = stage_a(c + 1, 0) if c + 1 < NCH else None
            hres_load(c, 0)
            for t in range(TT):
                nxt2 = stage_a(c + 1, t + 1) if (c + 1 < NCH and t + 1 < TT) else None
                stage_c(c, t)
                if nxt is not None:
                    stage_a2(c + 1, t, nxt)
                nxt = nxt2
        P.emit()


def final_norm_phase(C, S, src, dst, skey, dkey, g_ap):
    nc, P = C.nc, C.P
    with contextlib.ExitStack() as st:
        sb, ps = alloc(st, nc)
        ht = [sb(f"ht{i}", [128, D], F32) for i in range(2)]
        ot = [sb(f"ot{i}", [128, D], F32) for i in range(2)]
        junk = sb("junk", [128, D], BF16)
        ss = sb("ss", [128, 2], F32)
        vv = sb("vv", [128, 2], F32)
        rstd = sb("rstd", [128, 2], F32)
        gb = sb("gb", [128, D], F32)
        P.dma("sp", gb[:], g_ap.partition_broadcast(128), writes=["gb"])
        def fload(t):
            P.dma("sp", ht[t % 2][:], src[t * 128:(t + 1) * 128, :], reads=[f"{skey}{t}"], writes=[f"ht{t % 2}"])

        fload(0)
        for t in range(S // 128):
            i = t % 2
            rows = slice(t * 128, (t + 1) * 128)
            if t + 1 < S // 128:
                fload(t + 1)
            P.op("act", lambda e, i=i: e.activation(out=junk[:], in_=ht[i][:], func=AF.Square, accum_out=ss[:, i:i + 1]),
                 reads=[f"ht{i}"], writes=["junk", f"ss{i}"])
            P.op("pool", lambda e, i=i: e.tensor_scalar(out=vv[:, i:i + 1], in0=ss[:, i:i + 1], scalar1=1.0 / D, scalar2=RMS_EPS,
                                                        op0=ALU.mult, op1=ALU.add), reads=[f"ss{i}"], writes=[f"vv{i}"])
            P.op("pool", lambda e, i=i: e.tensor_tensor(out=rstd[:, i:i + 1], in0=vv[:, i:i + 1], in1=C.mhalf[:], op=ALU.pow),
                 reads=[f"vv{i}"], writes=[f"rstd{i}"])
            P.op("dve", lambda e, i=i: e.scalar_tensor_tensor(out=ot[i][:], in0=ht[i][:], scalar=rstd[:, i:i + 1], in1=gb[:],
                                                              op0=ALU.mult, op1=ALU.mult),
                 reads=[f"ht{i}", f"rstd{i}", "gb"], writes=[f"ot{i}"])
            P.dma("sp", dst[rows, :], ot[i][:], reads=[f"ot{i}"], writes=[f"{dkey}{t}"])
        P.wait_all("sp", [f"{dkey}{t}" for t in range(S // 128)])
        P.emit()


HS = 64
NH = 16
GN_EPS = 64e-5


def bcast_load(C, dst, vec_ap, key):
    C.P.dma("sp", dst, vec_ap.partition_broadcast(128), writes=[key])


def rwkv_pass_a_v1(C, S, h, hkey, W, scr):
    nc, P = C.nc, C.P
    NT = S // 128
    with contextlib.ExitStack() as st:
        sb, ps = alloc(st, nc)
        wr = sb("wr", [128, 8, D], BF16)
        wk = sb("wk", [128, 8, D], BF16)
        wv = sb("wv", [128, 8, D], BF16)
        w1 = sb("w1", [128, 8, 64], BF16)
        a1 = sb("a1", [128, 8, 64], BF16)
        g1 = sb("g1", [128, 8, 128], BF16)
        w2 = sb("w2", [64, D], BF16)
        a2 = sb("a2", [64, D], BF16)
        g2 = sb("g2", [128, D], BF16)
        w0b = sb("w0b", [128, D], F32)
        a0b = sb("a0b", [128, D], F32)
        kkb = sb("kkb", [128, D], F32)
        kab = sb("kab", [128, D], F32)
        rkb = sb("rkb", [128, D], F32)
        gcol = sb("gcol", [128, 8], F32)
        gfull = sb("gfull", [128, 8, 128], F32)
        mixc = sb("mixc", [128, 6, 8], F32)
        trif = sb("trif", [128, 128], F32)
        ones = sb("ones", [128, 1], F32)
        ht = [sb(f"ht{i}", [128, D], F32) for i in range(2)]
        xs = sb("xs", [128, D], BF16)
        junk = sb("junk", [128, D], BF16)
        st4 = sb("st4", [128, 8], F32)
        hnTe = [sb(f"hnTe{i}", [128, 8, 130], BF16) for i in range(2)]
        dxT = sb("dxT", [128, 8, 128], F32)
        tmpT = [sb(f"tmpT{i}", [128, 8, 128], F32) for i in range(2)]
        xT = [sb(f"xT{i}", [128, 8, 128], BF16) for i in range(2)]
        l1 = [sb(f"l1{i}", [128, 128], BF16) for i in range(3)]
        T = [sb(f"T{i}", [128, D], F32) for i in range(10)]
        ob = [sb(f"ob{i}", [128, D], BF16) for i in range(5)]
        s16 = sb("s16", [128, 4, 16], F32)
        gct = sb("gct", [128, 8], F32)
        PT = ps("PT", [128, D], BF16)
        PP = [ps(f"PP{i}", [128, D], F32) for i in range(2)]
        PL = ps("PL", [128, 512], F32)

        for wt, nm in ((wr, "rwkv_w_r"), (wk, "rwkv_w_k"), (wv, "rwkv_w_v")):
            for q in range(2):
                P.dma("pool", wt[:, :, q * 512:(q + 1) * 512], W[nm][:, q * 512:(q + 1) * 512].rearrange("(c p) n -> p c n", p=128),
                      writes=[nm], partial=(q > 0))
        for wt, nm in ((w1, "rwkv_w1"), (a1, "rwkv_a1"), (g1, "rwkv_g1")):
            P.dma("pool", wt[:], W[nm].rearrange("(c p) n -> p c n", p=128), writes=[nm])
        for wt, nm in ((w2, "rwkv_w2"), (a2, "rwkv_a2"), (g2, "rwkv_g2")):
            P.dma("pool", wt[:], W[nm], writes=[nm])
        for wt, nm in ((w0b, "rwkv_w0"), (a0b, "rwkv_a0"), (kkb, "rwkv_k_k"), (kab, "rwkv_k_a"), (rkb, "rwkv_r_k")):
            bcast_load(C, wt[:], W[nm], nm)
        load_col(C, gcol[:, :], W["norm_g"], "gcol", 8)
        P.op("dve", lambda e: e.tensor_copy(out=gfull[:], in_=gcol[:, :].unsqueeze(2).to_broadcast([128, 8, 128])),
             reads=["gcol"], writes=["gfull"])
        P.dma("sp", mixc[:], W["rwkv_mix"].rearrange("i (c p) -> p i c", p=128), writes=["mixc"], allow_slow_non_contiguous=True)
        P.op("pool", lambda e: e.memset(trif[:], 1.0), writes=["trif"])
        P.op("pool", lambda e: e.affine_select(out=trif[:], in_=trif[:], pattern=[[1, 128]], compare_op=ALU.is_ge, fill=0.0,
                                               base=0, channel_multiplier=-1), reads=["trif"], writes=["trif"])
        P.op("pool", lambda e: e.memset(ones[:], 1.0), writes=["ones"])
        P.op("pool", lambda e: e.memset(hnTe[0][:, :, 0:2], 0.0), writes=["hnTe0"])

        def proj(xbuf, wt, wkey, dstf, evac_key):
            k = proj.n % 2
            proj.n += 1
            for hf in range(2):
                for cc in range(8):
                    P.op("pe", lambda e, cc=cc, hf=hf, k=k: e.matmul(PP[k][:, hf * 512:(hf + 1) * 512], lhsT=xT[xbuf][:, cc, :],
                                                                    rhs=wt[:, cc, hf * 512:(hf + 1) * 512], start=(cc == 0), stop=(cc == 7)),
                         reads=[f"xT{xbuf}", wkey], writes=[f"PP{k}"], sig=(cc == 7 and hf == 1), partial=(hf > 0 or cc > 0))
            dstf(k)
        proj.n = 0

        def lora(xbuf, wt1, k1key, width, li, func, wt2, k2key, dstf):
            for cc in range(8):
                P.op("pe", lambda e, cc=cc: e.matmul(PL[0:width, li * 128:(li + 1) * 128], lhsT=wt1[:, cc, :], rhs=xT[xbuf][:, cc, :],
                                                     start=(cc == 0), stop=(cc == 7)),
                     reads=[f"xT{xbuf}", k1key], writes=[f"PL{li}"], sig=(cc == 7))
            P.op("act", lambda e: e.activation(out=l1[li][0:width, :], in_=PL[0:width, li * 128:(li + 1) * 128], func=func),
                 reads=[f"PL{li}"], writes=[f"l1{li}"])
            k = proj.n % 2
            proj.n += 1
            for hf in range(2):
                P.op("pe", lambda e, hf=hf, k=k: e.matmul(PP[k][:, hf * 512:(hf + 1) * 512], lhsT=l1[li][0:width, :],
                                                          rhs=wt2[0:width, hf * 512:(hf + 1) * 512], start=True, stop=True),
                     reads=[f"l1{li}", k2key], writes=[f"PP{k}"], sig=(hf == 1), partial=(hf > 0))
            dstf(k)

        mixn = [0]

        def mix(i):
            b = mixn[0] % 2
            mixn[0] += 1
            e1 = "dve" if b == 0 else "pool"
            P.op(e1, lambda e: e.tensor_tensor(out=tmpT[b][:], in0=dxT[:], in1=mixc[:, i, :].unsqueeze(2).to_broadcast([128, 8, 128]),
                                               op=ALU.mult), reads=["dxT", "mixc"], writes=[f"tmpT{b}"])
            cur = cur_ap[0]
            P.op(e1, lambda e: e.tensor_tensor(out=xT[b][:], in0=tmpT[b][:], in1=cur, op=ALU.add),
                 reads=[f"tmpT{b}", cur_key[0]], writes=[f"xT{b}"])
            return b

        cur_ap = [None]
        cur_key = [None]

        for t in range(NT):
            i = t % 2
            rows = slice(t * 128, (t + 1) * 128)
            hb = hnTe[i]
            P.dma("sp", ht[i][:], h[rows, :], reads=[f"{hkey}{t}"], writes=[f"ht{i}"])
            P.op("act", lambda e, i=i: e.activation(out=junk[:], in_=ht[i][:], func=AF.Square, accum_out=st4[:, 0:1]),
                 reads=[f"ht{i}"], writes=["junk", "st0"])
            P.op("pool", lambda e: e.tensor_scalar(out=st4[:, 1:2], in0=st4[:, 0:1], scalar1=1.0 / D, scalar2=RMS_EPS,
                                                   op0=ALU.mult, op1=ALU.add), reads=["st0"], writes=["st1"])
            P.op("pool", lambda e: e.tensor_tensor(out=st4[:, 2:3], in0=st4[:, 1:2], in1=C.mhalf[:], op=ALU.pow),
                 reads=["st1"], writes=["st2"])
            P.op("dve", lambda e, i=i: e.tensor_scalar(out=xs[:], in0=ht[i][:], scalar1=st4[:, 2:3], scalar2=None, op0=ALU.mult),
                 reads=[f"ht{i}", "st2"], writes=["xs"])
            for cc in range(8):
                P.op("pe", lambda e, cc=cc: e.transpose(out=PT[:, cc * 128:(cc + 1) * 128], in_=xs[:, cc * 128:(cc + 1) * 128],
                                                        identity=C.ident_b[:]),
                     reads=["xs", "ident_b"], writes=["PT"], sig=(cc == 7), partial=(cc > 0))
            P.op("dve", lambda e, hb=hb: e.tensor_tensor(out=hb[:, :, 2:130], in0=PT[:, :].rearrange("p (c j) -> p c j", c=8),
                                                         in1=gfull[:], op=ALU.mult),
                 reads=["PT", "gfull"], writes=[f"hnTe{i}"])
            if t + 1 < NT:
                P.op("pool", lambda e, hb=hb, i=i: e.tensor_copy(out=hnTe[1 - i][:, :, 0:2], in_=hb[:, :, 128:130]),
                     reads=[f"hnTe{i}"], writes=[f"hnTe{1 - i}"])
            cur_ap[0] = hb[:, :, 2:130]
            cur_key[0] = f"hnTe{i}"
            P.op("dve", lambda e, hb=hb: e.tensor_tensor(out=dxT[:], in0=hb[:, :, 1:129], in1=hb[:, :, 2:130], op=ALU.subtract),
                 reads=[f"hnTe{i}"], writes=["dxT"])
            R, KR, V, WP, AL, G, KK, KM, GA, TM = T
            RT, AT, BT, KT, VB = ob
            b = mix(0)
            proj(b, wr, "rwkv_w_r", lambda k: P.op("act", lambda e: e.activation(out=R[:], in_=PP[k][:], func=AF.Copy),
                                                   reads=[f"PP{k}"], writes=["R"]), "R")
            b = mix(1)
            lora(b, w1, "rwkv_w1", 64, 0, AF.Tanh, w2, "rwkv_w2",
                 lambda k: P.op("dve", lambda e: e.tensor_tensor(out=WP[:], in0=PP[k][:], in1=w0b[:], op=ALU.add),
                                reads=[f"PP{k}", "rwkv_w0"], writes=["WP"]))
            b = mix(2)
            proj(b, wk, "rwkv_w_k", lambda k: P.op("act", lambda e: e.activation(out=KR[:], in_=PP[k][:], func=AF.Copy),
                                                   reads=[f"PP{k}"], writes=["KR"]), "KR")
            b = mix(3)

            def evac_v(k):
                P.op("act", lambda e: e.activation(out=V[:], in_=PP[k][:], func=AF.Copy), reads=[f"PP{k}"], writes=["V"])
                P.op("pool", lambda e: e.tensor_copy(out=VB[:], in_=V[:]), reads=["V"], writes=["VB"])
            proj(b, wv, "rwkv_w_v", evac_v, "V")
            b = mix(4)

            def evac_a(k):
                P.op("dve", lambda e: e.tensor_tensor(out=AL[:], in0=PP[k][:], in1=a0b[:], op=ALU.add),
                     reads=[f"PP{k}", "rwkv_a0"], writes=["AL"])
                P.op("act", lambda e: e.activation(out=AL[:], in_=AL[:], func=AF.Sigmoid), reads=["AL"], writes=["AL"])
            lora(b, a1, "rwkv_a1", 64, 1, AF.Copy, a2, "rwkv_a2", evac_a)
            b = mix(5)
            lora(b, g1, "rwkv_g1", 128, 2, AF.Sigmoid, g2, "rwkv_g2",
                 lambda k: P.op("act", lambda e: e.activation(out=G[:], in_=PP[k][:], func=AF.Copy),
                                reads=[f"PP{k}"], writes=["G"]))
            P.dma("sp", scr["G"][rows, :], G[:], reads=["G"], writes=[f"sG{t}"])
            v3 = lambda ap: ap.rearrange("p (h n) -> p h n", h=NH)
            bc = lambda col: col.unsqueeze(2).to_broadcast([128, NH, HS])
            P.op("act", lambda e: e.activation(out=WP[:], in_=WP[:], func=AF.Exp, scale=-1.0), reads=["WP"], writes=["WP"])
            P.op("act", lambda e: e.activation(out=WP[:], in_=WP[:], func=AF.Ln, bias=1.0), reads=["WP"], writes=["WP"])
            P.op("act", lambda e: e.activation(out=WP[:], in_=WP[:], func=AF.Exp, scale=-1.0, bias=-0.5), reads=["WP"], writes=["WP"])
            P.op("dve", lambda e: e.tensor_tensor(out=KK[:], in0=KR[:], in1=kkb[:], op=ALU.mult), reads=["KR", "rwkv_k_k"], writes=["KK"])
            P.op("pool", lambda e: e.tensor_tensor(out=TM[:], in0=KK[:], in1=KK[:], op=ALU.mult), reads=["KK"], writes=["TM"])
            P.op("dve", lambda e: e.tensor_reduce(out=s16[:, 0, :], in_=v3(TM[:]), axis=AX.X, op=ALU.add), reads=["TM"], writes=["s16a"])
            P.op("pool", lambda e: e.tensor_scalar(out=s16[:, 1, :], in0=s16[:, 0, :], scalar1=1e-24, scalar2=None, op0=ALU.max),
                 reads=["s16a"], writes=["s16b"])
            P.op("pool", lambda e: e.tensor_tensor(out=s16[:, 1, :], in0=s16[:, 1, :], in1=C.mhalf[:, 0:1].to_broadcast([128, NH]),
                                                   op=ALU.pow), reads=["s16b"], writes=["s16b"])
            P.op("dve", lambda e: e.tensor_tensor(out=v3(KK[:]), in0=v3(KK[:]), in1=bc(s16[:, 1, :]), op=ALU.mult),
                 reads=["KK", "s16b"], writes=["KK"])
            P.op("dve", lambda e: e.scalar_tensor_tensor(out=KM[:], in0=AL[:], scalar=-1.0, in1=kab[:], op0=ALU.add, op1=ALU.mult),
                 reads=["AL", "rwkv_k_a"], writes=["KM"])
            P.op("dve", lambda e: e.scalar_tensor_tensor(out=KM[:], in0=KM[:], scalar=1.0, in1=KR[:], op0=ALU.add, op1=ALU.mult),
                 reads=["KM", "KR"], writes=["KM"])
            P.op("pool", lambda e: e.tensor_tensor(out=TM[:], in0=R[:], in1=rkb[:], op=ALU.mult), reads=["R", "rwkv_r_k"], writes=["TM"])
            P.op("pool", lambda e: e.tensor_tensor(out=TM[:], in0=TM[:], in1=KM[:], op=ALU.mult), reads=["TM", "KM"], writes=["TM"])
            P.op("dve", lambda e: e.tensor_reduce(out=s16[:, 2, :], in_=v3(TM[:]), axis=AX.X, op=ALU.add), reads=["TM"], writes=["s16c"])
            P.op("pool", lambda e: e.tensor_tensor(out=v3(TM[:]), in0=v3(V[:]), in1=bc(s16[:, 2, :]), op=ALU.mult),
                 reads=["V", "s16c"], writes=["TM"])
            P.dma("sp", scr["BON"][rows, :], TM[:], reads=["TM"], writes=[f"sBON{t}"])
            k = proj.n % 2
            proj.n += 1
            for hf in range(2):
                P.op("pe", lambda e, hf=hf, k=k: e.matmul(PP[k][:, hf * 512:(hf + 1) * 512], lhsT=trif[:], rhs=WP[:, hf * 512:(hf + 1) * 512],
                                                          start=True, stop=True),
                     reads=["trif", "WP"], writes=[f"PP{k}"], sig=(hf == 1), partial=(hf > 0))
            for cc in range(8):
                P.op("pe", lambda e, cc=cc: e.matmul(PL[:, 384 + cc:385 + cc], lhsT=WP[:, cc * 128:(cc + 1) * 128], rhs=ones[:],
                                                     start=True, stop=True),
                     reads=["WP", "ones"], writes=["PL3"], sig=(cc == 7), partial=(cc > 0))
            P.op("act", lambda e: e.activation(out=gct[:], in_=PL[:, 384:392], func=AF.Exp, scale=-1.0), reads=["PL3"], writes=["gct"])
            P.dma("sp", scr["GC"][t], gct[:], reads=["gct"], writes=[f"sGC{t}"])
            P.op("act", lambda e, k=k: e.activation(out=GA[:], in_=PP[k][:], func=AF.Exp, scale=-1.0), reads=[f"PP{k}"], writes=["GA"])
            P.op("pool", lambda e: e.tensor_tensor(out=RT[:], in0=R[:], in1=GA[:], op=ALU.mult), reads=["R", "GA"], writes=["RT"])
            P.dma("sp", scr["RT"][rows, :], RT[:], reads=["RT"], writes=[f"sRT{t}"])
            P.op("act", lambda e, k=k: e.activation(out=GA[:], in_=PP[k][:], func=AF.Exp), reads=[f"PP{k}"], writes=["GA"])
            P.op("dve", lambda e: e.tensor_tensor(out=KT[:], in0=KM[:], in1=GA[:], op=ALU.mult), reads=["KM", "GA"], writes=["KT"])
            P.dma("sp", scr["KT"][rows, :], KT[:], reads=["KT"], writes=[f"sKT{t}"])
            P.op("pool", lambda e: e.tensor_tensor(out=TM[:], in0=KK[:], in1=AL[:], op=ALU.mult), reads=["KK", "AL"], writes=["TM"])
            P.op("dve", lambda e: e.tensor_tensor(out=BT[:], in0=TM[:], in1=GA[:], op=ALU.mult), reads=["TM", "GA"], writes=["BT"])
            P.dma("sp", scr["BT"][rows, :], BT[:], reads=["BT"], writes=[f"sBT{t}"])
            P.op("dve", lambda e, k=k: e.tensor_tensor(out=TM[:], in0=PP[k][:], in1=WP[:], op=ALU.subtract),
                 reads=[f"PP{k}", "WP"], writes=["TM"])
            P.op("act", lambda e: e.activation(out=TM[:], in_=TM[:], func=AF.Exp, scale=-1.0), reads=["TM"], writes=["TM"])
            P.op("dve", lambda e: e.scalar_tensor_tensor(out=AT[:], in0=KK[:], scalar=-1.0, in1=TM[:], op0=ALU.mult, op1=ALU.mult),
                 reads=["KK", "TM"], writes=["AT"])
            P.dma("sp", scr["AT"][rows, :], AT[:], reads=["AT"], writes=[f"sAT{t}"])
            P.dma("sp", scr["VB"][rows, :], VB[:], reads=["VB"], writes=[f"sVB{t}"])
        P.emit()


def rwkv_pass_a(C, S, h, hkey, W, scr):
    nc, P = C.nc, C.P
    NT = S // 128
    HW = 512
    with contextlib.ExitStack() as st:
        sb, ps = alloc(st, nc)
        wr = sb("wr", [128, 8, D], BF16)
        wk = sb("wk", [128, 8, D], BF16)
        wv = sb("wv", [128, 8, D], BF16)
        w1 = sb("w1", [128, 2, 8, 64], BF16)
        a1 = sb("a1", [128, 2, 8, 64], BF16)
        g1 = sb("g1", [128, 2, 8, 128], BF16)
        w2 = sb("w2", [64, D], BF16)
        a2 = sb("a2", [64, D], BF16)
        g2 = sb("g2", [128, D], BF16)
        w0b = sb("w0b", [128, D], F32)
        a0b = sb("a0b", [128, D], F32)
        kkb = sb("kkb", [128, D], F32)
        kab = sb("kab", [128, D], F32)
        rkb = sb("rkb", [128, D], F32)
        gcol = sb("gcol", [128, 8], F32)
        gfull = sb("gfull", [128, 8, 128], F32)
        mixc = sb("mixc", [128, 6, 8], F32)
        trif = sb("trif", [128, 128], F32)
        ones = sb("ones", [128, 1], F32)
        ht = [sb(f"ht{i}", [128, D], F32) for i in range(2)]
        xs = sb("xs", [128, D], BF16)
        junk = sb("junk", [128, D], BF16)
        st4 = sb("st4", [128, 8], F32)
        hnTe = [sb(f"hnTe{i}", [128, 8, 130], BF16) for i in range(2)]
        dxT = sb("dxT", [128, 8, 128], F32)
        dxb = sb("dxb", [128, 8, 128], BF16)
        tmpT = sb("tmpT", [128, 8, 128], F32)
        xT = [[sb(f"xT{j}_{i}", [128, 8, 128], BF16) for i in range(3)] for j in range(2)]
        l1 = [[sb(f"l1{j}_{i}", [128, 128], BF16) for i in range(3)] for j in range(2)]
        TS = [[sb(f"T{u}_{i}", [128, HW], F32) for i in range(10)] for u in range(2)]
        OB = [[sb(f"ob{u}_{i}", [128, HW], BF16) for i in range(5)] for u in range(2)]
        s16 = [sb(f"s16_{u}", [128, 4, 8], F32) for u in range(2)]
        gct = [sb(f"gct{u}", [128, 4], F32) for u in range(2)]
        PT = ps("PT", [128, D], BF16)
        PPn = 4
        PP = [ps(f"PP{i}", [128, HW], F32) for i in range(PPn)]
        PCm = [ps(f"PC{i}", [128, HW], F32) for i in range(2)]
        PL = ps("PL", [128, 512], F32)

        for wt, nm in ((wr, "rwkv_w_r"), (wk, "rwkv_w_k"), (wv, "rwkv_w_v")):
            for q in range(2):
                P.dma("pool", wt[:, :, q * 512:(q + 1) * 512], W[nm][:, q * 512:(q + 1) * 512].rearrange("(c p) n -> p c n", p=128),
                      writes=[nm], partial=(q > 0))
        for wt, nm in ((w1, "rwkv_w1"), (a1, "rwkv_a1"), (g1, "rwkv_g1")):
            P.dma("pool", wt[:, 0, :, :], W[nm].rearrange("(c p) n -> p c n", p=128), writes=[nm])
        for wt, nm in ((w2, "rwkv_w2"), (a2, "rwkv_a2"), (g2, "rwkv_g2")):
            P.dma("pool", wt[:], W[nm], writes=[nm])
        for wt, nm in ((w0b, "rwkv_w0"), (a0b, "rwkv_a0"), (kkb, "rwkv_k_k"), (kab, "rwkv_k_a"), (rkb, "rwkv_r_k")):
            bcast_load(C, wt[:], W[nm], nm)
        load_col(C, gcol[:, :], W["norm_g"], "gcol", 8)
        P.op("dve", lambda e: e.tensor_copy(out=gfull[:], in_=gcol[:, :].unsqueeze(2).to_broadcast([128, 8, 128])),
             reads=["gcol"], writes=["gfull"])
        P.dma("sp", mixc[:], W["rwkv_mix"].rearrange("i (c p) -> p i c", p=128), writes=["mixc"], allow_slow_non_contiguous=True)
        for wt, nm, mi, wd_ in ((w1, "rwkv_w1", 1, 64), (a1, "rwkv_a1", 4, 64), (g1, "rwkv_g1", 5, 128)):
            P.op("dve", lambda e, wt=wt, mi=mi, wd_=wd_: e.tensor_tensor(out=wt[:, 1, :, :], in0=wt[:, 0, :, :],
                                                                         in1=mixc[:, mi, :].unsqueeze(2).to_broadcast([128, 8, wd_]), op=ALU.mult),
                 reads=[nm, "mixc"], writes=[nm])
        P.op("pool", lambda e: e.memset(trif[:], 1.0), writes=["trif"])
        P.op("pool", lambda e: e.affine_select(out=trif[:], in_=trif[:], pattern=[[1, 128]], compare_op=ALU.is_ge, fill=0.0,
                                               base=0, channel_multiplier=-1), reads=["trif"], writes=["trif"])
        P.op("pool", lambda e: e.memset(ones[:], 1.0), writes=["ones"])
        P.op("pool", lambda e: e.memset(hnTe[0][:, :, 0:2], 0.0), writes=["hnTe0"])
        ppn = [0]

        def next_pp():
            k = ppn[0] % PPn
            ppn[0] += 1
            return k

        def do_tile(t):
            i = t % 2
            rows = slice(t * 128, (t + 1) * 128)
            hb = hnTe[i]
            hk = f"hnTe{i}"
            cur = hb[:, :, 2:130]
            P.dma("sp", ht[i][:], h[rows, :], reads=[f"{hkey}{t}"], writes=[f"ht{i}"])
            P.op("act", lambda e: e.activation(out=junk[:], in_=ht[i][:], func=AF.Square, accum_out=st4[:, 0:1]),
                 reads=[f"ht{i}"], writes=["junk", "st0"])
            P.op("pool", lambda e: e.tensor_scalar(out=st4[:, 1:2], in0=st4[:, 0:1], scalar1=1.0 / D, scalar2=RMS_EPS,
                                                   op0=ALU.mult, op1=ALU.add), reads=["st0"], writes=["st1"])
            P.op("pool", lambda e: e.tensor_tensor(out=st4[:, 2:3], in0=st4[:, 1:2], in1=C.mhalf[:], op=ALU.pow),
                 reads=["st1"], writes=["st2"])
            P.op("dve", lambda e: e.tensor_scalar(out=xs[:], in0=ht[i][:], scalar1=st4[:, 2:3], scalar2=None, op0=ALU.mult),
                 reads=[f"ht{i}", "st2"], writes=["xs"])
            for cc in range(8):
                P.op("pe", lambda e, cc=cc: e.transpose(out=PT[:, cc * 128:(cc + 1) * 128], in_=xs[:, cc * 128:(cc + 1) * 128],
                                                        identity=C.ident_b[:]),
                     reads=["xs", "ident_b"], writes=["PT"], sig=(cc == 7), partial=(cc > 0))
            P.op("dve", lambda e: e.tensor_tensor(out=hb[:, :, 2:130], in0=PT[:, :].rearrange("p (c j) -> p c j", c=8),
                                                  in1=gfull[:], op=ALU.mult),
                 reads=["PT", "gfull"], writes=[hk])
            if t + 1 < NT:
                P.op("pool", lambda e: e.tensor_copy(out=hnTe[1 - i][:, :, 0:2], in_=hb[:, :, 128:130]),
                     reads=[hk], writes=[f"hnTe{1 - i}"])
            P.op("dve", lambda e: e.tensor_tensor(out=dxT[:], in0=hb[:, :, 1:129], in1=hb[:, :, 2:130], op=ALU.subtract),
                 reads=[hk], writes=["dxT"])
            P.op("act", lambda e: e.activation(out=dxb[:], in_=dxT[:], func=AF.Copy), reads=["dxT"], writes=["dxb"])
            for xi, mi in ((0, 0), (1, 2), (2, 3)):
                e1 = "dve" if xi != 1 else "pool"
                P.op(e1, lambda e, mi=mi: e.tensor_tensor(out=tmpT[:], in0=dxT[:], in1=mixc[:, mi, :].unsqueeze(2).to_broadcast([128, 8, 128]),
                                                          op=ALU.mult), reads=["dxT", "mixc"], writes=["tmpT"])
                P.op(e1, lambda e, xi=xi: e.tensor_tensor(out=xT[i][xi][:], in0=tmpT[:], in1=cur, op=ALU.add),
                     reads=["tmpT", hk], writes=[f"xT{i}_{xi}"])
            for li, (wt, nm, width, func) in enumerate(((w1, "rwkv_w1", 64, AF.Tanh), (a1, "rwkv_a1", 64, AF.Copy), (g1, "rwkv_g1", 128, AF.Sigmoid))):
                for cc in range(8):
                    P.op("pe", lambda e, cc=cc, wt=wt, width=width, li=li: e.matmul(PL[0:width, li * 128:(li + 1) * 128], lhsT=wt[:, 0, cc, :], rhs=hb[:, cc, 2:130],
                                                                                    start=(cc == 0), stop=False),
                         reads=[hk, nm], writes=["PL"], sig=False)
                for cc in range(8):
                    P.op("pe", lambda e, cc=cc, wt=wt, width=width, li=li: e.matmul(PL[0:width, li * 128:(li + 1) * 128], lhsT=wt[:, 1, cc, :], rhs=dxb[:, cc, :],
                                                                                    start=False, stop=(cc == 7)),
                         reads=["dxb", nm], writes=["PL"], sig=(cc == 7), partial=True)
                P.op("act", lambda e, li=li, width=width, func=func: e.activation(out=l1[i][li][0:width, :], in_=PL[0:width, li * 128:(li + 1) * 128], func=func),
                     reads=["PL"], writes=[f"l1{i}_{li}"])

        def unit_x(t, hf):
            rows = slice(t * 128, (t + 1) * 128)
            u = (2 * t + hf) % 2
            cs = slice(hf * HW, (hf + 1) * HW)
            R, KR, V, WP, AL, G, KK, KM, GA, TM = TS[u]
            RT, AT, BT, KT, VB = OB[u]
            T_ = lambda n: f"T{u}_{n}"
            O_ = lambda n: f"ob{u}_{n}"
            v3 = lambda ap: ap.rearrange("p (h n) -> p h n", h=8)
            bc = lambda col: col.unsqueeze(2).to_broadcast([128, 8, HS])
            sx = s16[u]

            ti = t % 2
            kn = [0]

            def next_k():
                kn[0] += 1
                return 2 * hf + kn[0] % 2

            def proj(xi, wt, wkey):
                k = next_k()
                for cc in range(8):
                    P.op("pe", lambda e, cc=cc: e.matmul(PP[k][:, :], lhsT=xT[ti][xi][:, cc, :], rhs=wt[:, cc, cs], start=(cc == 0), stop=(cc == 7)),
                         reads=[f"xT{ti}_{xi}", wkey], writes=[f"PP{k}"], sig=(cc == 7))
                return k

            def lora2(li, width, wt2, k2key):
                k = next_k()
                P.op("pe", lambda e: e.matmul(PP[k][:, :], lhsT=l1[ti][li][0:width, :], rhs=wt2[0:width, cs], start=True, stop=True),
                     reads=[f"l1{ti}_{li}", k2key], writes=[f"PP{k}"])
                return k

            kr_ = proj(0, wr, "rwkv_w_r")
            P.op("act", lambda e: e.activation(out=R[:], in_=PP[kr_][:], func=AF.Copy), reads=[f"PP{kr_}"], writes=[T_("R")])
            kw_ = lora2(0, 64, w2, "rwkv_w2")
            P.op("dve", lambda e: e.tensor_tensor(out=WP[:], in0=PP[kw_][:], in1=w0b[:, cs], op=ALU.add), reads=[f"PP{kw_}", "rwkv_w0"], writes=[T_("WP")])
            kk_ = proj(1, wk, "rwkv_w_k")
            P.op("act", lambda e: e.activation(out=KR[:], in_=PP[kk_][:], func=AF.Copy), reads=[f"PP{kk_}"], writes=[T_("KR")])
            ka_ = lora2(1, 64, a2, "rwkv_a2")
            P.op("dve", lambda e: e.tensor_tensor(out=AL[:], in0=PP[ka_][:], in1=a0b[:, cs], op=ALU.add), reads=[f"PP{ka_}", "rwkv_a0"], writes=[T_("AL")])
            kv_ = proj(2, wv, "rwkv_w_v")
            P.op("act", lambda e: e.activation(out=V[:], in_=PP[kv_][:], func=AF.Copy), reads=[f"PP{kv_}"], writes=[T_("V")])
            P.op("act", lambda e: e.activation(out=VB[:], in_=PP[kv_][:], func=AF.Copy), reads=[f"PP{kv_}"], writes=[O_("VB")])
            kg_ = lora2(2, 128, g2, "rwkv_g2")
            P.op("act", lambda e: e.activation(out=G[:], in_=PP[kg_][:], func=AF.Copy), reads=[f"PP{kg_}"], writes=[T_("G")])
            P.dma("sp", scr["VB"][rows, cs], VB[:], reads=[O_("VB")], writes=[f"sVB{t}"], partial=True)
            P.dma("sp", scr["G"][rows, cs], G[:], reads=[T_("G")], writes=[f"sG{t}"], partial=True)
            P.op("act", lambda e: e.activation(out=AL[:], in_=AL[:], func=AF.Sigmoid), reads=[T_("AL")], writes=[T_("AL")])
            P.op("act", lambda e: e.activation(out=WP[:], in_=WP[:], func=AF.Exp, scale=-1.0), reads=[T_("WP")], writes=[T_("WP")])
            P.op("act", lambda e: e.activation(out=WP[:], in_=WP[:], func=AF.Ln, bias=1.0), reads=[T_("WP")], writes=[T_("WP")])
            P.op("act", lambda e: e.activation(out=WP[:], in_=WP[:], func=AF.Exp, scale=-1.0, bias=-0.5), reads=[T_("WP")], writes=[T_("WP")])
            return dict(t=t, hf=hf)

        def unit_y(stt):
            t, hf = stt["t"], stt["hf"]
            rows = slice(t * 128, (t + 1) * 128)
            u = (2 * t + hf) % 2
            cs = slice(hf * HW, (hf + 1) * HW)
            R, KR, V, WP, AL, G, KK, KM, GA, TM = TS[u]
            RT, AT, BT, KT, VB = OB[u]
            T_ = lambda n: f"T{u}_{n}"
            O_ = lambda n: f"ob{u}_{n}"
            v3 = lambda ap: ap.rearrange("p (h n) -> p h n", h=8)
            bc = lambda col: col.unsqueeze(2).to_broadcast([128, 8, HS])
            sx = s16[u]
            PCu = PCm[u]
            P.op("pe", lambda e: e.matmul(PCu[:, :], lhsT=trif[:], rhs=WP[:], start=True, stop=True),
                 reads=["trif", T_("WP")], writes=[f"PC{u}"])
            P.op("dve", lambda e: e.tensor_tensor(out=KK[:], in0=KR[:], in1=kkb[:, cs], op=ALU.mult), reads=[T_("KR"), "rwkv_k_k"], writes=[T_("KK")])
            P.op("act", lambda e: e.activation(out=TM[:], in_=KK[:], func=AF.Square), reads=[T_("KK")], writes=[T_("TM")])
            P.op("dve", lambda e: e.tensor_reduce(out=sx[:, 0, :], in_=v3(TM[:]), axis=AX.X, op=ALU.add), reads=[T_("TM")], writes=[f"sa{u}"])
            P.op("pool", lambda e: e.tensor_scalar(out=sx[:, 1, :], in0=sx[:, 0, :], scalar1=1e-24, scalar2=None, op0=ALU.max),
                 reads=[f"sa{u}"], writes=[f"sb{u}"])
            P.op("pool", lambda e: e.tensor_tensor(out=sx[:, 1, :], in0=sx[:, 1, :], in1=C.mhalf[:, 0:1].to_broadcast([128, 8]), op=ALU.pow),
                 reads=[f"sb{u}"], writes=[f"sb{u}"])
            P.op("pool", lambda e: e.tensor_tensor(out=v3(KK[:]), in0=v3(KK[:]), in1=bc(sx[:, 1, :]), op=ALU.mult),
                 reads=[T_("KK"), f"sb{u}"], writes=[T_("KK")])
            P.op("dve", lambda e: e.scalar_tensor_tensor(out=KM[:], in0=AL[:], scalar=-1.0, in1=kab[:, cs], op0=ALU.add, op1=ALU.mult),
                 reads=[T_("AL"), "rwkv_k_a"], writes=[T_("KM")])
            P.op("dve", lambda e: e.scalar_tensor_tensor(out=KM[:], in0=KM[:], scalar=1.0, in1=KR[:], op0=ALU.add, op1=ALU.mult),
                 reads=[T_("KM"), T_("KR")], writes=[T_("KM")])
            P.op("pool", lambda e: e.tensor_tensor(out=TM[:], in0=R[:], in1=rkb[:, cs], op=ALU.mult), reads=[T_("R"), "rwkv_r_k", f"sa{u}"], writes=[T_("TM")])
            P.op("dve", lambda e: e.tensor_tensor(out=TM[:], in0=TM[:], in1=KM[:], op=ALU.mult), reads=[T_("TM"), T_("KM")], writes=[T_("TM")])
            P.op("dve", lambda e: e.tensor_reduce(out=sx[:, 2, :], in_=v3(TM[:]), axis=AX.X, op=ALU.add), reads=[T_("TM")], writes=[f"sc{u}"])
            P.op("pool", lambda e: e.tensor_tensor(out=v3(TM[:]), in0=v3(V[:]), in1=bc(sx[:, 2, :]), op=ALU.mult),
                 reads=[T_("V"), f"sc{u}"], writes=[T_("TM")])
            P.dma("sp", scr["BON"][rows, cs], TM[:], reads=[T_("TM")], writes=[f"sBON{t}"], partial=True)
            P.op("act", lambda e: e.activation(out=GA[:], in_=PCu[:], func=AF.Exp, scale=-1.0), reads=[f"PC{u}"], writes=[T_("GA")])
            P.dma("sp", scr["GC"][t:t + 1, cs], GA[127:128, :], reads=[T_("GA")], writes=[f"sGC{t}"], partial=True)
            P.op("pool", lambda e: e.tensor_tensor(out=RT[:], in0=R[:], in1=GA[:], op=ALU.mult), reads=[T_("R"), T_("GA")], writes=[O_("RT")])
            P.dma("sp", scr["RT"][rows, cs], RT[:], reads=[O_("RT")], writes=[f"sRT{t}"], partial=True)
            P.op("act", lambda e: e.activation(out=GA[:], in_=PCu[:], func=AF.Exp), reads=[f"PC{u}", O_("RT")], writes=[T_("GA")])
            P.op("dve", lambda e: e.tensor_tensor(out=KT[:], in0=KM[:], in1=GA[:], op=ALU.mult), reads=[T_("KM"), T_("GA")], writes=[O_("KT")])
            P.dma("sp", scr["KT"][rows, cs], KT[:], reads=[O_("KT")], writes=[f"sKT{t}"], partial=True)
            P.op("pool", lambda e: e.tensor_tensor(out=KM[:], in0=KK[:], in1=AL[:], op=ALU.mult), reads=[T_("KK"), T_("AL"), O_("KT")], writes=[T_("KM")])
            P.op("dve", lambda e: e.tensor_tensor(out=BT[:], in0=KM[:], in1=GA[:], op=ALU.mult), reads=[T_("KM"), T_("GA")], writes=[O_("BT")])
            P.dma("sp", scr["BT"][rows, cs], BT[:], reads=[O_("BT")], writes=[f"sBT{t}"], partial=True)
            P.op("dve", lambda e: e.tensor_tensor(out=G[:], in0=PCu[:], in1=WP[:], op=ALU.subtract),
                 reads=[f"PC{u}", T_("WP"), f"sG{t}"], writes=[T_("G")])
            P.op("act", lambda e: e.activation(out=G[:], in_=G[:], func=AF.Exp, scale=-1.0), reads=[T_("G")], writes=[T_("G")])
            P.op("dve", lambda e: e.scalar_tensor_tensor(out=AT[:], in0=KK[:], scalar=-1.0, in1=G[:], op0=ALU.mult, op1=ALU.mult),
                 reads=[T_("KK"), T_("G")], writes=[O_("AT")])
            P.dma("sp", scr["AT"][rows, cs], AT[:], reads=[O_("AT")], writes=[f"sAT{t}"], partial=True)

        do_tile(0)
        for t in range(NT):
            streams = []
            for hf in range(2):
                P.begin_capture()
                unit_y(unit_x(t, hf))
                streams.append(P.end_capture())
            if t + 1 < NT:
                P.begin_capture()
                do_tile(t + 1)
                streams.append(P.end_capture())
            P.replay(streams)
        P.emit()


def rwkv_pass_b_v2(C, S, h, hkey, W, scr):
    nc, P = C.nc, C.P
    NT = S // 128
    with contextlib.ExitStack() as st:
        sb, ps = alloc(st, nc)
        wo = sb("wo", [128, 8, D], BF16)
        gnw = sb("gnw", [128, D], F32)
        gnb = sb("gnb", [128, D], F32)
        mk1 = sb("mk1", [128, 256], F32)
        mksl = sb("mksl", [128, 128], F32)
        inb = [[sb(f"in{n}_{i}", [128, D], BF16) for n in range(5)] for i in range(2)]
        gb = [sb(f"gb{i}", [128, D], F32) for i in range(2)]
        bon = [sb(f"bon{i}", [128, D], F32) for i in range(2)]
        gc = [sb(f"gc{i}", [128, 8], F32) for i in range(2)]
        hres = [sb(f"hres{i}", [128, D], F32) for i in range(2)]
        ARt = sb("ARt", [128, 8, 2, 128], BF16)
        BtT = sb("BtT", [128, 8, 128], BF16)
        KtT = sb("KtT", [128, 8, 128], BF16)
        AB1 = sb("AB1", [128, NH, 256], BF16)
        AK1 = sb("AK1", [128, NH, 256], BF16)
        M0 = sb("M0", [128, 8, 128], BF16)
        Mb = [sb(f"Mb{i}", [128, 8, 128], BF16) for i in range(2)]
        MTb = [sb(f"MTb{i}", [128, 8, 128], BF16) for i in range(2)]
        X32 = sb("X32", [128, NH, 128], F32)
        Xb = sb("Xb", [128, 8, 128], BF16)
        W1c = sb("W1c", [128, NH, 64], BF16)
        W1T = sb("W1T", [128, 8, 128], BF16)
        S32 = sb("S32", [128, 8, 64], F32)
        Sb = sb("Sb", [128, 8, 64], BF16)
        Ub = sb("Ub", [128, D], BF16)
        Ysb = sb("Ysb", [128, D], F32)
        Ysq = sb("Ysq", [128, D], F32)
        s16 = sb("s16", [128, 6, 16], F32)
        otm = sb("otm", [128, D], BF16)
        oT = sb("oT", [128, 8, 128], BF16)
        hout = [sb(f"hout{i}", [128, D], F32) for i in range(2)]
        BK = [ps(f"BK{i}", [128, 512], F32) for i in range(8)]
        bkb = lambda b: BK[b][:, :].bitcast(BF16)

        for q in range(2):
            P.dma("pool", wo[:, :, q * 512:(q + 1) * 512], W["rwkv_w_o"][:, q * 512:(q + 1) * 512].rearrange("(c p) n -> p c n", p=128),
                  writes=["wo"], partial=(q > 0))
        bcast_load(C, gnw[:], W["rwkv_gn_w"], "gnw")
        bcast_load(C, gnb[:], W["rwkv_gn_b"], "gnb")
        P.op("pool", lambda e: e.memset(mk1[:], 1.0), writes=["mk1"])
        P.op("pool", lambda e: e.affine_select(out=mk1[:, 0:128], in_=mk1[:, 0:128], pattern=[[1, 128]], compare_op=ALU.is_gt, fill=0.0,
                                               base=0, channel_multiplier=-1), reads=["mk1"], writes=["mk1"])
        P.op("pool", lambda e: e.affine_select(out=mk1[:, 128:256], in_=mk1[:, 128:256], pattern=[[1, 128]], compare_op=ALU.is_ge, fill=0.0,
                                               base=0, channel_multiplier=-1), reads=["mk1"], writes=["mk1"])
        P.op("pool", lambda e: e.memset(mksl[:], 1.0), writes=["mksl"])
        P.op("pool", lambda e: e.affine_select(out=mksl[:], in_=mksl[:], pattern=[[-1, 128]], compare_op=ALU.is_gt, fill=0.0,
                                               base=0, channel_multiplier=1), reads=["mksl"], writes=["mksl"])
        P.op("pool", lambda e: e.memset(S32[:], 0.0), writes=["S32"])
        P.op("pool", lambda e: e.memset(Sb[:], 0.0), writes=["Sb"])

        v3 = lambda ap: ap.rearrange("p (h n) -> p h n", h=NH)
        bc = lambda col: col.unsqueeze(2).to_broadcast([128, NH, HS])
        names = ["RT", "AT", "BT", "KT", "VB"]
        evn = [0]

        def evac_copy(out, in_, reads, writes):
            e1 = "act" if evn[0] % 2 == 0 else "dve"
            evn[0] += 1
            if e1 == "act":
                P.op("act", lambda e: e.activation(out=out, in_=in_, func=AF.Copy), reads=reads, writes=writes)
            else:
                P.op("dve", lambda e: e.tensor_copy(out=out, in_=in_), reads=reads, writes=writes)

        def load(t):
            i = t % 2
            rows = slice(t * 128, (t + 1) * 128)
            for n in range(5):
                P.dma("act" if n % 2 else "sp", inb[i][n][:], scr[names[n]][rows, :], reads=[f"s{names[n]}{t}"], writes=[f"in{n}_{i}"])
            P.dma("sp", gb[i][:], scr["G"][rows, :], reads=[f"sG{t}"], writes=[f"gb{i}"])
            P.dma("act", bon[i][:], scr["BON"][rows, :], reads=[f"sBON{t}"], writes=[f"bon{i}"])
            P.dma("sp", gc[i][:], scr["GC"][t].rearrange("(c p) -> p c", p=128), reads=[f"sGC{t}"], writes=[f"gc{i}"],
                  allow_slow_non_contiguous=True)
            P.dma("act", hres[i][:], h[rows, :], reads=[f"{hkey}{t}"], writes=[f"hres{i}"])

        load(0)

        def do_tile(t):
            i = t % 2
            rows = slice(t * 128, (t + 1) * 128)
            if t + 1 < NT:
                load(t + 1)
            RT, AT, BT, KT, VB = inb[i]
            kin = [f"in{n}_{i}" for n in range(5)]
            for n, (src_t, dst_ap, dkey) in enumerate(((AT, ARt[:, :, 0, :], "ARt"), (RT, ARt[:, :, 1, :], "ARt"),
                                                       (BT, BtT[:, :, :], "BtT"), (KT, KtT[:, :, :], "KtT"))):
                b = 6 + n % 2
                for cc in range(8):
                    P.op("pe", lambda e, cc=cc, b=b, src_t=src_t: e.transpose(out=bkb(b)[:, cc * 128:(cc + 1) * 128],
                                                                             in_=src_t[:, cc * 128:(cc + 1) * 128], identity=C.ident_b[:]),
                         reads=[kin[[1, 0, 2, 3][n]], "ident_b"], writes=[f"bk{b}"], sig=(cc == 7), partial=(cc > 0))
                evac_copy(dst_ap, bkb(b).rearrange("p (c j) -> p c j", c=8), [f"bk{b}"], [dkey] if n != 1 else ["ARt"])
            for rd in range(2):
                for cl in range(4):
                    c = rd * 4 + cl
                    base = (cl % 2) * 4
                    for hh in range(2):
                        h_ = 2 * c + hh
                        hl = h_ - rd * 8
                        p0 = 64 * hh
                        ps_ = slice(p0, p0 + 64)
                        b1 = base + 2 * hh
                        b2 = base + 2 * hh + 1
                        P.op("pe", lambda e, c=c, ps_=ps_, b1=b1: e.matmul(BK[b1][:, 0:256], lhsT=BtT[ps_, c, :],
                                                                          rhs=ARt[ps_, c, :, :], start=True, stop=True),
                             reads=["BtT", "ARt"], writes=[f"bk{b1}"], sig=False)
                        P.op("pe", lambda e, c=c, ps_=ps_, b1=b1: e.matmul(BK[b1][:, 256:384], lhsT=ARt[ps_, c, 0, :],
                                                                          rhs=BtT[ps_, c, :], start=True, stop=True),
                             reads=["BtT", "ARt"], writes=[f"bk{b1}"], partial=True)
                        P.op("pe", lambda e, c=c, ps_=ps_, b2=b2: e.matmul(BK[b2][:, 0:256], lhsT=KtT[ps_, c, :],
                                                                          rhs=ARt[ps_, c, :, :], start=True, stop=True),
                             reads=["KtT", "ARt"], writes=[f"bk{b2}"])
                        P.op("dve", lambda e, h_=h_, b1=b1: e.tensor_tensor(out=AB1[:, h_, :], in0=BK[b1][:, 0:256], in1=mk1[:], op=ALU.mult),
                             reads=[f"bk{b1}", "mk1"], writes=["AB1"])
                        P.op("dve", lambda e, hl=hl, b1=b1: e.tensor_tensor(out=M0[:, hl, :], in0=BK[b1][:, 256:384], in1=mksl[:], op=ALU.mult),
                             reads=[f"bk{b1}", "mksl"], writes=["M0"])
                        P.op("dve", lambda e, h_=h_, b2=b2: e.tensor_tensor(out=AK1[:, h_, :], in0=BK[b2][:, 0:256], in1=mk1[:], op=ALU.mult),
                             reads=[f"bk{b2}", "mk1"], writes=["AK1"])
                hs_ = slice(rd * 8, rd * 8 + 8)
                P.op("pool", lambda e, hs_=hs_, rd=rd: e.tensor_copy(out=X32[:, hs_, 0:64],
                                                                    in_=AT[:, rd * 512:(rd + 1) * 512].rearrange("p (h n) -> p h n", h=8)),
                     reads=[kin[1]], writes=["X32_0", "X32_1"])
                for hl in range(8):
                    h_ = rd * 8 + hl
                    bX = 3 * (hl // 4)
                    P.op("pe", lambda e, hl=hl, h_=h_, bX=bX: e.matmul(BK[bX][:, (hl % 4) * 128 + 64:(hl % 4) * 128 + 128], lhsT=AK1[:, h_, 0:128],
                                                                        rhs=VB[:, h_ * 64:(h_ + 1) * 64], start=True, stop=True),
                         reads=["AK1", kin[4]], writes=[f"bk{bX}"], sig=(hl % 4 == 3), partial=(hl % 4 > 0))
                for sq in range(2):
                    hsq = slice(rd * 8 + sq * 4, rd * 8 + sq * 4 + 4)
                    P.op("act", lambda e, sq=sq, hsq=hsq: e.activation(out=X32[:, hsq, 64:128],
                                                                       in_=BK[3 * sq][:, :].rearrange("p (h n) -> p h n", h=4)[:, :, 64:128], func=AF.Copy),
                         reads=[f"bk{3 * sq}", f"X32_{sq}"], writes=[f"X32_{sq}"])
                    P.op("act", lambda e, sq=sq, hsq=hsq: e.activation(out=Xb[:, sq * 4:sq * 4 + 4, :], in_=X32[:, hsq, :], func=AF.Copy),
                         reads=[f"X32_{sq}"], writes=[f"Xb{sq}"])
                for L in range(7):
                    pp = L % 2
                    for sq in range(2):
                        bX, bM, bMT = 3 * sq, 3 * sq + 1, 3 * sq + 2
                        hsq = slice(rd * 8 + sq * 4, rd * 8 + sq * 4 + 4)
                        mk_ = [f"MTb{pp}_{sq}", f"Mb{pp}_{sq}"]
                        for hq in range(4):
                            hl = sq * 4 + hq
                            h_ = rd * 8 + hl
                            mt = AB1[:, h_, 0:128] if L == 0 else MTb[pp][:, hl, :]
                            P.op("pe", lambda e, hl=hl, hq=hq, mt=mt, bX=bX: e.matmul(BK[bX][:, hq * 128:(hq + 1) * 128], lhsT=mt, rhs=Xb[:, hl, :],
                                                                                      start=True, stop=True),
                                 reads=["AB1" if L == 0 else mk_[0], f"Xb{sq}"], writes=[f"bk{bX}"], sig=(hq == 3), partial=(hq > 0))
                        if L < 6:
                            for hq in range(4):
                                hl = sq * 4 + hq
                                h_ = rd * 8 + hl
                                mt = AB1[:, h_, 0:128] if L == 0 else MTb[pp][:, hl, :]
                                m = M0[:, hl, :] if L == 0 else Mb[pp][:, hl, :]
                                rk = ["AB1", "M0"] if L == 0 else mk_
                                P.op("pe", lambda e, hq=hq, mt=mt, m=m, bM=bM: e.matmul(BK[bM][:, hq * 128:(hq + 1) * 128], lhsT=mt, rhs=m,
                                                                                        start=True, stop=True),
                                     reads=rk, writes=[f"bk{bM}"], sig=(hq == 3), partial=(hq > 0))
                                P.op("pe", lambda e, hq=hq, mt=mt, m=m, bMT=bMT: e.matmul(BK[bMT][:, hq * 128:(hq + 1) * 128], lhsT=m, rhs=mt,
                                                                                          start=True, stop=True),
                                     reads=rk, writes=[f"bk{bMT}"], sig=(hq == 3), partial=(hq > 0))
                        P.op("dve", lambda e, hsq=hsq, bX=bX: e.tensor_tensor(out=X32[:, hsq, :], in0=BK[bX][:, :].rearrange("p (h n) -> p h n", h=4),
                                                                             in1=X32[:, hsq, :], op=ALU.add),
                             reads=[f"bk{bX}", f"X32_{sq}"], writes=[f"X32_{sq}"])
                        if L < 6:
                            P.op("act", lambda e, sq=sq, hsq=hsq: e.activation(out=Xb[:, sq * 4:sq * 4 + 4, :], in_=X32[:, hsq, :], func=AF.Copy),
                                 reads=[f"X32_{sq}"], writes=[f"Xb{sq}"])
                            P.op("act", lambda e, sq=sq, pp=pp, bM=bM: e.activation(out=Mb[1 - pp][:, sq * 4:sq * 4 + 4, :],
                                                                                    in_=BK[bM][:, :].rearrange("p (h n) -> p h n", h=4), func=AF.Copy),
                                 reads=[f"bk{bM}"], writes=[f"Mb{1 - pp}_{sq}"])
                            P.op("dve", lambda e, sq=sq, pp=pp, bMT=bMT: e.tensor_copy(out=MTb[1 - pp][:, sq * 4:sq * 4 + 4, :],
                                                                                       in_=BK[bMT][:, :].rearrange("p (h n) -> p h n", h=4)),
                                 reads=[f"bk{bMT}"], writes=[f"MTb{1 - pp}_{sq}"])
                P.op("pool", lambda e, hs_=hs_: e.tensor_copy(out=W1c[:, hs_, :], in_=X32[:, hs_, 0:64]), reads=["X32_0", "X32_1"], writes=["W1c"], partial=(rd > 0))
            for cc in range(8):
                P.op("pe", lambda e, cc=cc: e.transpose(out=bkb(6)[:, cc * 128:(cc + 1) * 128],
                                                        in_=W1c[:, 2 * cc:2 * cc + 2, :].rearrange("p h n -> p (h n)"), identity=C.ident_b[:]),
                     reads=["W1c", "ident_b"], writes=["bk6"], sig=(cc == 7), partial=(cc > 0))
            evac_copy(W1T[:, :, :], bkb(6).rearrange("p (c j) -> p c j", c=8), ["bk6"], ["W1T"])
            for h_ in range(NH):
                c, p0 = h_ // 2, 64 * (h_ % 2)
                P.op("pe", lambda e, h_=h_, c=c, p0=p0: e.matmul(BK[h_ % 2][:, c * 64:(c + 1) * 64], lhsT=W1T[p0:p0 + 64, c, :],
                                                                  rhs=Sb[p0:p0 + 64, c, :], start=True, stop=True),
                     reads=["W1T", "Sb"], writes=[f"bk{h_ % 2}"], sig=(h_ >= 14), partial=(h_ >= 2))
            for bq in range(2):
                P.op("dve", lambda e, bq=bq: e.tensor_tensor(out=Ub[:, :].rearrange("p (c two n) -> p c two n", two=2, n=64)[:, :, bq, :],
                                                             in0=BK[bq][:, :].rearrange("p (h n) -> p h n", h=8),
                                                             in1=X32[:, :, :].rearrange("p (c two) n -> p c two n", two=2)[:, :, bq, 64:128], op=ALU.add),
                     reads=[f"bk{bq}", "X32_0", "X32_1"], writes=["Ub"], partial=(bq > 0))
            for h_ in range(NH):
                c, p0 = h_ // 2, 64 * (h_ % 2)
                ob_ = BK[2 + h_ % 2][:, c * 64:(c + 1) * 64]
                P.op("pe", lambda e, c=c, p0=p0, ob_=ob_: e.matmul(ob_, lhsT=ARt[p0:p0 + 64, c, 1, :], rhs=Sb[p0:p0 + 64, c, :], start=True, stop=False),
                     reads=["ARt", "Sb"], writes=[f"bk{2 + h_ % 2}"], sig=False, partial=(h_ >= 2))
                P.op("pe", lambda e, h_=h_, ob_=ob_: e.matmul(ob_, lhsT=AB1[:, h_, 128:256], rhs=Ub[:, h_ * 64:(h_ + 1) * 64], start=False, stop=False),
                     reads=["AB1", "Ub"], writes=[f"bk{2 + h_ % 2}"], sig=False, partial=True)
                P.op("pe", lambda e, h_=h_, ob_=ob_: e.matmul(ob_, lhsT=AK1[:, h_, 128:256], rhs=VB[:, h_ * 64:(h_ + 1) * 64], start=False, stop=True),
                     reads=["AK1", kin[4]], writes=[f"bk{2 + h_ % 2}"], sig=(h_ >= 14), partial=True)
            for c in range(8):
                ob_ = BK[4 + c // 4][:, (c % 4) * 128:(c % 4 + 1) * 128]
                P.op("pe", lambda e, c=c, ob_=ob_: e.matmul(ob_, lhsT=BT[:, c * 128:(c + 1) * 128], rhs=Ub[:, c * 128:(c + 1) * 128], start=True, stop=False),
                     reads=[kin[2], "Ub"], writes=[f"bk{4 + c // 4}"], sig=False, partial=(c % 4 > 0))
                P.op("pe", lambda e, c=c, ob_=ob_: e.matmul(ob_, lhsT=KT[:, c * 128:(c + 1) * 128], rhs=VB[:, c * 128:(c + 1) * 128], start=False, stop=True),
                     reads=[kin[3], kin[4]], writes=[f"bk{4 + c // 4}"], sig=(c % 4 == 3), partial=True)
            for bq in range(2):
                for hh in range(2):
                    ps_ = slice(64 * hh, 64 * hh + 64)
                    P.op("dve", lambda e, bq=bq, hh=hh, ps_=ps_: e.tensor_tensor(
                        out=S32[ps_, bq * 4:bq * 4 + 4, :], in0=BK[4 + bq][ps_, :].rearrange("p (c n) -> p c n", c=4)[:, :, hh * 64:(hh + 1) * 64],
                        in1=S32[ps_, bq * 4:bq * 4 + 4, :], op=ALU.add),
                         reads=[f"bk{4 + bq}", "S32", "Sb"], writes=["S32"], partial=True)
            P.op("dve", lambda e, i=i: e.tensor_tensor(out=S32[:], in0=S32[:], in1=gc[i][:, :].unsqueeze(2).to_broadcast([128, 8, 64]), op=ALU.mult),
                 reads=["S32", f"gc{i}"], writes=["S32"])
            P.op("pool", lambda e: e.tensor_copy(out=Sb[:], in_=S32[:]), reads=["S32"], writes=["Sb"])
            for bq in range(2):
                yv = lambda ap, bq=bq: ap.rearrange("p (c two n) -> p c two n", two=2, n=64)[:, :, bq, :]
                P.op("act", lambda e, bq=bq, yv=yv: e.activation(out=yv(Ysb[:, :]), in_=BK[2 + bq][:, :].rearrange("p (c n) -> p c n", c=8), func=AF.Copy),
                     reads=[f"bk{2 + bq}"], writes=["Ysb"], partial=(bq > 0))
                P.op("act", lambda e, bq=bq, yv=yv: e.activation(out=yv(Ysq[:, :]), in_=BK[2 + bq][:, :].rearrange("p (c n) -> p c n", c=8), func=AF.Square),
                     reads=[f"bk{2 + bq}"], writes=["Ysq"], partial=(bq > 0))
            P.op("dve", lambda e: e.tensor_reduce(out=s16[:, 0, :], in_=v3(Ysb[:]), axis=AX.X, op=ALU.add), reads=["Ysb"], writes=["q0"])
            P.op("dve", lambda e: e.tensor_reduce(out=s16[:, 1, :], in_=v3(Ysq[:]), axis=AX.X, op=ALU.add), reads=["Ysq"], writes=["q1"])
            P.op("pool", lambda e: e.tensor_scalar(out=s16[:, 2, :], in0=s16[:, 0, :], scalar1=1.0 / HS, scalar2=None, op0=ALU.mult),
                 reads=["q0"], writes=["q2"])
            P.op("pool", lambda e: e.tensor_tensor(out=s16[:, 3, :], in0=s16[:, 2, :], in1=s16[:, 2, :], op=ALU.mult), reads=["q2"], writes=["q3"])
            P.op("dve", lambda e: e.scalar_tensor_tensor(out=s16[:, 4, :], in0=s16[:, 1, :], scalar=1.0 / HS, in1=s16[:, 3, :],
                                                         op0=ALU.mult, op1=ALU.subtract), reads=["q1", "q3"], writes=["q4"])
            P.op("pool", lambda e: e.tensor_scalar(out=s16[:, 4, :], in0=s16[:, 4, :], scalar1=GN_EPS, scalar2=None, op0=ALU.add),
                 reads=["q4"], writes=["q4"])
            P.op("pool", lambda e: e.tensor_tensor(out=s16[:, 5, :], in0=s16[:, 4, :], in1=C.mhalf[:, 0:1].to_broadcast([128, NH]), op=ALU.pow),
                 reads=["q4"], writes=["q5"])
            P.op("dve", lambda e: e.tensor_tensor(out=v3(Ysb[:]), in0=v3(Ysb[:]), in1=bc(s16[:, 2, :]), op=ALU.subtract),
                 reads=["Ysb", "q2"], writes=["Ysb"])
            P.op("pool", lambda e: e.tensor_tensor(out=v3(Ysb[:]), in0=v3(Ysb[:]), in1=bc(s16[:, 5, :]), op=ALU.mult),
                 reads=["Ysb", "q5"], writes=["Ysb"])
            P.op("dve", lambda e: e.tensor_tensor(out=Ysb[:], in0=Ysb[:], in1=gnw[:], op=ALU.mult), reads=["Ysb", "gnw"], writes=["Ysb"])
            P.op("pool", lambda e: e.tensor_tensor(out=Ysb[:], in0=Ysb[:], in1=gnb[:], op=ALU.add), reads=["Ysb", "gnb"], writes=["Ysb"])
            P.op("dve", lambda e, i=i: e.tensor_tensor(out=Ysb[:], in0=Ysb[:], in1=bon[i][:], op=ALU.add), reads=["Ysb", f"bon{i}"], writes=["Ysb"])
            P.op("pool", lambda e, i=i: e.tensor_tensor(out=otm[:], in0=Ysb[:], in1=gb[i][:], op=ALU.mult), reads=["Ysb", f"gb{i}"], writes=["otm"])
            for cc in range(8):
                P.op("pe", lambda e, cc=cc: e.transpose(out=bkb(7)[:, cc * 128:(cc + 1) * 128], in_=otm[:, cc * 128:(cc + 1) * 128],
                                                        identity=C.ident_b[:]),
                     reads=["otm", "ident_b"], writes=["bk7"], sig=(cc == 7), partial=(cc > 0))
            evac_copy(oT[:, :, :], bkb(7).rearrange("p (c j) -> p c j", c=8), ["bk7"], ["oT"])
            for hf in range(2):
                for cc in range(8):
                    P.op("pe", lambda e, cc=cc, hf=hf: e.matmul(BK[hf][:, :], lhsT=oT[:, cc, :], rhs=wo[:, cc, hf * 512:(hf + 1) * 512],
                                                                start=(cc == 0), stop=(cc == 7)),
                         reads=["oT", "wo"], writes=[f"bk{hf}"], sig=(cc == 7))
                P.op("dve", lambda e, hf=hf, i=i: e.tensor_tensor(out=hout[i][:, hf * 512:(hf + 1) * 512], in0=BK[hf][:, :],
                                                                  in1=hres[i][:, hf * 512:(hf + 1) * 512], op=ALU.add),
                     reads=[f"bk{hf}", f"hres{i}"], writes=[f"hout{i}"], partial=(hf > 0))
            P.dma("sp", h[rows, :], hout[i][:], reads=[f"hout{i}"], writes=[f"{hkey}{t}"])

        for t in range(NT):
            do_tile(t)
        P.emit()


def rwkv_pass_b(C, S, h, hkey, W, scr):
    nc, P = C.nc, C.P
    NT = S // 128
    with contextlib.ExitStack() as st:
        sb, ps = alloc(st, nc)
        wo = sb("wo", [128, 8, D], BF16)
        gnw = sb("gnw", [128, D], F32)
        gnb = sb("gnb", [128, D], F32)
        mk1 = sb("mk1", [128, 256], F32)
        mksl = sb("mksl", [128, 128], F32)
        inb = [[sb(f"in{n}_{i}", [128, D], BF16) for n in range(5)] for i in range(2)]
        gb = sb("gb", [128, D], F32)
        bon = sb("bon", [128, D], F32)
        gc = [sb(f"gc{i}", [128, 8], F32) for i in range(2)]
        hres = sb("hres", [128, D], F32)
        ARt = [sb(f"ARt{i}", [128, 8, 2, 128], BF16) for i in range(2)]
        BtT = sb("BtT", [128, 8, 128], BF16)
        KtT = sb("KtT", [128, 8, 128], BF16)
        AB1 = [sb(f"AB1{i}", [128, NH, 256], BF16) for i in range(2)]
        AK1 = [sb(f"AK1{i}", [128, NH, 256], BF16) for i in range(2)]
        M0 = sb("M0", [128, 8, 128], BF16)
        Mb = [sb(f"Mb{i}", [128, 8, 128], BF16) for i in range(2)]
        MTb = [sb(f"MTb{i}", [128, 8, 128], BF16) for i in range(2)]
        X32 = [sb(f"X32{i}", [128, NH, 128], F32) for i in range(2)]
        Xb = sb("Xb", [128, 8, 128], BF16)
        W1c = sb("W1c", [128, NH, 64], BF16)
        W1T = [sb(f"W1T{i}", [128, 8, 128], BF16) for i in range(2)]
        S32 = sb("S32", [128, 8, 64], F32)
        Sb = sb("Sb", [128, 8, 64], BF16)
        Ub = sb("Ub", [128, D], BF16)
        Ysb = sb("Ysb", [128, D], F32)
        Ysq = sb("Ysq", [128, D], F32)
        s16 = sb("s16", [128, 6, 16], F32)
        otm = sb("otm", [128, D], BF16)
        oT = sb("oT", [128, 8, 128], BF16)
        BK = [ps(f"BK{i}", [128, 512], F32) for i in range(8)]
        bkb = lambda b: BK[b][:, :].bitcast(BF16)

        for q in range(2):
            P.dma("pool", wo[:, :, q * 512:(q + 1) * 512], W["rwkv_w_o"][:, q * 512:(q + 1) * 512].rearrange("(c p) n -> p c n", p=128),
                  writes=["wo"], partial=(q > 0))
        bcast_load(C, gnw[:], W["rwkv_gn_w"], "gnw")
        bcast_load(C, gnb[:], W["rwkv_gn_b"], "gnb")
        P.op("pool", lambda e: e.memset(mk1[:], 1.0), writes=["mk1"])
        P.op("pool", lambda e: e.affine_select(out=mk1[:, 0:128], in_=mk1[:, 0:128], pattern=[[1, 128]], compare_op=ALU.is_gt, fill=0.0,
                                               base=0, channel_multiplier=-1), reads=["mk1"], writes=["mk1"])
        P.op("pool", lambda e: e.affine_select(out=mk1[:, 128:256], in_=mk1[:, 128:256], pattern=[[1, 128]], compare_op=ALU.is_ge, fill=0.0,
                                               base=0, channel_multiplier=-1), reads=["mk1"], writes=["mk1"])
        P.op("pool", lambda e: e.memset(mksl[:], 1.0), writes=["mksl"])
        P.op("pool", lambda e: e.affine_select(out=mksl[:], in_=mksl[:], pattern=[[-1, 128]], compare_op=ALU.is_gt, fill=0.0,
                                               base=0, channel_multiplier=1), reads=["mksl"], writes=["mksl"])
        P.op("pool", lambda e: e.memset(S32[:], 0.0), writes=["S32"])
        P.op("pool", lambda e: e.memset(Sb[:], 0.0), writes=["Sb"])

        v3 = lambda ap: ap.rearrange("p (h n) -> p h n", h=NH)
        bc = lambda col: col.unsqueeze(2).to_broadcast([128, NH, HS])
        names = ["RT", "AT", "BT", "KT", "VB"]

        def head(t):
            i = t % 2
            rows = slice(t * 128, (t + 1) * 128)
            RT, AT, BT, KT, VB = inb[i]
            kin = [f"in{n}_{i}" for n in range(5)]
            ARt_, AB1_, AK1_, X32_, W1T_ = ARt[i], AB1[i], AK1[i], X32[i], W1T[i]
            kA, kAB, kAK, kW = f"ARt{i}", f"AB1{i}", f"AK1{i}", f"W1T{i}"
            kX = lambda sq: f"X32{i}_{sq}"
            for n in range(5):
                P.dma("act" if n % 2 else "sp", inb[i][n][:], scr[names[n]][rows, :], reads=[f"s{names[n]}{t}"], writes=[kin[n]])
            P.dma("sp", gc[i][:], scr["GC"][t].rearrange("(c p) -> p c", p=128), reads=[f"sGC{t}"], writes=[f"gc{i}"],
                  allow_slow_non_contiguous=True)
            for n, (src_t, dst_ap, dkey, skey) in enumerate(((AT, ARt_[:, :, 0, :], kA, kin[1]), (RT, ARt_[:, :, 1, :], kA, kin[0]),
                                                             (BT, BtT[:, :, :], "BtT", kin[2]), (KT, KtT[:, :, :], "KtT", kin[3]))):
                b = 4 + n % 2
                for cc in range(8):
                    P.op("pe", lambda e, cc=cc, b=b, src_t=src_t: e.transpose(out=bkb(b)[:, cc * 128:(cc + 1) * 128],
                                                                             in_=src_t[:, cc * 128:(cc + 1) * 128], identity=C.ident_b[:]),
                         reads=[skey, "ident_b"], writes=[f"bk{b}"], sig=(cc == 7), partial=(cc > 0))
                if n % 2 == 0:
                    P.op("act", lambda e, dst_ap=dst_ap, b=b: e.activation(out=dst_ap, in_=bkb(b).rearrange("p (c j) -> p c j", c=8), func=AF.Copy),
                         reads=[f"bk{b}"], writes=[dkey], partial=(n == 1))
                else:
                    P.op("dve", lambda e, dst_ap=dst_ap, b=b: e.tensor_copy(out=dst_ap, in_=bkb(b).rearrange("p (c j) -> p c j", c=8)),
                         reads=[f"bk{b}"], writes=[dkey], partial=(n == 1))
            for rd in range(2):
                for cl in range(4):
                    c = rd * 4 + cl
                    for hh in range(2):
                        h_ = 2 * c + hh
                        hl = h_ - rd * 8
                        ps_ = slice(64 * hh, 64 * hh + 64)
                        b1 = 2 * (h_ % 3)
                        b2 = b1 + 1
                        P.op("pe", lambda e, c=c, ps_=ps_, b1=b1: e.matmul(BK[b1][:, 0:256], lhsT=BtT[ps_, c, :], rhs=ARt_[ps_, c, :, :], start=True, stop=True),
                             reads=["BtT", kA], writes=[f"bk{b1}"], sig=False)
                        P.op("pe", lambda e, c=c, ps_=ps_, b1=b1: e.matmul(BK[b1][:, 256:384], lhsT=ARt_[ps_, c, 0, :], rhs=BtT[ps_, c, :], start=True, stop=True),
                             reads=["BtT", kA], writes=[f"bk{b1}"], partial=True)
                        P.op("pe", lambda e, c=c, ps_=ps_, b2=b2: e.matmul(BK[b2][:, 0:256], lhsT=KtT[ps_, c, :], rhs=ARt_[ps_, c, :, :], start=True, stop=True),
                             reads=["KtT", kA], writes=[f"bk{b2}"])
                        P.op("dve", lambda e, h_=h_, b1=b1: e.tensor_tensor(out=AB1_[:, h_, :], in0=BK[b1][:, 0:256], in1=mk1[:], op=ALU.mult),
                             reads=[f"bk{b1}", "mk1"], writes=[kAB], partial=True)
                        P.op("dve", lambda e, hl=hl, b1=b1: e.tensor_tensor(out=M0[:, hl, :], in0=BK[b1][:, 256:384], in1=mksl[:], op=ALU.mult),
                             reads=[f"bk{b1}", "mksl"], writes=["M0"], partial=True)
                        P.op("dve", lambda e, h_=h_, b2=b2: e.tensor_tensor(out=AK1_[:, h_, :], in0=BK[b2][:, 0:256], in1=mk1[:], op=ALU.mult),
                             reads=[f"bk{b2}", "mk1"], writes=[kAK], partial=True)
                hs_ = slice(rd * 8, rd * 8 + 8)
                P.op("pool", lambda e, hs_=hs_, rd=rd: e.tensor_copy(out=X32_[:, hs_, 0:64],
                                                                    in_=AT[:, rd * 512:(rd + 1) * 512].rearrange("p (h n) -> p h n", h=8)),
                     reads=[kin[1]], writes=[kX(0), kX(1)])
                for hl in range(8):
                    h_ = rd * 8 + hl
                    bX = 3 * (hl // 4)
                    P.op("pe", lambda e, hl=hl, h_=h_, bX=bX: e.matmul(BK[bX][:, (hl % 4) * 128 + 64:(hl % 4) * 128 + 128], lhsT=AK1_[:, h_, 0:128],
                                                                        rhs=VB[:, h_ * 64:(h_ + 1) * 64], start=True, stop=True),
                         reads=[kAK, kin[4]], writes=[f"bk{bX}"], sig=(hl % 4 == 3), partial=(hl % 4 > 0))
                for sq in range(2):
                    hsq = slice(rd * 8 + sq * 4, rd * 8 + sq * 4 + 4)
                    P.op("act", lambda e, sq=sq, hsq=hsq: e.activation(out=X32_[:, hsq, 64:128],
                                                                       in_=BK[3 * sq][:, :].rearrange("p (h n) -> p h n", h=4)[:, :, 64:128], func=AF.Copy),
                         reads=[f"bk{3 * sq}", kX(sq)], writes=[kX(sq)])
                    P.op("act", lambda e, sq=sq, hsq=hsq: e.activation(out=Xb[:, sq * 4:sq * 4 + 4, :], in_=X32_[:, hsq, :], func=AF.Copy),
                         reads=[kX(sq)], writes=[f"Xb{sq}"])
                for L in range(7):
                    pp = L % 2
                    for sq in range(2):
                        bX, bM, bMT = 3 * sq, 3 * sq + 1, 3 * sq + 2
                        hsq = slice(rd * 8 + sq * 4, rd * 8 + sq * 4 + 4)
                        mk_ = [f"MTb{pp}_{sq}", f"Mb{pp}_{sq}"]
                        for hq in range(4):
                            hl = sq * 4 + hq
                            h_ = rd * 8 + hl
                            mt = AB1_[:, h_, 0:128] if L == 0 else MTb[pp][:, hl, :]
                            P.op("pe", lambda e, hl=hl, hq=hq, mt=mt, bX=bX: e.matmul(BK[bX][:, hq * 128:(hq + 1) * 128], lhsT=mt, rhs=Xb[:, hl, :],
                                                                                      start=True, stop=True),
                                 reads=[kAB if L == 0 else mk_[0], f"Xb{sq}"], writes=[f"bk{bX}"], sig=(hq == 3), partial=(hq > 0))
                        if L < 6:
                            for hq in range(4):
                                hl = sq * 4 + hq
                                h_ = rd * 8 + hl
                                mt = AB1_[:, h_, 0:128] if L == 0 else MTb[pp][:, hl, :]
                                m = M0[:, hl, :] if L == 0 else Mb[pp][:, hl, :]
                                rk = [kAB, "M0"] if L == 0 else mk_
                                P.op("pe", lambda e, hq=hq, mt=mt, m=m, bM=bM: e.matmul(BK[bM][:, hq * 128:(hq + 1) * 128], lhsT=mt, rhs=m, start=True, stop=True),
                                     reads=rk, writes=[f"bk{bM}"], sig=(hq == 3), partial=(hq > 0))
                                P.op("pe", lambda e, hq=hq, mt=mt, m=m, bMT=bMT: e.matmul(BK[bMT][:, hq * 128:(hq + 1) * 128], lhsT=m, rhs=mt, start=True, stop=True),
                                     reads=rk, writes=[f"bk{bMT}"], sig=(hq == 3), partial=(hq > 0))
                        P.op("dve", lambda e, hsq=hsq, bX=bX: e.tensor_tensor(out=X32_[:, hsq, :], in0=BK[bX][:, :].rearrange("p (h n) -> p h n", h=4),
                                                                             in1=X32_[:, hsq, :], op=ALU.add),
                             reads=[f"bk{bX}", kX(sq)], writes=[kX(sq)])
                        if L < 6:
                            P.op("act", lambda e, sq=sq, hsq=hsq: e.activation(out=Xb[:, sq * 4:sq * 4 + 4, :], in_=X32_[:, hsq, :], func=AF.Copy),
                                 reads=[kX(sq)], writes=[f"Xb{sq}"])
                            P.op("act", lambda e, sq=sq, pp=pp, bM=bM: e.activation(out=Mb[1 - pp][:, sq * 4:sq * 4 + 4, :],
                                                                                    in_=BK[bM][:, :].rearrange("p (h n) -> p h n", h=4), func=AF.Copy),
                                 reads=[f"bk{bM}"], writes=[f"Mb{1 - pp}_{sq}"])
                            if sq == 0:
                                P.op("act", lambda e, sq=sq, pp=pp, bMT=bMT: e.activation(out=MTb[1 - pp][:, sq * 4:sq * 4 + 4, :],
                                                                                          in_=BK[bMT][:, :].rearrange("p (h n) -> p h n", h=4), func=AF.Copy),
                                     reads=[f"bk{bMT}"], writes=[f"MTb{1 - pp}_{sq}"])
                            else:
                                P.op("dve", lambda e, sq=sq, pp=pp, bMT=bMT: e.tensor_copy(out=MTb[1 - pp][:, sq * 4:sq * 4 + 4, :],
                                                                                           in_=BK[bMT][:, :].rearrange("p (h n) -> p h n", h=4)),
                                     reads=[f"bk{bMT}"], writes=[f"MTb{1 - pp}_{sq}"])
                P.op("pool", lambda e, hs_=hs_: e.tensor_copy(out=W1c[:, hs_, :], in_=X32_[:, hs_, 0:64]), reads=[kX(0), kX(1)], writes=["W1c"], partial=(rd > 0))
            for cc in range(8):
                P.op("pe", lambda e, cc=cc: e.transpose(out=bkb(4)[:, cc * 128:(cc + 1) * 128],
                                                        in_=W1c[:, 2 * cc:2 * cc + 2, :].rearrange("p h n -> p (h n)"), identity=C.ident_b[:]),
                     reads=["W1c", "ident_b"], writes=["bk4"], sig=(cc == 7), partial=(cc > 0))
            P.op("act", lambda e: e.activation(out=W1T_[:, :, :], in_=bkb(4).rearrange("p (c j) -> p c j", c=8), func=AF.Copy),
                 reads=["bk4"], writes=[kW])

        def tail(t):
            i = t % 2
            rows = slice(t * 128, (t + 1) * 128)
            RT, AT, BT, KT, VB = inb[i]
            kin = [f"in{n}_{i}" for n in range(5)]
            ARt_, AB1_, AK1_, X32_, W1T_ = ARt[i], AB1[i], AK1[i], X32[i], W1T[i]
            kA, kAB, kAK, kW = f"ARt{i}", f"AB1{i}", f"AK1{i}", f"W1T{i}"
            kXs = [f"X32{i}_0", f"X32{i}_1"]
            P.dma("sp", gb[:], scr["G"][rows, :], reads=[f"sG{t}"], writes=["gb"])
            P.dma("act", bon[:], scr["BON"][rows, :], reads=[f"sBON{t}"], writes=["bon"])
            P.dma("act", hres[:], h[rows, :], reads=[f"{hkey}{t}"], writes=["hres"])
            for h_ in range(NH):
                c, p0 = h_ // 2, 64 * (h_ % 2)
                P.op("pe", lambda e, h_=h_, c=c, p0=p0: e.matmul(BK[6 + h_ % 2][:, c * 64:(c + 1) * 64], lhsT=W1T_[p0:p0 + 64, c, :],
                                                                  rhs=Sb[p0:p0 + 64, c, :], start=True, stop=True),
                     reads=[kW, "Sb"], writes=[f"bk{6 + h_ % 2}"], sig=(h_ >= 14), partial=(h_ >= 2))
            for bq in range(2):
                P.op("dve", lambda e, bq=bq: e.tensor_tensor(out=Ub[:, :].rearrange("p (c two n) -> p c two n", two=2, n=64)[:, :, bq, :],
                                                             in0=BK[6 + bq][:, :].rearrange("p (h n) -> p h n", h=8),
                                                             in1=X32_[:, :, :].rearrange("p (c two) n -> p c two n", two=2)[:, :, bq, 64:128], op=ALU.add),
                     reads=[f"bk{6 + bq}"] + kXs, writes=["Ub"], partial=(bq > 0))
            for h_ in range(NH):
                c, p0 = h_ // 2, 64 * (h_ % 2)
                ob_ = BK[6 + h_ % 2][:, c * 64:(c + 1) * 64]
                P.op("pe", lambda e, c=c, p0=p0, ob_=ob_: e.matmul(ob_, lhsT=ARt_[p0:p0 + 64, c, 1, :], rhs=Sb[p0:p0 + 64, c, :], start=True, stop=False),
                     reads=[kA, "Sb"], writes=[f"bk{6 + h_ % 2}"], sig=False, partial=(h_ >= 2))
                P.op("pe", lambda e, h_=h_, ob_=ob_: e.matmul(ob_, lhsT=AB1_[:, h_, 128:256], rhs=Ub[:, h_ * 64:(h_ + 1) * 64], start=False, stop=False),
                     reads=[kAB, "Ub"], writes=[f"bk{6 + h_ % 2}"], sig=False, partial=True)
                P.op("pe", lambda e, h_=h_, ob_=ob_: e.matmul(ob_, lhsT=AK1_[:, h_, 128:256], rhs=VB[:, h_ * 64:(h_ + 1) * 64], start=False, stop=True),
                     reads=[kAK, kin[4]], writes=[f"bk{6 + h_ % 2}"], sig=(h_ >= 14), partial=True)
            for bq in range(2):
                yv = lambda ap, bq=bq: ap.rearrange("p (c two n) -> p c two n", two=2, n=64)[:, :, bq, :]
                P.op("act", lambda e, bq=bq, yv=yv: e.activation(out=yv(Ysb[:, :]), in_=BK[6 + bq][:, :].rearrange("p (c n) -> p c n", c=8), func=AF.Copy),
                     reads=[f"bk{6 + bq}"], writes=["Ysb"], partial=(bq > 0))
                P.op("act", lambda e, bq=bq, yv=yv: e.activation(out=yv(Ysq[:, :]), in_=BK[6 + bq][:, :].rearrange("p (c n) -> p c n", c=8), func=AF.Square),
                     reads=[f"bk{6 + bq}"], writes=["Ysq"], partial=(bq > 0))
            for c in range(8):
                ob_ = BK[6 + c // 4][:, (c % 4) * 128:(c % 4 + 1) * 128]
                P.op("pe", lambda e, c=c, ob_=ob_: e.matmul(ob_, lhsT=BT[:, c * 128:(c + 1) * 128], rhs=Ub[:, c * 128:(c + 1) * 128], start=True, stop=False),
                     reads=[kin[2], "Ub"], writes=[f"bk{6 + c // 4}"], sig=False, partial=(c % 4 > 0))
                P.op("pe", lambda e, c=c, ob_=ob_: e.matmul(ob_, lhsT=KT[:, c * 128:(c + 1) * 128], rhs=VB[:, c * 128:(c + 1) * 128], start=False, stop=True),
                     reads=[kin[3], kin[4]], writes=[f"bk{6 + c // 4}"], sig=(c % 4 == 3), partial=True)
            for bq in range(2):
                for hh in range(2):
                    ps_ = slice(64 * hh, 64 * hh + 64)
                    P.op("dve", lambda e, bq=bq, hh=hh, ps_=ps_: e.tensor_tensor(
                        out=S32[ps_, bq * 4:bq * 4 + 4, :], in0=BK[6 + bq][ps_, :].rearrange("p (c n) -> p c n", c=4)[:, :, hh * 64:(hh + 1) * 64],
                        in1=S32[ps_, bq * 4:bq * 4 + 4, :], op=ALU.add),
                         reads=[f"bk{6 + bq}", "S32", "Sb"], writes=["S32"], partial=True)
            P.op("dve", lambda e: e.tensor_tensor(out=S32[:], in0=S32[:], in1=gc[i][:, :].unsqueeze(2).to_broadcast([128, 8, 64]), op=ALU.mult),
                 reads=["S32", f"gc{i}"], writes=["S32"])
            P.op("pool", lambda e: e.tensor_copy(out=Sb[:], in_=S32[:]), reads=["S32"], writes=["Sb"])
            P.op("dve", lambda e: e.tensor_reduce(out=s16[:, 0, :], in_=v3(Ysb[:]), axis=AX.X, op=ALU.add), reads=["Ysb"], writes=["q0"])
            P.op("dve", lambda e: e.tensor_reduce(out=s16[:, 1, :], in_=v3(Ysq[:]), axis=AX.X, op=ALU.add), reads=["Ysq"], writes=["q1"])
            P.op("pool", lambda e: e.tensor_scalar(out=s16[:, 2, :], in0=s16[:, 0, :], scalar1=1.0 / HS, scalar2=None, op0=ALU.mult),
                 reads=["q0"], writes=["q2"])
            P.op("pool", lambda e: e.tensor_tensor(out=s16[:, 3, :], in0=s16[:, 2, :], in1=s16[:, 2, :], op=ALU.mult), reads=["q2"], writes=["q3"])
            P.op("dve", lambda e: e.scalar_tensor_tensor(out=s16[:, 4, :], in0=s16[:, 1, :], scalar=1.0 / HS, in1=s16[:, 3, :],
                                                         op0=ALU.mult, op1=ALU.subtract), reads=["q1", "q3"], writes=["q4"])
            P.op("pool", lambda e: e.tensor_scalar(out=s16[:, 4, :], in0=s16[:, 4, :], scalar1=GN_EPS, scalar2=None, op0=ALU.add),
                 reads=["q4"], writes=["q4"])
            P.op("pool", lambda e: e.tensor_tensor(out=s16[:, 5, :], in0=s16[:, 4, :], in1=C.mhalf[:, 0:1].to_broadcast([128, NH]), op=ALU.pow),
                 reads=["q4"], writes=["q5"])
            P.op("dve", lambda e: e.tensor_tensor(out=v3(Ysb[:]), in0=v3(Ysb[:]), in1=bc(s16[:, 2, :]), op=ALU.subtract),
                 reads=["Ysb", "q2"], writes=["Ysb"])
            P.op("pool", lambda e: e.tensor_tensor(out=v3(Ysb[:]), in0=v3(Ysb[:]), in1=bc(s16[:, 5, :]), op=ALU.mult),
                 reads=["Ysb", "q5"], writes=["Ysb"])
            P.op("dve", lambda e: e.tensor_tensor(out=Ysb[:], in0=Ysb[:], in1=gnw[:], op=ALU.mult), reads=["Ysb", "gnw"], writes=["Ysb"])
            P.op("pool", lambda e: e.tensor_tensor(out=Ysb[:], in0=Ysb[:], in1=gnb[:], op=ALU.add), reads=["Ysb", "gnb"], writes=["Ysb"])
            P.op("dve", lambda e: e.tensor_tensor(out=Ysb[:], in0=Ysb[:], in1=bon[:], op=ALU.add), reads=["Ysb", "bon"], writes=["Ysb"])
            P.op("pool", lambda e: e.tensor_tensor(out=otm[:], in0=Ysb[:], in1=gb[:], op=ALU.mult), reads=["Ysb", "gb"], writes=["otm"])
            for cc in range(8):
                P.op("pe", lambda e, cc=cc: e.transpose(out=bkb(6)[:, cc * 128:(cc + 1) * 128], in_=otm[:, cc * 128:(cc + 1) * 128],
                                                        identity=C.ident_b[:]),
                     reads=["otm", "ident_b"], writes=["bk6"], sig=(cc == 7), partial=(cc > 0))
            P.op("act", lambda e: e.activation(out=oT[:, :, :], in_=bkb(6).rearrange("p (c j) -> p c j", c=8), func=AF.Copy),
                 reads=["bk6"], writes=["oT"])
            for hf in range(2):
                for cc in range(8):
                    P.op("pe", lambda e, cc=cc, hf=hf: e.matmul(BK[6 + hf][:, :], lhsT=oT[:, cc, :], rhs=wo[:, cc, hf * 512:(hf + 1) * 512],
                                                                start=(cc == 0), stop=(cc == 7)),
                         reads=["oT", "wo"], writes=[f"bk{6 + hf}"], sig=(cc == 7))
                P.op("dve", lambda e, hf=hf: e.tensor_tensor(out=hres[:, hf * 512:(hf + 1) * 512], in0=BK[6 + hf][:, :],
                                                             in1=hres[:, hf * 512:(hf + 1) * 512], op=ALU.add),
                     reads=[f"bk{6 + hf}", "hres"], writes=["hres"])
            P.dma("sp", h[rows, :], hres[:], reads=["hres"], writes=[f"{hkey}{t}"])

        head(0)
        for t in range(NT):
            P.begin_capture()
            tail(t)
            s_tail = P.end_capture()
            streams = [s_tail]
            if t + 1 < NT:
                P.begin_capture()
                head(t + 1)
                streams.append(P.end_capture())
            P.replay(streams)
        P.emit()


NHM = 8
DN = 128
DR = 64
VX_W = 130
PI = float(np.pi)


def norm_tile_fm(C, P, ht_ap, hkey_r, bufs, gfull, dstT, dkey, tcols):
    junk, st4, xs, PT = bufs["junk"], bufs["st4"], bufs["xs"], bufs["PT"]
    P.op("act", lambda e: e.activation(out=junk[:], in_=ht_ap, func=AF.Square, accum_out=st4[:, 0:1]),
         reads=[hkey_r], writes=["junk", "st0"])
    P.op("pool", lambda e: e.tensor_scalar(out=st4[:, 1:2], in0=st4[:, 0:1], scalar1=1.0 / D, scalar2=RMS_EPS,
                                           op0=ALU.mult, op1=ALU.add), reads=["st0"], writes=["st1"])
    P.op("pool", lambda e: e.tensor_tensor(out=st4[:, 2:3], in0=st4[:, 1:2], in1=C.mhalf[:], op=ALU.pow),
         reads=["st1"], writes=["st2"])
    P.op("dve", lambda e: e.tensor_scalar(out=xs[:], in0=ht_ap, scalar1=st4[:, 2:3], scalar2=None, op0=ALU.mult),
         reads=[hkey_r, "st2"], writes=["xs"])
    for cc in range(8):
        P.op("pe", lambda e, cc=cc: e.transpose(out=PT[:, cc * 128:(cc + 1) * 128], in_=xs[:, cc * 128:(cc + 1) * 128],
                                                identity=C.ident_b[:]),
             reads=["xs", "ident_b"], writes=["PT"], sig=(cc == 7), partial=(cc > 0))
    P.op("dve", lambda e: e.tensor_tensor(out=dstT[:, :, tcols], in0=PT[:, :].rearrange("p (c j) -> p c j", c=8),
                                          in1=gfull[:], op=ALU.mult),
         reads=["PT", "gfull"], writes=[dkey])


def rope_tables(C, S, sb, positions, invf_ap, sgn_ap, scr=None, reload=False):
    P = C.P
    if reload:
        cosT = sb("cosT", [64, S], F32)
        sinT = sb("sinT", [64, S], F32)
        P.dma("sp", cosT[:], scr["COS"], reads=["sCOS"], writes=["cosT"])
        P.dma("act", sinT[:], scr["SIN"], reads=["sSIN"], writes=["sinT"])
        return cosT, sinT
    posi = sb("posi", [64, S], I32)
    ang = sb("ang", [64, S], F32)
    tmp = sb("rtmp", [64, S], F32)
    tmi = sb("rtmi", [64, S], I32)
    cosT = sb("cosT", [64, S], F32)
    sinT = sb("sinT", [64, S], F32)
    invf = sb("invf", [64, 1], F32)
    sgn = sb("sgn", [64, 1], F32)
    P.dma("sp", posi[:], positions.partition_broadcast(64), writes=["posi"])
    P.dma("sp", invf[:], invf_ap.rearrange("(p o) -> p o", o=1), writes=["invf"])
    P.dma("sp", sgn[:], sgn_ap.rearrange("(p o) -> p o", o=1), writes=["sgn"])
    P.op("dve", lambda e: e.tensor_copy(out=ang[:], in_=posi[:]), reads=["posi"], writes=["ang"])
    P.op("dve", lambda e: e.tensor_scalar(out=ang[:], in0=ang[:], scalar1=invf[:, 0:1], scalar2=None, op0=ALU.mult),
         reads=["ang", "invf"], writes=["ang"])

    def reduce_sin(dst, shift, dkey):
        P.op("dve", lambda e: e.tensor_scalar(out=tmp[:], in0=ang[:], scalar1=shift, scalar2=1.0 / (2 * PI), op0=ALU.add, op1=ALU.mult),
             reads=["ang"], writes=["rtmp"])
        P.op("dve", lambda e: e.tensor_copy(out=tmi[:], in_=tmp[:]), reads=["rtmp"], writes=["rtmi"])
        P.op("dve", lambda e: e.tensor_copy(out=tmp[:], in_=tmi[:]), reads=["rtmi"], writes=["rtmp"])
        P.op("dve", lambda e: e.tensor_scalar(out=tmp[:], in0=tmp[:], scalar1=-2 * PI, scalar2=shift, op0=ALU.mult, op1=ALU.add),
             reads=["rtmp"], writes=["rtmp"])
        P.op("dve", lambda e: e.tensor_tensor(out=dst[:], in0=tmp[:], in1=ang[:], op=ALU.add), reads=["rtmp", "ang"], writes=[dkey])
        P.op("dve", lambda e: e.tensor_scalar(out=tmp[:], in0=dst[:], scalar1=PI, scalar2=-2 * PI, op0=ALU.is_gt, op1=ALU.mult),
             reads=[dkey], writes=["rtmp"])
        P.op("dve", lambda e: e.tensor_tensor(out=dst[:], in0=dst[:], in1=tmp[:], op=ALU.add), reads=[dkey, "rtmp"], writes=[dkey])
        P.op("dve", lambda e: e.tensor_scalar(out=tmp[:], in0=dst[:], scalar1=-PI, scalar2=2 * PI, op0=ALU.is_lt, op1=ALU.mult),
             reads=[dkey], writes=["rtmp"])
        P.op("dve", lambda e: e.tensor_tensor(out=dst[:], in0=dst[:], in1=tmp[:], op=ALU.add), reads=[dkey, "rtmp"], writes=[dkey])
        P.op("act", lambda e: e.activation(out=dst[:], in_=dst[:], func=AF.Sin), reads=[dkey], writes=[dkey])

    reduce_sin(sinT, 0.0, "sinT")
    reduce_sin(cosT, PI / 2, "cosT")
    P.op("dve", lambda e: e.tensor_scalar(out=sinT[:], in0=sinT[:], scalar1=sgn[:, 0:1], scalar2=None, op0=ALU.mult),
         reads=["sinT", "sgn"], writes=["sinT"])
    if scr is not None:
        P.dma("sp", scr["COS"], cosT[:], reads=["cosT"], writes=["sCOS"])
        P.dma("act", scr["SIN"], sinT[:], reads=["sinT"], writes=["sSIN"])
    return cosT, sinT


def mla_kv_phase(C, S, h, hkey, W, scr):
    nc, P = C.nc, C.P
    NT = S // 128
    with contextlib.ExitStack() as st:
        sb, ps = alloc(st, nc)
        cosT, sinT = rope_tables(C, S, sb, W["positions"], W["rope_invf"], W["rope_sgn"], scr=scr)
        wdkv = sb("wdkv", [128, 8, 320], BF16)
        wdsw = sb("wdsw", [128, 8, 64], BF16)
        wukv = sb("wukv", [128, 2, 2048], BF16)
        gcol = sb("gcol", [128, 8], F32)
        gfull = sb("gfull", [128, 8, 128], F32)
        glat = sb("glat", [128, 2], F32)
        ht = [sb(f"ht{i}", [128, D], F32) for i in range(2)]
        bufs = {"junk": sb("junk", [128, D], BF16), "st4": sb("st4", [128, 8], F32), "xs": sb("xs", [128, D], BF16),
                "PT": ps("PT", [128, D], BF16)}
        hT = sb("hT", [128, 8, 128], BF16)
        cs = sb("cs", [128, 256], BF16)
        cT = sb("cT", [128, 2, 128], BF16)
        knT = [sb(f"knT{i}", [128, 8, 128], BF16) for i in range(2)]
        vx = [sb(f"vx{i}", [128, 8, VX_W], BF16) for i in range(2)]
        kr = [sb(f"kr{i}", [64, 128], BF16) for i in range(2)]
        t1 = sb("t1", [64, 128], F32)
        t2 = sb("t2", [64, 128], F32)
        PC = ps("PC", [128, 512], F32)
        PK = [ps(f"PK{i}", [128, 512], F32) for i in range(2)]
        PV = [ps(f"PV{i}", [128, 512], F32) for i in range(2)]
        PR = ps("PR", [128, 512], F32)

        P.dma("pool", wdkv[:], W["mla_w_dkv"].rearrange("(c p) n -> p c n", p=128), writes=["wdkv"])
        P.dma("pool", wdsw[:, :, 0:32], W["mla_w_dkv"][:, 288:320].rearrange("(c p) n -> p c n", p=128), writes=["wdsw"])
        P.dma("pool", wdsw[:, :, 32:64], W["mla_w_dkv"][:, 256:288].rearrange("(c p) n -> p c n", p=128), writes=["wdsw"], partial=True)
        for q in range(2):
            P.dma("pool", wukv[:, :, q * 1024:(q + 1) * 1024], W["mla_w_ukv"][:, q * 1024:(q + 1) * 1024].rearrange("(c p) n -> p c n", p=128),
                  writes=["wukv"], partial=(q > 0))
        load_col(C, gcol[:, :], W["kv_norm_g"], "gcol", 8)
        load_col(C, glat[:, :], W["mla_kv_latent_g"], "glat", 2)
        P.op("dve", lambda e: e.tensor_copy(out=gfull[:], in_=gcol[:, :].unsqueeze(2).to_broadcast([128, 8, 128])),
             reads=["gcol"], writes=["gfull"])
        for i in range(2):
            P.op("pool", lambda e, i=i: e.memset(vx[i][:], 1.0), writes=[f"vx{i}"])
        wuv = wukv[:, :, :].rearrange("p c (h x) -> p c h x", h=NHM)

        def hload(t):
            P.dma("sp", ht[t % 2][:], h[t * 128:(t + 1) * 128, :], reads=[f"{hkey}{t}"], writes=[f"ht{t % 2}"])

        hload(0)

        def do_tile(t):
            i = t % 2
            rows = slice(t * 128, (t + 1) * 128)
            tc_ = slice(t * 128, (t + 1) * 128)
            if t + 1 < NT:
                hload(t + 1)
            norm_tile_fm(C, P, ht[i][:], f"ht{i}", bufs, gfull, hT, "hT", slice(0, 128))
            for cc in range(8):
                P.op("pe", lambda e, cc=cc: e.matmul(PC[:, 0:320], lhsT=hT[:, cc, :], rhs=wdkv[:, cc, :], start=(cc == 0), stop=(cc == 7)),
                     reads=["hT", "wdkv"], writes=["PC"], sig=(cc == 7))
            st4 = bufs["st4"]
            P.op("act", lambda e: e.activation(out=bufs["junk"][:, 0:256], in_=PC[:, 0:256], func=AF.Square, accum_out=st4[:, 4:5]),
                 reads=["PC"], writes=["junk", "st4"])
            P.op("pool", lambda e: e.tensor_scalar(out=st4[:, 5:6], in0=st4[:, 4:5], scalar1=1.0 / 256, scalar2=RMS_EPS, op0=ALU.mult, op1=ALU.add),
                 reads=["st4"], writes=["st5"])
            P.op("pool", lambda e: e.tensor_tensor(out=st4[:, 6:7], in0=st4[:, 5:6], in1=C.mhalf[:], op=ALU.pow), reads=["st5"], writes=["st6"])
            P.op("dve", lambda e: e.tensor_scalar(out=cs[:], in0=PC[:, 0:256], scalar1=st4[:, 6:7], scalar2=None, op0=ALU.mult),
                 reads=["PC", "st6"], writes=["cs"])
            PT = bufs["PT"]
            for cc in range(2):
                P.op("pe", lambda e, cc=cc: e.transpose(out=PT[:, cc * 128:(cc + 1) * 128], in_=cs[:, cc * 128:(cc + 1) * 128], identity=C.ident_b[:]),
                     reads=["cs", "ident_b"], writes=["PT"], sig=(cc == 1), partial=(cc > 0))
            for cc in range(2):
                P.op("act", lambda e, cc=cc: e.activation(out=cT[:, cc, :], in_=PT[:, cc * 128:(cc + 1) * 128], func=AF.Copy, scale=glat[:, cc:cc + 1]),
                     reads=["PT", "glat"], writes=["cT"], partial=(cc > 0))
            for hh in range(NHM):
                for cc in range(2):
                    P.op("pe", lambda e, hh=hh, cc=cc: e.matmul(PK[hh // 4][:, (hh % 4) * 128:(hh % 4 + 1) * 128], lhsT=wuv[:, cc, hh, 0:128],
                                                                rhs=cT[:, cc, :], start=(cc == 0), stop=(cc == 1)),
                         reads=["wukv", "cT"], writes=[f"PK{hh // 4}"], sig=(cc == 1 and hh % 4 == 3), partial=not (hh % 4 == 0 and cc == 0))
            for q in range(2):
                P.op("act" if q == 0 else "dve",
                     (lambda e, q=q: e.activation(out=knT[i][:, q * 4:q * 4 + 4, :], in_=PK[q][:, :].rearrange("p (h n) -> p h n", h=4), func=AF.Copy)) if q == 0 else
                     (lambda e, q=q: e.tensor_copy(out=knT[i][:, q * 4:q * 4 + 4, :], in_=PK[q][:, :].rearrange("p (h n) -> p h n", h=4))),
                     reads=[f"PK{q}"], writes=[f"knT{i}"], partial=(q > 0))
            P.dma("sp", scr["KN"][:, :, tc_].rearrange("h d t -> d h t"), knT[i][:], reads=[f"knT{i}"], writes=[f"sKN{t}"])
            for q in range(2):
                for cc in range(2):
                    P.op("pe", lambda e, q=q, cc=cc: e.matmul(PV[q][:, :], lhsT=cT[:, cc, :], rhs=wuv[:, cc, q * 4:q * 4 + 4, 128:256],
                                                              start=(cc == 0), stop=(cc == 1)),
                         reads=["wukv", "cT"], writes=[f"PV{q}"], sig=(cc == 1))
                P.op("act" if q == 0 else "dve",
                     (lambda e, q=q: e.activation(out=vx[i][:, q * 4:q * 4 + 4, 0:128], in_=PV[q][:, :].rearrange("p (h n) -> p h n", h=4), func=AF.Copy)) if q == 0 else
                     (lambda e, q=q: e.tensor_copy(out=vx[i][:, q * 4:q * 4 + 4, 0:128], in_=PV[q][:, :].rearrange("p (h n) -> p h n", h=4))),
                     reads=[f"PV{q}"], writes=[f"vx{i}"], partial=True)
            P.dma("sp", scr["VX"][rows, :, :], vx[i][:], reads=[f"vx{i}"], writes=[f"sVX{t}"])
            for cc in range(8):
                P.op("pe", lambda e, cc=cc: e.matmul(PR[0:64, 0:128], lhsT=wdkv[:, cc, 256:320], rhs=hT[:, cc, :], start=(cc == 0), stop=(cc == 7)),
                     reads=["hT", "wdkv"], writes=["PR"], sig=False)
            for cc in range(8):
                P.op("pe", lambda e, cc=cc: e.matmul(PR[0:64, 128:256], lhsT=wdsw[:, cc, :], rhs=hT[:, cc, :], start=(cc == 0), stop=(cc == 7)),
                     reads=["hT", "wdsw"], writes=["PR"], sig=(cc == 7), partial=True)
            P.op("dve", lambda e: e.tensor_tensor(out=t1[:], in0=PR[0:64, 0:128], in1=cosT[:, tc_], op=ALU.mult), reads=["PR", "cosT"], writes=["t1"])
            P.op("dve", lambda e: e.tensor_tensor(out=t2[:], in0=PR[0:64, 128:256], in1=sinT[:, tc_], op=ALU.mult), reads=["PR", "sinT"], writes=["t2"])
            P.op("pool", lambda e: e.tensor_tensor(out=kr[i][:], in0=t1[:], in1=t2[:], op=ALU.add), reads=["t1", "t2"], writes=[f"kr{i}"])
            P.dma("sp", scr["KR"][:, tc_], kr[i][:], reads=[f"kr{i}"], writes=[f"sKR{t}"])

        for t in range(NT):
            do_tile(t)
        P.emit()


def mla_q_phase(C, S, h, hkey, W, scr):
    nc, P = C.nc, C.P
    NT = S // 128
    with contextlib.ExitStack() as st:
        sb, ps = alloc(st, nc)
        cosT, sinT = rope_tables(C, S, sb, W["positions"], W["rope_invf"], W["rope_sgn"], scr=scr, reload=True)
        wdq = sb("wdq", [128, 8, 512], BF16)
        wuq = sb("wuq", [128, 4, 1536], BF16)
        wusw = sb("wusw", [128, 4, NHM, 64], BF16)
        gcol = sb("gcol", [128, 8], F32)
        gfull = sb("gfull", [128, 8, 128], F32)
        glat = sb("glat", [128, 4], F32)
        ht = [sb(f"ht{i}", [128, D], F32) for i in range(2)]
        bufs = {"junk": sb("junk", [128, D], BF16), "st4": sb("st4", [128, 8], F32), "xs": sb("xs", [128, D], BF16),
                "PT": ps("PT", [128, D], BF16)}
        hT = sb("hT", [128, 8, 128], BF16)
        qs = sb("qs", [128, 512], BF16)
        qlT = sb("qlT", [128, 4, 128], BF16)
        qnT = [sb(f"qnT{i}", [128, 8, 128], BF16) for i in range(2)]
        qr = [sb(f"qr{i}", [64, 8, 128], BF16) for i in range(2)]
        t1 = sb("t1", [64, 8, 128], F32)
        t2 = sb("t2", [64, 8, 128], F32)
        PC = ps("PC", [128, 512], F32)
        PK = [ps(f"PK{i}", [128, 512], F32) for i in range(2)]
        PR = [ps(f"PR{i}", [128, 1024], F32) for i in range(2)]

        P.dma("pool", wdq[:], W["mla_w_dq"].rearrange("(c p) n -> p c n", p=128), writes=["wdq"])
        P.dma("pool", wuq[:], W["mla_w_uq"].rearrange("(c p) n -> p c n", p=128), writes=["wuq"])
        wq4 = W["mla_w_uq"].rearrange("(c p) (h x) -> p c h x", p=128, h=NHM)
        for cc in range(4):
            P.dma("pool", wusw[:, cc, :, 0:32], wq4[:, cc, :, 160:192], writes=["wusw"], partial=True)
            P.dma("pool", wusw[:, cc, :, 32:64], wq4[:, cc, :, 128:160], writes=["wusw"], partial=True)
        load_col(C, gcol[:, :], W["norm_g"], "gcol", 8)
        load_col(C, glat[:, :], W["mla_q_latent_g"], "glat", 4)
        P.op("dve", lambda e: e.tensor_copy(out=gfull[:], in_=gcol[:, :].unsqueeze(2).to_broadcast([128, 8, 128])),
             reads=["gcol"], writes=["gfull"])
        wu4 = wuq[:, :, :].rearrange("p c (h x) -> p c h x", h=NHM)

        def hload(t):
            P.dma("sp", ht[t % 2][:], h[t * 128:(t + 1) * 128, :], reads=[f"{hkey}{t}"], writes=[f"ht{t % 2}"])

        hload(0)

        def do_tile(t):
            i = t % 2
            rows = slice(t * 128, (t + 1) * 128)
            tc_ = slice(t * 128, (t + 1) * 128)
            if t + 1 < NT:
                hload(t + 1)
            norm_tile_fm(C, P, ht[i][:], f"ht{i}", bufs, gfull, hT, "hT", slice(0, 128))
            for cc in range(8):
                P.op("pe", lambda e, cc=cc: e.matmul(PC[:, :], lhsT=hT[:, cc, :], rhs=wdq[:, cc, :], start=(cc == 0), stop=(cc == 7)),
                     reads=["hT", "wdq"], writes=["PC"], sig=(cc == 7))
            st4 = bufs["st4"]
            P.op("act", lambda e: e.activation(out=bufs["junk"][:, 0:512], in_=PC[:, :], func=AF.Square, accum_out=st4[:, 4:5]),
                 reads=["PC"], writes=["junk", "st4"])
            P.op("pool", lambda e: e.tensor_scalar(out=st4[:, 5:6], in0=st4[:, 4:5], scalar1=1.0 / 512, scalar2=RMS_EPS, op0=ALU.mult, op1=ALU.add),
                 reads=["st4"], writes=["st5"])
            P.op("pool", lambda e: e.tensor_tensor(out=st4[:, 6:7], in0=st4[:, 5:6], in1=C.mhalf[:], op=ALU.pow), reads=["st5"], writes=["st6"])
            P.op("dve", lambda e: e.tensor_scalar(out=qs[:], in0=PC[:, :], scalar1=st4[:, 6:7], scalar2=None, op0=ALU.mult),
                 reads=["PC", "st6"], writes=["qs"])
            PT = bufs["PT"]
            for cc in range(4):
                P.op("pe", lambda e, cc=cc: e.transpose(out=PT[:, cc * 128:(cc + 1) * 128], in_=qs[:, cc * 128:(cc + 1) * 128], identity=C.ident_b[:]),
                     reads=["qs", "ident_b"], writes=["PT"], sig=(cc == 3), partial=(cc > 0))
            for cc in range(4):
                P.op("act", lambda e, cc=cc: e.activation(out=qlT[:, cc, :], in_=PT[:, cc * 128:(cc + 1) * 128], func=AF.Copy, scale=glat[:, cc:cc + 1]),
                     reads=["PT", "glat"], writes=["qlT"], partial=(cc > 0))
            for hh in range(NHM):
                for cc in range(4):
                    P.op("pe", lambda e, hh=hh, cc=cc: e.matmul(PK[hh // 4][:, (hh % 4) * 128:(hh % 4 + 1) * 128], lhsT=wu4[:, cc, hh, 0:128],
                                                                rhs=qlT[:, cc, :], start=(cc == 0), stop=(cc == 3)),
                         reads=["wuq", "qlT"], writes=[f"PK{hh // 4}"], sig=(cc == 3 and hh % 4 == 3), partial=not (hh % 4 == 0 and cc == 0))
            for q in range(2):
                P.op("act" if q == 0 else "dve",
                     (lambda e, q=q: e.activation(out=qnT[i][:, q * 4:q * 4 + 4, :], in_=PK[q][:, :].rearrange("p (h n) -> p h n", h=4), func=AF.Copy)) if q == 0 else
                     (lambda e, q=q: e.tensor_copy(out=qnT[i][:, q * 4:q * 4 + 4, :], in_=PK[q][:, :].rearrange("p (h n) -> p h n", h=4))),
                     reads=[f"PK{q}"], writes=[f"qnT{i}"], partial=(q > 0))
            P.dma("sp", scr["QN"][:, :, tc_].rearrange("h d t -> d h t"), qnT[i][:], reads=[f"qnT{i}"], writes=[f"sQN{t}"])
            for hh in range(NHM):
                for cc in range(4):
                    P.op("pe", lambda e, hh=hh, cc=cc: e.matmul(PR[0][0:64, hh * 128:(hh + 1) * 128], lhsT=wu4[:, cc, hh, 128:192], rhs=qlT[:, cc, :],
                                                                start=(cc == 0), stop=(cc == 3)),
                         reads=["wuq", "qlT"], writes=["PR0"], sig=False, partial=True)
            for hh in range(NHM):
                for cc in range(4):
                    P.op("pe", lambda e, hh=hh, cc=cc: e.matmul(PR[1][0:64, hh * 128:(hh + 1) * 128], lhsT=wusw[:, cc, hh, :], rhs=qlT[:, cc, :],
                                                                start=(cc == 0), stop=(cc == 3)),
                         reads=["wusw", "qlT"], writes=["PR1", "PR0"], sig=(hh == 7 and cc == 3), partial=True)
            cb = cosT[:, tc_].unsqueeze(1).to_broadcast([64, NHM, 128])
            sbb = sinT[:, tc_].unsqueeze(1).to_broadcast([64, NHM, 128])
            P.op("dve", lambda e: e.tensor_tensor(out=t1[:], in0=PR[0][0:64, :].rearrange("p (h n) -> p h n", h=NHM), in1=cb, op=ALU.mult),
                 reads=["PR0", "cosT"], writes=["t1"])
            P.op("dve", lambda e: e.tensor_tensor(out=t2[:], in0=PR[1][0:64, :].rearrange("p (h n) -> p h n", h=NHM), in1=sbb, op=ALU.mult),
                 reads=["PR1", "sinT"], writes=["t2"])
            P.op("pool", lambda e: e.tensor_tensor(out=qr[i][:], in0=t1[:], in1=t2[:], op=ALU.add), reads=["t1", "t2"], writes=[f"qr{i}"])
            P.dma("sp", scr["QR"][:, :, tc_].rearrange("h d t -> d h t"), qr[i][:], reads=[f"qr{i}"], writes=[f"sQR{t}"])

        for t in range(NT):
            do_tile(t)
        P.emit()


def mla_attn_phase(C, S, scr):
    nc, P = C.nc, C.P
    NT = S // 128
    NQC = S // 512
    scale = float((DN + DR) ** -0.5)
    with contextlib.ExitStack() as st:
        sb, ps = alloc(st, nc)
        krT = sb("krT", [128, S], BF16)
        knT = [sb(f"knT{i}", [128, S], BF16) for i in range(2)]
        qnT = [sb(f"qnT{i}", [128, S], BF16) for i in range(2)]
        qrT = [sb(f"qrT{i}", [128, S], BF16) for i in range(2)]
        vx = [sb(f"vx{i}", [128, NT, VX_W], BF16) for i in range(2)]
        pt = [sb(f"pt{i}", [128, 512], BF16) for i in range(4)]
        rs = sb("rs", [128, 4], F32)
        ao = [sb(f"ao{i}", [128, 128], BF16) for i in range(4)]
        PS = [ps(f"PS{i}", [128, 512], F32) for i in range(2)]
        PO = [ps(f"PO{i}", [128, 512], F32) for i in range(4)]
        allk = lambda n: [f"s{n}{t}" for t in range(NT)]
        P.dma("sp", krT[0:64, :], scr["KR"][:, :], reads=allk("KR"), writes=["krT"])
        P.dma("act", krT[64:128, :], scr["KR"][:, :], reads=allk("KR"), writes=["krT"], partial=True)
        cnt = {"s": 0, "p": 0, "a": 0}

        def load_head(hh):
            b = hh % 2
            P.dma("sp", knT[b][:], scr["KN"][hh], reads=allk("KN"), writes=[f"knT{b}"])
            P.dma("act", qnT[b][:], scr["QN"][hh], reads=allk("QN"), writes=[f"qnT{b}"])
            P.dma("sp", qrT[b][0:64, :], scr["QR"][hh], reads=allk("QR"), writes=[f"qrT{b}"])
            P.dma("sp", qrT[b][64:128, :], scr["QR"][hh], reads=allk("QR"), writes=[f"qrT{b}"], partial=True)
            P.dma("act", vx[b][:], scr["VX"][:, hh, :].rearrange("(t p) x -> p t x", p=128), reads=allk("VX"), writes=[f"vx{b}"])

        load_head(0)
        for hh in range(NHM):
            b = hh % 2
            if hh + 1 < NHM:
                load_head(hh + 1)
            for qc in range(NQC):
                nkt = 4 * qc + 4

                def qk(kt, qc=qc, b=b):
                    q0 = max(qc * 512, kt * 128)
                    q1 = (qc + 1) * 512
                    n = q1 - q0
                    s_ = cnt["s"] % 2
                    cnt["s"] += 1
                    p_ = cnt["p"] % 4
                    cnt["p"] += 1
                    ks = slice(kt * 128, (kt + 1) * 128)
                    P.op("pe", lambda e: e.matmul(PS[s_][:, 0:n], lhsT=knT[b][:, ks], rhs=qnT[b][:, q0:q1], start=True, stop=False),
                         reads=[f"knT{b}", f"qnT{b}"], writes=[f"PS{s_}"], sig=False)
                    return (kt, p_, q0, q1, n, s_, ks)

                def qk_rope(st_, b=b):
                    kt, p_, q0, q1, n, s_, ks = st_
                    rp = slice(64 * (kt % 2), 64 * (kt % 2) + 64)
                    P.op("pe", lambda e: e.matmul(PS[s_][:, 0:n], lhsT=krT[rp, ks], rhs=qrT[b][rp, q0:q1], start=False, stop=True),
                         reads=["krT", f"qrT{b}"], writes=[f"PS{s_}"], partial=True)

                def qk_exp(st_, qc=qc):
                    kt, p_, q0, q1, n, s_, ks = st_
                    P.op("act", lambda e: e.activation(out=pt[p_][:, 0:n], in_=PS[s_][:, 0:n], func=AF.Exp, scale=scale),
                         reads=[f"PS{s_}"], writes=[f"pt{p_}"])
                    if kt >= 4 * qc:
                        P.op("pool", lambda e: e.affine_select(out=pt[p_][:, 0:128], in_=pt[p_][:, 0:128], pattern=[[1, 128]],
                                                               compare_op=ALU.is_ge, fill=0.0, base=0, channel_multiplier=-1),
                             reads=[f"pt{p_}"], writes=[f"pt{p_}"])
                    return (kt, p_, q0)

                def pv(st_, qc=qc, b=b):
                    kt, p_, q0 = st_
                    for j in range(4):
                        qt = qc * 4 + j
                        if qt < kt:
                            continue
                        c0 = qt * 128 - q0
                        P.op("pe", lambda e, j=j, c0=c0, qt=qt: e.matmul(PO[j][:, 0:VX_W], lhsT=pt[p_][:, c0:c0 + 128], rhs=vx[b][:, kt, :],
                                                                        start=(kt == 0), stop=(kt == qt)),
                             reads=[f"pt{p_}", f"vx{b}"], writes=[f"PO{j}"], sig=(kt == qt))

                pend = []
                for k2 in range(0, nkt, 2):
                    a_ = qk(k2)
                    b_ = qk(k2 + 1)
                    qk_rope(a_)
                    qk_rope(b_)
                    sa_ = qk_exp(a_)
                    sb_ = qk_exp(b_)
                    for pr in pend:
                        pv(pr)
                    pend = [sa_, sb_]
                for pr in pend:
                    pv(pr)
                for j in range(4):
                    qt = qc * 4 + j
                    a_ = cnt["a"] % 4
                    cnt["a"] += 1
                    P.op("dve", lambda e, j=j: e.reciprocal(out=rs[:, j:j + 1], in_=PO[j][:, 128:129]), reads=[f"PO{j}"], writes=[f"rs{j}"])
                    P.op("dve", lambda e, j=j, a_=a_: e.tensor_scalar(out=ao[a_][:], in0=PO[j][:, 0:128], scalar1=rs[:, j:j + 1], scalar2=None, op0=ALU.mult),
                         reads=[f"PO{j}", f"rs{j}"], writes=[f"ao{a_}"])
                    P.dma("sp", scr["AO"][qt * 128:(qt + 1) * 128, hh * 128:(hh + 1) * 128], ao[a_][:], reads=[f"ao{a_}"], writes=[f"sAO{qt}"], partial=True)
        P.emit()


def mla_out_phase(C, S, h, hkey, W, scr):
    nc, P = C.nc, C.P
    NT = S // 128
    with contextlib.ExitStack() as st:
        sb, ps = alloc(st, nc)
        wo = sb("wo", [128, 8, D], BF16)
        aot = [sb(f"aot{i}", [128, D], BF16) for i in range(2)]
        hres = [sb(f"hres{i}", [128, D], F32) for i in range(2)]
        hout = [sb(f"hout{i}", [128, D], F32) for i in range(2)]
        oT = sb("oT", [128, 8, 128], BF16)
        PT = ps("PT", [128, D], BF16)
        PO = [ps(f"PO{i}", [128, 512], F32) for i in range(2)]
        for q in range(2):
            P.dma("pool", wo[:, :, q * 512:(q + 1) * 512], W["mla_w_o"][:, q * 512:(q + 1) * 512].rearrange("(c p) n -> p c n", p=128),
                  writes=["wo"], partial=(q > 0))

        def oload(t):
            rows = slice(t * 128, (t + 1) * 128)
            P.dma("sp", aot[t % 2][:], scr["AO"][rows, :], reads=[f"sAO{t}"], writes=[f"aot{t % 2}"])
            P.dma("sp", hres[t % 2][:], h[rows, :], reads=[f"{hkey}{t}"], writes=[f"hres{t % 2}"])

        oload(0)

        def do_tile(t):
            i = t % 2
            rows = slice(t * 128, (t + 1) * 128)
            if t + 1 < NT:
                oload(t + 1)
            for cc in range(8):
                P.op("pe", lambda e, cc=cc: e.transpose(out=PT[:, cc * 128:(cc + 1) * 128], in_=aot[i][:, cc * 128:(cc + 1) * 128], identity=C.ident_b[:]),
                     reads=[f"aot{i}", "ident_b"], writes=["PT"], sig=(cc == 7), partial=(cc > 0))
            P.op("act", lambda e: e.activation(out=oT[:, :, :], in_=PT[:, :].rearrange("p (c j) -> p c j", c=8), func=AF.Copy),
                 reads=["PT"], writes=["oT"])
            for hf in range(2):
                for cc in range(8):
                    P.op("pe", lambda e, cc=cc, hf=hf: e.matmul(PO[hf][:, :], lhsT=oT[:, cc, :], rhs=wo[:, cc, hf * 512:(hf + 1) * 512],
                                                                start=(cc == 0), stop=(cc == 7)),
                         reads=["oT", "wo"], writes=[f"PO{hf}"], sig=(cc == 7))
                P.op("dve", lambda e, hf=hf: e.tensor_tensor(out=hout[i][:, hf * 512:(hf + 1) * 512], in0=PO[hf][:, :],
                                                             in1=hres[i][:, hf * 512:(hf + 1) * 512], op=ALU.add),
                     reads=[f"PO{hf}", f"hres{i}"], writes=[f"hout{i}"], partial=(hf > 0))
            P.dma("sp", h[rows, :], hout[i][:], reads=[f"hout{i}"], writes=[f"{hkey}{t}"])

        for t in range(NT):
            do_tile(t)
        P.emit()


PARAM_SHAPES = {
    "norm_g": [2, 3, D], "ffn_w_gate": [2, 2, D, DFF], "ffn_w_up": [2, 2, D, DFF], "ffn_w_down": [2, 2, DFF, D],
    "rwkv_mix": [1, 6, D], "rwkv_w_r": [1, D, D], "rwkv_w_k": [1, D, D], "rwkv_w_v": [1, D, D], "rwkv_w_o": [1, D, D],
    "rwkv_w0": [1, D], "rwkv_w1": [1, D, 64], "rwkv_w2": [1, 64, D], "rwkv_a0": [1, D], "rwkv_a1": [1, D, 64],
    "rwkv_a2": [1, 64, D], "rwkv_g1": [1, D, 128], "rwkv_g2": [1, 128, D], "rwkv_k_k": [1, D], "rwkv_k_a": [1, D],
    "rwkv_r_k": [1, 16, 64], "rwkv_gn_w": [1, D], "rwkv_gn_b": [1, D], "kv_norm_g": [D], "mla_w_dkv": [D, 320],
    "mla_kv_latent_g": [256], "mla_w_ukv": [256, 2048], "mla_w_dq": [1, D, 512], "mla_q_latent_g": [1, 512],
    "mla_w_uq": [1, 512, 1536], "mla_w_o": [1, D, D], "final_norm_g": [D],
}


def rope_consts():
    invf = (np.float32(10000.0) ** (-(np.arange(0, DR, 2, dtype=np.float32)) / np.float32(DR))).astype(np.float32)
    invf2 = np.concatenate([invf, invf]).astype(np.float32)
    sgn = np.concatenate([-np.ones(32, np.float32), np.ones(32, np.float32)])
    return invf2, sgn


def build_program(S):
    nc = bass.Bass("TRN2", target_bir_lowering=False)
    NT = S // 128
    x = nc.dram_tensor("x", [S, D], F32, kind="ExternalInput").ap()
    pos = nc.dram_tensor("positions", [S], I32, kind="ExternalInput").ap()
    A = {n: nc.dram_tensor(n, shp, F32, kind="ExternalInput").ap() for n, shp in PARAM_SHAPES.items()}
    invf = nc.dram_tensor("rope_invf", [DR], F32, kind="ExternalInput").ap()
    sgn = nc.dram_tensor("rope_sgn", [DR], F32, kind="ExternalInput").ap()
    out = nc.dram_tensor("out", [S, D], F32, kind="ExternalOutput").ap()
    h = nc.dram_tensor("h_scr", [S, D], F32, kind="Internal").ap()
    scr = {}
    for n in ("RT", "AT", "BT", "KT", "VB", "AO"):
        scr[n] = nc.dram_tensor("scr_" + n, [S, D], BF16, kind="Internal").ap()
    for n in ("G", "BON"):
        scr[n] = nc.dram_tensor("scr_" + n, [S, D], F32, kind="Internal").ap()
    scr["GC"] = nc.dram_tensor("scr_GC", [NT, D], F32, kind="Internal").ap()
    scr["KN"] = nc.dram_tensor("scr_KN", [NHM, DN, S], BF16, kind="Internal").ap()
    scr["QN"] = nc.dram_tensor("scr_QN", [NHM, DN, S], BF16, kind="Internal").ap()
    scr["QR"] = nc.dram_tensor("scr_QR", [NHM, DR, S], BF16, kind="Internal").ap()
    scr["KR"] = nc.dram_tensor("scr_KR", [DR, S], BF16, kind="Internal").ap()
    scr["VX"] = nc.dram_tensor("scr_VX", [S, NHM, VX_W], BF16, kind="Internal").ap()
    scr["COS"] = nc.dram_tensor("scr_COS", [DR, S], F32, kind="Internal").ap()
    scr["SIN"] = nc.dram_tensor("scr_SIN", [DR, S], F32, kind="Internal").ap()
    with contextlib.ExitStack() as st:
        C = Ctx()
        C.nc = nc
        C.stack = st
        C.P = Prog(nc, st)
        setup_consts(C)
        ffn = lambda src, dst, sk, l, j: ffn_phase(C, S, src, dst, sk, "h", A["norm_g"][l, 2 * j], A["ffn_w_gate"][l, j],
                                                   A["ffn_w_up"][l, j], A["ffn_w_down"][l, j])
        ffn(x, h, "x", 0, 0)
        Wr = {n: A[n][0] for n in PARAM_SHAPES if n.startswith("rwkv")}
        Wr["rwkv_r_k"] = A["rwkv_r_k"][0].rearrange("h n -> (h n)")
        Wr["norm_g"] = A["norm_g"][0, 1]
        rwkv_pass_a(C, S, h, "h", Wr, scr)
        rwkv_pass_b(C, S, h, "h", Wr, scr)
        ffn(h, h, "h", 0, 1)
        Wm = {"positions": pos, "rope_invf": invf, "rope_sgn": sgn, "kv_norm_g": A["kv_norm_g"], "mla_w_dkv": A["mla_w_dkv"],
              "mla_kv_latent_g": A["mla_kv_latent_g"], "mla_w_ukv": A["mla_w_ukv"], "mla_w_dq": A["mla_w_dq"][0],
              "mla_q_latent_g": A["mla_q_latent_g"][0], "mla_w_uq": A["mla_w_uq"][0], "mla_w_o": A["mla_w_o"][0],
              "norm_g": A["norm_g"][1, 1]}
        mla_kv_phase(C, S, h, "h", Wm, scr)
        ffn(h, h, "h", 1, 0)
        mla_q_phase(C, S, h, "h", Wm, scr)
        mla_attn_phase(C, S, scr)
        mla_out_phase(C, S, h, "h", Wm, scr)
        ffn(h, h, "h", 1, 1)
        final_norm_phase(C, S, h, out, "h", "o", A["final_norm_g"])
    return nc


def kernel(**inputs):
    x = np.ascontiguousarray(np.asarray(inputs["x"], dtype=np.float32))
    B, S, _ = x.shape
    positions = np.ascontiguousarray(np.asarray(inputs["positions"]).astype(np.int32))
    params = {n: np.ascontiguousarray(np.asarray(inputs[n], dtype=np.float32)) for n in PARAM_SHAPES}
    invf2, sgn = rope_consts()
    nc = build_program(S)
    in_maps = []
    for b in range(B):
        m = dict(params)
        m["x"] = x[b]
        m["positions"] = positions[b]
        m["rope_invf"] = invf2
        m["rope_sgn"] = sgn
        in_maps.append(m)
    res = run_bass_kernel_spmd(nc, in_maps, core_ids=list(range(B)))
    return np.stack([np.asarray(r["out"]) for r in res.results], axis=0).astype(np.float32)
```
